# Optimizing a Trainium2 kernel written in Bass

```python
import math
import jax, jax.numpy as jnp
from jax import lax
import numpy as np

D_MODEL = 4096
BATCH = 2
SEQ = 8192
DEPTH = 2

CHUNK = 64
D_MIX = D_MODEL
D_A = D_MIX // 2
D_B = D_MIX - D_A
SGU_BLOCK = 128
SGU_GROUPS = 8
SGU_GW = D_A // SGU_GROUPS
DA_HEADS = 8
DA_HD = D_B // DA_HEADS // 2
ROT_DIM = DA_HD // 4
ROPE_THETA = 500000.0
Q_BLOCK = 128
EPS = 1e-5
SPLITS = (D_A, D_A, D_A, D_B, D_B, D_B, D_B)
D_IN = sum(SPLITS)

kernel_name = "hybrid_gmlp_diffattn_chunk_causal"


def rmsnorm(x, g):
    xf = x.astype(jnp.float32)
    y = xf * lax.rsqrt(jnp.mean(xf * xf, axis=-1, keepdims=True) + EPS)
    return (y * g.astype(jnp.float32)).astype(x.dtype)


def layernorm(x, g, b):
    xf = x.astype(jnp.float32)
    mu = jnp.mean(xf, axis=-1, keepdims=True)
    xc = xf - mu
    var = jnp.mean(xc * xc, axis=-1, keepdims=True)
    y = xc * lax.rsqrt(var + EPS) * g.astype(jnp.float32) + b.astype(jnp.float32)
    return y.astype(x.dtype)


def lambda_init_fn(layer_idx):
    return 0.8 - 0.6 * math.exp(-0.3 * layer_idx)


def rope_tables(positions):
    inv_freq = ROPE_THETA ** (-jnp.arange(0, ROT_DIM, 2, dtype=jnp.float32) / ROT_DIM)
    ang = positions.astype(jnp.float32)[..., None] * inv_freq
    return jnp.cos(ang)[:, :, None, None, :], jnp.sin(ang)[:, :, None, None, :]


def partial_rope(t, cos, sin):
    tf = t.astype(jnp.float32)
    half = ROT_DIM // 2
    x1 = tf[..., :half]
    x2 = tf[..., half:ROT_DIM]
    rot = jnp.concatenate([x1 * cos - x2 * sin, x2 * cos + x1 * sin], axis=-1)
    return jnp.concatenate([rot, tf[..., ROT_DIM:]], axis=-1).astype(t.dtype)


def sgu_mix(v, w_s, b_s):
    B, S, _ = v.shape
    nc = S // SGU_BLOCK
    vb = v.reshape(B, nc, SGU_BLOCK, SGU_GROUPS, SGU_GW)
    idx = jnp.arange(SGU_BLOCK)
    mask = (idx[None, :] // CHUNK) <= (idx[:, None] // CHUNK)
    w = jnp.where(mask[None], w_s, jnp.zeros_like(w_s))
    out = jnp.einsum('gij,bcjge->bcige', w, vb) + b_s.T[None, None, :, :, None]
    return out.reshape(B, S, D_A)


def diff_attention(q, k, v, lam):
    B, S, H, _, d = q.shape
    nb = S // Q_BLOCK
    q = q * jnp.asarray(d ** -0.5, q.dtype)
    qb = q.reshape(B, nb, Q_BLOCK, H, 2, d).transpose(1, 0, 3, 4, 2, 5)
    kt = k.transpose(0, 2, 3, 1, 4)
    vt = v.transpose(0, 2, 1, 3).astype(jnp.float32)
    k_chunk = jnp.arange(S) // CHUNK
    neg = jnp.finfo(jnp.float32).min

    def block(args):
        qi, bi = args
        s = jnp.einsum('bhcqd,bhckd->bhcqk', qi, kt).astype(jnp.float32)
        q_chunk = (bi * Q_BLOCK + jnp.arange(Q_BLOCK)) // CHUNK
        mask = k_chunk[None, :] <= q_chunk[:, None]
        s = jnp.where(mask, s, neg)
        p = jax.nn.softmax(s, axis=-1)
        a = p[:, :, 0] - lam * p[:, :, 1]
        o = jnp.einsum('bhqk,bhke->bhqe', a, vt)
        return o.astype(v.dtype)

    out = lax.map(block, (qb, jnp.arange(nb)))
    return out.transpose(1, 0, 3, 2, 4).reshape(B, S, H, 2 * d)


def setup_inputs(seed: int = 0) -> dict:
    key = jax.random.key(seed)
    ks = jax.random.split(key, 16)
    f32 = jnp.float32
    x = jax.random.normal(ks[0], (BATCH, SEQ, D_MODEL), f32)
    offset = jax.random.randint(ks[1], (BATCH, 1), 0, 100000, dtype=jnp.int32)
    positions = offset + jnp.arange(SEQ, dtype=jnp.int32)[None, :]
    norm_g = 1.0 + 0.02 * jax.random.normal(ks[2], (DEPTH, D_MODEL), f32)
    w_in = jax.random.normal(ks[3], (DEPTH, D_MODEL, D_IN), f32) * (D_MODEL ** -0.5)
    ln_g = 1.0 + 0.02 * jax.random.normal(ks[4], (DEPTH, D_A), f32)
    ln_b = 0.02 * jax.random.normal(ks[5], (DEPTH, D_A), f32)
    sgu_w = jax.random.normal(ks[6], (DEPTH, SGU_GROUPS, SGU_BLOCK, SGU_BLOCK), f32) * (SGU_BLOCK ** -0.5)
    sgu_b = 1.0 + 0.02 * jax.random.normal(ks[7], (DEPTH, SGU_GROUPS, SGU_BLOCK), f32)
    lam_q1 = 0.1 * jax.random.normal(ks[8], (DEPTH, DA_HD), f32)
    lam_k1 = 0.1 * jax.random.normal(ks[9], (DEPTH, DA_HD), f32)
    lam_q2 = 0.1 * jax.random.normal(ks[10], (DEPTH, DA_HD), f32)
    lam_k2 = 0.1 * jax.random.normal(ks[11], (DEPTH, DA_HD), f32)
    subln_g = 1.0 + 0.02 * jax.random.normal(ks[12], (DEPTH, 2 * DA_HD), f32)
    w_out = jax.random.normal(ks[13], (DEPTH, D_MIX, D_MODEL), f32) * (0.5 * D_MIX ** -0.5)
    final_g = 1.0 + 0.02 * jax.random.normal(ks[14], (D_MODEL,), f32)
    return {"x": x, "positions": positions, "norm_g": norm_g, "w_in": w_in,
            "ln_g": ln_g, "ln_b": ln_b, "sgu_w": sgu_w, "sgu_b": sgu_b,
            "lam_q1": lam_q1, "lam_k1": lam_k1, "lam_q2": lam_q2, "lam_k2": lam_k2,
            "subln_g": subln_g, "w_out": w_out, "final_g": final_g}


def reference(x, positions, norm_g, w_in, ln_g, ln_b, sgu_w, sgu_b,
              lam_q1, lam_k1, lam_q2, lam_k2, subln_g, w_out, final_g):
    B, S, _ = x.shape
    cos, sin = rope_tables(positions)
    cuts = list(np.cumsum(SPLITS)[:-1])
    for l in range(DEPTH):
        h = rmsnorm(x, norm_g[l])
        z = jnp.einsum('bsd,de->bse', h, w_in[l])
        u_a, v_a, g_a, q, k, v_b, g_b = jnp.split(z, cuts, axis=-1)

        u_a = jax.nn.gelu(u_a, approximate=False)
        v_a = layernorm(jax.nn.gelu(v_a, approximate=False), ln_g[l], ln_b[l])
        y_a = u_a * sgu_mix(v_a, sgu_w[l], sgu_b[l])
        y_a = y_a * jax.nn.silu(g_a)

        q = partial_rope(q.reshape(B, S, DA_HEADS, 2, DA_HD), cos, sin)
        k = partial_rope(k.reshape(B, S, DA_HEADS, 2, DA_HD), cos, sin)
        v_b = v_b.reshape(B, S, DA_HEADS, 2 * DA_HD)
        lam_init = lambda_init_fn(l)
        lam = (jnp.exp(jnp.sum(lam_q1[l].astype(jnp.float32) * lam_k1[l].astype(jnp.float32)))
               - jnp.exp(jnp.sum(lam_q2[l].astype(jnp.float32) * lam_k2[l].astype(jnp.float32)))
               + lam_init)
        o = diff_attention(q, k, v_b, lam)
        o = rmsnorm(o, subln_g[l]) * jnp.asarray(1.0 - lam_init, o.dtype)
        y_b = o.reshape(B, S, D_B) * jax.nn.silu(g_b)

        y = jnp.concatenate([y_a, y_b], axis=-1)
        x = x + jnp.einsum('bse,ed->bsd', y, w_out[l])
    return rmsnorm(x, final_g)
```

```python
import math
import os
import numpy as np
import concourse.bass as bass
import concourse.mybir as mybir
from concourse.bass_utils import run_bass_kernel_spmd

F32 = mybir.dt.float32
BF16 = mybir.dt.bfloat16
I32 = mybir.dt.int32
AF = mybir.ActivationFunctionType
ALU = mybir.AluOpType
AX = mybir.AxisListType

EPS = 1e-5
ROPE_THETA = 500000.0
NCORES = 8
RANKS = 4


class Cfg:
    def __init__(self, S=8192, DM=4096, G=8, H=8, DEPTH=2, debug=False, stop=99):
        self.S, self.DM, self.G, self.H, self.DEPTH = S, DM, G, H, DEPTH
        self.debug = debug
        self.stop = stop
        self.DA = G * 256
        self.DB = H * 256
        self.DMIX = self.DA + self.DB
        self.DIN = 3 * self.DA + 4 * self.DB
        self.TPC = S // RANKS
        self.NST = self.TPC // 512
        self.NBLK = self.TPC // 128
        self.KC = DM // 128
        self.HPC = H // RANKS
        self.NCA = self.DA // 128
        self.NCB = self.DB // 128
        self.NCM = self.DMIX // 128


import types


def _freeze(fn):
    if fn.__closure__ is None:
        return fn
    cells = []
    for cl in fn.__closure__:
        try:
            cells.append(types.CellType(cl.cell_contents))
        except ValueError:
            cells.append(cl)
    return types.FunctionType(fn.__code__, fn.__globals__, fn.__name__, fn.__defaults__, tuple(cells))


class _Op:
    __slots__ = ("eng", "fn", "deps", "kind", "semkey", "count", "signalled", "inc", "idx")

    def __init__(self, eng, fn, kind, semkey, inc):
        self.eng, self.fn, self.kind, self.semkey, self.inc = eng, fn, kind, semkey, inc
        self.deps = []
        self.count = None
        self.signalled = False


class Sched:
    ENGS = ("pe", "act", "dve", "pool", "sp")

    def __init__(self):
        self.ops = []
        self.last_writer = {}
        self.readers = {}
        self.epoch = 0
        self.last_eng_op = {}
        self.async_since_fence = []
        self.ncc = 0
        self.persist = {}
        self.persist_ops = []

    def op(self, eng, fn, r=(), w=(), dma=None, inc=None, persist=False):
        kind = "d" if dma is not None else "c"
        semkey = ("dma", dma) if dma is not None else ("eng", eng, self.epoch)
        o = _Op(eng, _freeze(fn), kind, semkey, inc if inc is not None else (16 if kind == "d" else 1))
        deps = {}
        for k in r:
            lw = self.last_writer.get(k)
            if lw is None:
                lw = self.persist.get(k)
            if lw is not None:
                deps[id(lw)] = lw
        for k in w:
            lw = self.last_writer.get(k)
            if lw is not None:
                deps[id(lw)] = lw
            for rd in self.readers.get(k, ()):
                deps[id(rd)] = rd
        o.deps = list(deps.values())
        for k in w:
            self.last_writer[k] = o
            self.readers[k] = []
        for k in r:
            self.readers.setdefault(k, []).append(o)
        self.ops.append(o)
        if persist:
            for k in w:
                self.persist[k] = o
            self.persist_ops.append(o)
        elif kind == "d":
            self.async_since_fence.append(o)
        else:
            self.last_eng_op[eng] = o
        return o

    def fence(self, final=False):
        tails = list(self.last_eng_op.values()) + list(self.async_since_fence)
        if final:
            tails += self.persist_ops
        new_last = {}
        for e in self.ENGS:
            o = _Op(e, lambda eng: eng.nop(), "c", ("eng", e, self.epoch), 1)
            o.deps = [t for t in tails]
            self.ops.append(o)
            new_last[e] = o
        self.last_eng_op = new_last
        self.async_since_fence = []
        self.last_writer = {}
        self.readers = {}

    def finalize(self):
        for i, o in enumerate(self.ops):
            o.idx = i
        for o in self.ops:
            best = {}
            for d in o.deps:
                b = best.get(d.semkey)
                if b is None or d.idx > b.idx:
                    best[d.semkey] = d
            o.deps = list(best.values())
        for o in self.ops:
            for d in o.deps:
                if d.kind == "d":
                    d.signalled = True
                elif d.eng != o.eng or o.kind == "d" or d.eng != "pe":
                    d.signalled = True
        counts = {}
        for o in self.ops:
            if o.kind == "d":
                o.signalled = True
            if o.signalled:
                c = counts.get(o.semkey, 0) + o.inc
                counts[o.semkey] = c
                o.count = c
        self.counts = counts
        return list(counts.keys())

    def emit(self, block, sems):
        per_eng = {e: [] for e in self.ENGS}
        for o in self.ops:
            per_eng[o.eng].append(o)

        def run(eng_name):
            def body(e):
                waited = {}
                for o in per_eng[eng_name]:
                    need = {}
                    for d in o.deps:
                        if not d.signalled:
                            continue
                        if d.kind == "c" and d.eng == eng_name and eng_name == "pe" and o.kind == "c":
                            continue
                        if need.get(d.semkey, 0) < d.count:
                            need[d.semkey] = d.count
                    for sk, c in need.items():
                        if waited.get(sk, 0) >= c:
                            continue
                        e.wait_ge(sems[sk], c)
                        waited[sk] = c
                    ins = o.fn(e)
                    if o.signalled:
                        ins.then_inc(sems[o.semkey], o.inc)
            return body

        block.tensor(run("pe"))
        block.scalar(run("act"))
        block.vector(run("dve"))
        block.gpsimd(run("pool"))
        block.sync(run("sp"))


class Arena:
    def __init__(self, ap, nbytes):
        self.ap, self.cap, self.top = ap, nbytes, 0

    def alloc(self, shape, dt):
        if isinstance(shape, int):
            shape = (shape,)
        n = 1
        for s in shape:
            n *= s
        nb = n * (2 if dt == BF16 else 4)
        off = self.top
        self.top = off + (nb + 63) // 64 * 64
        assert self.top <= self.cap, ("SBUF arena overflow", self.top, self.cap)
        v = self.ap[:, off // 2:(off + nb) // 2]
        if dt != BF16:
            v = v.bitcast(dt)
        if len(shape) == 2:
            v = v.rearrange("p (a b) -> p a b", a=shape[0], b=shape[1])
        elif len(shape) == 3:
            v = v.rearrange("p (a b c) -> p a b c", a=shape[0], b=shape[1], c=shape[2])
        return v


def lambda_init_fn(layer_idx):
    return 0.8 - 0.6 * math.exp(-0.3 * layer_idx)


def build_program(cfg):
    c = cfg
    S_, DM, G, H, DEPTH = c.S, c.DM, c.G, c.H, c.DEPTH
    DA, DB, DMIX, DIN = c.DA, c.DB, c.DMIX, c.DIN
    TPC, NST, NBLK, KC, HPC = c.TPC, c.NST, c.NBLK, c.KC, c.HPC
    NCA, NCB, NCM = c.NCA, c.NCB, c.NCM
    NKT = S_ // 128
    NQG = S_ // 512
    HC = HPC * 2
    OFF_U, OFF_V, OFF_GA = 0, DA, 2 * DA
    OFF_Q, OFF_K, OFF_VB, OFF_GB = 3 * DA, 3 * DA + DB, 3 * DA + 2 * DB, 3 * DA + 3 * DB

    nc = bass.Bass("TRN2", target_bir_lowering=False)

    def din(name, shape, dt=F32):
        return nc.dram_tensor(name, list(shape), dt, kind="ExternalInput").ap()

    x_in = din("x", [TPC, DM])
    pos_in = din("pos_t", [128, NBLK], I32)
    invf_in = din("invf", [1, 16])
    ngc_in = din("norm_gc", [DEPTH, 128, KC])
    lgc_in = din("ln_gc", [DEPTH, 128, NCA])
    lbc_in = din("ln_bc", [DEPTH, 128, NCA])
    sguw_in = din("sgu_w", [DEPTH, G, 128, 128])
    sgub_in = din("sgu_b", [DEPTH, 1, G * 128])
    lam_in = din("lam", [DEPTH, 1, 4 * 128])
    subg_in = din("subln_g", [DEPTH, 1, 256])
    fing_in = din("final_g", [1, DM])
    win_in = din("w_in_sh", [DEPTH, KC * 32, DIN])
    wout_in = din("w_out_sh", [DEPTH, NCM * 32, DM])
    out_d = nc.dram_tensor("out", [TPC, DM], F32, kind="ExternalOutput").ap()

    def dscr(name, shape, dt=BF16):
        return nc.dram_tensor(name, list(shape), dt).ap()

    wsh_in = [dscr("wsh_in%d" % l, [DIN // 512, 32, KC, 512]) for l in range(DEPTH)]
    wsh_out = [dscr("wsh_out%d" % l, [DM // 512, 32, NCM, 512]) for l in range(DEPTH)]
    wg_in = [dscr("wg_in%d" % l, [DIN // 512, 128, KC * 512]) for l in range(DEPTH)]
    wg_out = [dscr("wg_out%d" % l, [DM // 512, 128, NCM * 512]) for l in range(DEPTH)]
    xs = [dscr("xs%d" % l, [TPC, DM], F32) for l in range(DEPTH)]
    qT_send = dscr("qT_send", [DB, TPC])
    kT_send = dscr("kT_send", [DB, TPC])
    v_send = dscr("v_send", [H, TPC, 256])
    qT_g = dscr("qT_g", [H, RANKS, 256, TPC])
    kT_g = dscr("kT_g", [H, RANKS, 256, TPC])
    v_g = dscr("v_g", [H, RANKS * TPC, 256])
    onT_send = dscr("onT_send", [RANKS, HPC * 256, TPC])
    onT_g = dscr("onT_g", [RANKS, HPC, RANKS, 256, TPC])
    kT_my = dscr("kT_my", [HPC, RANKS, 256, TPC])
    qT_my = dscr("qT_my", [HPC, RANKS, 256, TPC])
    v_my = dscr("v_my", [HPC, RANKS * TPC, 256])
    on_my = dscr("on_my", [HPC, RANKS, 256, TPC])
    yaT_sp = dscr("yaT_sp", [DA, TPC])
    sgbT_sp = dscr("sgbT_sp", [DB, TPC])

    ARENA_BYTES = int(os.environ.get("ARENA_KB", "206")) * 1024
    ctxs = [nc.sbuf_tensor("arena", [128, ARENA_BYTES // 2], BF16)]
    ctxs += [nc.psum_tensor("pb%d" % i, [128, 512], F32) for i in range(8)]
    ents = [cx.__enter__() for cx in ctxs]
    arena = Arena(ents[0], ARENA_BYTES)
    PB = ents[1:9]
    PBb = [p[:, 0:512].bitcast(BF16) for p in PB]

    S = Sched()
    hp_cache = {}
    GROUP4 = [[0, 1, 2, 3], [4, 5, 6, 7]]
    GROUP8 = [list(range(8))]

    def collective(groups, src, dst, r, w, semkey, persist=False):
        S.op("pool", lambda e: e.collective_compute("AllGather", ALU.bypass, replica_groups=groups,
                                                    ins=[src.opt()], outs=[dst.opt()]),
             r=r, w=w, dma=semkey, inc=1, persist=persist)

    def dump(name, ap, rkeys):
        if not (isinstance(c.debug, (list, tuple)) and ("dump_" + name) in c.debug):
            return
        shp = list(ap.shape)
        n = 1
        for v_ in shp[1:]:
            n *= v_
        dd = nc.dram_tensor("dump_" + name, [shp[0], n], ap.dtype, kind="ExternalOutput").ap()
        src = ap
        if len(shp) == 3:
            dd = dd.rearrange("p (a b) -> p a b", a=shp[1])
        elif len(shp) == 4:
            dd = dd.rearrange("p (a b c) -> p a b c", a=shp[1], b=shp[2])
        S.op("sp", lambda e: e.dma_start(out=dd, in_=src), r=rkeys, w=[("dump", name)], dma="dump_" + name)

    ident = arena.alloc(128, BF16)
    ones_bf = arena.alloc(128, BF16)
    eps_t = arena.alloc(1, F32)
    cos_t = arena.alloc((NBLK, 16), F32)
    sin_t = arena.alloc((NBLK, 16), F32)
    gcol = arena.alloc(KC, F32)
    lgc = arena.alloc(NCA, F32)
    lbc = arena.alloc(NCA, F32)
    wmT = arena.alloc((G, 128), BF16)
    Cb = arena.alloc((NCA, 128), F32)
    gsub = arena.alloc(256, F32)
    neg_lam = arena.alloc(1, F32)
    PERSIST_TOP = arena.top

    def cast_gather(src3, l, sh, g, nchunk, cols, tag):
        src = src3[l]
        ntile = cols // 512
        ckeys = []
        for kc in range(nchunk):
            S.op("pool", lambda e, kc=kc: e.dma_start(
                out=sh[:, :, kc, :].rearrange("t i n -> i t n"),
                in_=src[kc * 32:(kc + 1) * 32, :].rearrange("i (t n) -> i t n", n=512)),
                 w=[(tag, "sh", kc)], dma="cast", persist=True)
            ckeys.append((tag, "sh", kc))
        for t in range(ntile):
            collective(GROUP4, sh[t].rearrange("i k n -> i (k n)"), g[t],
                       r=(ckeys if t == 0 else []) + ([(tag, "thr", t - 8)] if t >= 8 else []),
                       w=[(tag, "g"), (tag, "thr", t)], semkey="cc_" + tag, persist=True)

    S.op("pool", lambda e: e.memset(ident, 0.0), w=["ident"])
    S.op("pool", lambda e: e.affine_select(out=ident, in_=ident, compare_op=ALU.not_equal, fill=1.0,
                                           base=0, pattern=[[-1, 128]], channel_multiplier=1),
         r=["ident"], w=["ident"])
    S.op("pool", lambda e: e.memset(ones_bf, 1.0), w=["ones"])
    S.op("pool", lambda e: e.memset(eps_t, EPS), w=["eps"])
    for l in range(DEPTH if not os.environ.get("NOGATHER") else 0):
        cast_gather(win_in, l, wsh_in[l], wg_in[l], KC, DIN, "win%d" % l)
        cast_gather(wout_in, l, wsh_out[l], wg_out[l], NCM, DM, "wout%d" % l)

    mark = arena.top
    pos_sb = arena.alloc(NBLK, I32)
    posf = arena.alloc(NBLK, F32)
    invf = arena.alloc(16, F32)
    ang = arena.alloc((NBLK, 16), F32)
    tq = arena.alloc((NBLK, 16), F32)
    ki = arena.alloc((NBLK, 16), I32)
    kf = arena.alloc((NBLK, 16), F32)
    rr = arena.alloc((NBLK, 16), F32)
    r2 = arena.alloc((NBLK, 16), F32)
    mk = arena.alloc((NBLK, 16), F32)
    S.op("sp", lambda e: e.dma_start(out=pos_sb, in_=pos_in[:, :]), w=["pos"], dma="ldc0")
    S.op("sp", lambda e: e.dma_start(out=invf, in_=invf_in.partition_broadcast(128)), w=["invf"], dma="ldc1")
    S.op("dve", lambda e: e.tensor_copy(out=posf, in_=pos_sb), r=["pos"], w=["posf"])
    S.op("dve", lambda e: e.tensor_tensor(out=ang, in0=posf.unsqueeze(2).broadcast_to([128, NBLK, 16]),
                                          in1=invf.unsqueeze(1).broadcast_to([128, NBLK, 16]), op=ALU.mult),
         r=["posf", "invf"], w=["ang"])
    TWO_PI = float(2 * np.pi)
    C1 = 6.28125
    C2 = float(2 * np.pi - 6.28125)
    S.op("dve", lambda e: e.tensor_scalar(out=tq, in0=ang, scalar1=float(1 / (2 * np.pi)), scalar2=None, op0=ALU.mult),
         r=["ang"], w=["tq"])
    S.op("dve", lambda e: e.tensor_copy(out=ki, in_=tq), r=["tq"], w=["ki"])
    S.op("dve", lambda e: e.tensor_copy(out=kf, in_=ki), r=["ki"], w=["kf"])
    S.op("dve", lambda e: e.scalar_tensor_tensor(out=rr, in0=kf, scalar=-C1, in1=ang, op0=ALU.mult, op1=ALU.add),
         r=["kf", "ang"], w=["rr"])
    S.op("dve", lambda e: e.scalar_tensor_tensor(out=rr, in0=kf, scalar=-C2, in1=rr, op0=ALU.mult, op1=ALU.add),
         r=["kf", "rr"], w=["rr"])

    def wrap(buf, key):
        S.op("dve", lambda e: e.tensor_single_scalar(out=mk, in_=buf, scalar=float(np.pi), op=ALU.is_gt), r=[key], w=["mk"])
        S.op("dve", lambda e: e.scalar_tensor_tensor(out=buf, in0=mk, scalar=-TWO_PI, in1=buf, op0=ALU.mult, op1=ALU.add),
             r=["mk", key], w=[key])
        S.op("dve", lambda e: e.tensor_single_scalar(out=mk, in_=buf, scalar=-float(np.pi), op=ALU.is_lt), r=[key], w=["mk"])
        S.op("dve", lambda e: e.scalar_tensor_tensor(out=buf, in0=mk, scalar=TWO_PI, in1=buf, op0=ALU.mult, op1=ALU.add),
             r=["mk", key], w=[key])

    wrap(rr, "rr")
    S.op("dve", lambda e: e.tensor_scalar(out=r2, in0=rr, scalar1=float(np.pi / 2), scalar2=None, op0=ALU.add), r=["rr"], w=["r2"])
    wrap(r2, "r2")
    S.op("act", lambda e: e.activation(out=sin_t, in_=rr, func=AF.Sin), r=["rr"], w=["sin"])
    S.op("act", lambda e: e.activation(out=cos_t, in_=r2, func=AF.Sin), r=["r2"], w=["cos"])
    dump("ident0", ident, ["ident"])
    dump("eps0", eps_t, ["eps"])
    dump("ones0", ones_bf, ["ones"])
    dump("cos", cos_t, ["cos"])
    dump("sin", sin_t, ["sin"])
    S.fence()
    arena.top = mark

    for l in range(DEPTH if c.stop >= 1 else 0):
        x_cur = x_in if l == 0 else xs[l - 1]
        x_next = xs[l]
        S.epoch = l + 1
        lam_init = lambda_init_fn(l)

        arena.top = PERSIST_TOP
        w_sb = arena.alloc((G, 128), F32)
        wm_b = arena.alloc((G, 128), BF16)
        bs_bc = arena.alloc((G, 128), F32)
        lamv = arena.alloc((4, 128), F32)
        lprod = arena.alloc((2, 128), F32)
        lsum = arena.alloc(2, F32)
        lexp = arena.alloc(2, F32)
        S.op("sp", lambda e, l=l: e.dma_start(out=gcol, in_=ngc_in[l]), w=["gcol"], dma="ldc0")
        S.op("sp", lambda e, l=l: e.dma_start(out=lgc, in_=lgc_in[l]), w=["lgc"], dma="ldc1")
        S.op("sp", lambda e, l=l: e.dma_start(out=lbc, in_=lbc_in[l]), w=["lbc"], dma="ldc2")
        S.op("sp", lambda e, l=l: e.dma_start(out=w_sb, in_=sguw_in[l].rearrange("g i j -> i g j")), w=["w_sb"], dma="ldc3")
        S.op("sp", lambda e, l=l: e.dma_start(out=bs_bc.rearrange("p g i -> p (g i)"), in_=sgub_in[l].partition_broadcast(128)),
             w=["bs_bc"], dma="ldc4")
        S.op("sp", lambda e, l=l: e.dma_start(out=lamv.rearrange("p a b -> p (a b)"), in_=lam_in[l].partition_broadcast(128)),
             w=["lamv"], dma="ldc5")
        S.op("sp", lambda e, l=l: e.dma_start(out=gsub, in_=subg_in[l].partition_broadcast(128)), w=["gsub"], dma="ldc6")
        S.op("pool", lambda e: e.memset(w_sb[0:64, :, 64:128], 0.0), r=["w_sb"], w=["w_sb"])
        S.op("dve", lambda e: e.tensor_copy(out=wm_b, in_=w_sb), r=["w_sb"], w=["wm_b"])
        for g in range(G):
            S.op("pe", lambda e, g=g: e.transpose(PBb[6][:, g * 128:(g + 1) * 128], wm_b[:, g, :], ident),
                 r=["wm_b", "ident"], w=[("pb6", g)])
        S.op("dve", lambda e: e.tensor_copy(out=wmT.rearrange("p g i -> p (g i)"), in_=PBb[6][:, 0:G * 128]),
             r=[("pb6", g) for g in range(G)], w=["wmT"])
        if l == 0:
            dump("w_sb", w_sb, ["w_sb"])
            dump("wm_b", wm_b, ["wm_b"])
            dump("wmT2", wmT, ["wmT"])
        wmT_f = wmT.rearrange("p g i -> p (g i)")
        npiece = (G * 128 + 511) // 512
        for pc in range(npiece):
            n = min(512, G * 128 - pc * 512)
            S.op("pe", lambda e, pc=pc, n=n: e.matmul(PB[4 + pc][:, 0:n], ones_bf, wmT_f[:, pc * 512:pc * 512 + n],
                                                      start=True, stop=True),
                 r=["wmT", "ones"], w=[("rw", pc)])
        for fc in range(NCA):
            g = fc // 2
            pc, o = (g * 128) // 512, (g * 128) % 512
            S.op("dve", lambda e, fc=fc, g=g, pc=pc, o=o: e.scalar_tensor_tensor(
                out=Cb[:, fc, :], in0=PB[4 + pc][:, o:o + 128], scalar=lbc[:, fc:fc + 1], in1=bs_bc[:, g, :],
                op0=ALU.mult, op1=ALU.add), r=[("rw", pc), "lbc", "bs_bc"], w=[("Cb", fc)])
        S.op("dve", lambda e: e.tensor_tensor(out=lprod[:, 0, :], in0=lamv[:, 0, :], in1=lamv[:, 1, :], op=ALU.mult),
             r=["lamv"], w=["lp0"])
        S.op("dve", lambda e: e.tensor_tensor(out=lprod[:, 1, :], in0=lamv[:, 2, :], in1=lamv[:, 3, :], op=ALU.mult),
             r=["lamv"], w=["lp1"])
        S.op("dve", lambda e: e.reduce_sum(out=lsum, in_=lprod, axis=AX.X), r=["lp0", "lp1"], w=["lsum"])
        S.op("act", lambda e: e.activation(out=lexp, in_=lsum, func=AF.Exp), r=["lsum"], w=["lexp"])
        S.op("dve", lambda e: e.tensor_tensor(out=neg_lam, in0=lexp[:, 1:2], in1=lexp[:, 0:1], op=ALU.subtract),
             r=["lexp"], w=["neg_lam"])
        S.op("dve", lambda e, li=lam_init: e.tensor_scalar(out=neg_lam, in0=neg_lam, scalar1=-float(li), scalar2=None, op0=ALU.add),
             r=["neg_lam"], w=["neg_lam"])
        S.op("dve", lambda e, li=lam_init: e.tensor_scalar(out=gsub, in0=gsub, scalar1=float(1.0 - li), scalar2=None, op0=ALU.mult),
             r=["gsub"], w=["gsub"])
        S.fence()

        if c.stop < 2:
            continue
        arena.top = PERSIST_TOP
        wslot = [arena.alloc((KC, 512), BF16) for _ in range(2)]
        hT = arena.alloc((KC, 512), BF16)
        gv = arena.alloc((4, DA), F32)
        gv_flat = gv.rearrange("p a b -> p (a b)")
        vln = arena.alloc((4, DA), BF16)
        vln_flat = vln.rearrange("p a b -> p (a b)")
        nxl = 2 if 2 * DM <= 4 * DA else 1
        xl = [gv_flat[:, i * DM:(i + 1) * DM] for i in range(nxl)]
        nxn = 2 if 2 * DM <= 4 * DA else 1
        xn = [vln_flat[:, i * DM:(i + 1) * DM] for i in range(nxn)]

        def gv_keys(lo, hi):
            return [("gv", b) for b in range(lo // DA, (hi - 1) // DA + 1)]

        def vln_keys(lo, hi):
            return [("vln", b) for b in range(lo // DA, (hi - 1) // DA + 1)]

        ssq = arena.alloc(4, F32)
        rstd = arena.alloc(4, F32)
        stats = arena.alloc((4, 6 * ((DA + 511) // 512)), F32)
        mv = arena.alloc((4, 2), F32)
        lrs = arena.alloc(4, F32)
        qkb = [arena.alloc(512, BF16) for _ in range(2)]
        rt = [arena.alloc((4, 64), F32) for _ in range(2)]
        qst = [arena.alloc((4, 512), BF16) for _ in range(2)]
        vst = [arena.alloc(512, BF16) for _ in range(2)]
        gu = [arena.alloc(512, BF16) for _ in range(2)]
        t1 = [arena.alloc(512, F32) for _ in range(2)]
        p1 = arena.alloc((4, 512), BF16)
        sg = [arena.alloc(512, BF16) for _ in range(2)]
        yst = [arena.alloc(512, BF16) for _ in range(2)]
        sgst = [arena.alloc(512, BF16) for _ in range(2)]

        nA, nB = DA // 512, DB // 512
        tiles = []
        for t in range(nA):
            tiles.append(("va", t, OFF_V + t * 512))
        for t in range(nB):
            tiles.append(("q", t, OFF_Q + t * 512))
        for t in range(nB):
            tiles.append(("k", t, OFF_K + t * 512))
        for t in range(nB):
            tiles.append(("vb", t, OFF_VB + t * 512))
        for t in range(nA):
            tiles.append(("u", t, OFF_U + t * 512))
            tiles.append(("ga", t, OFF_GA + t * 512))
        for t in range(nB):
            tiles.append(("gb", t, OFF_GB + t * 512))
        NT = len(tiles)

        cnt = {"acc": 0, "tr": 0, "m": 0, "z": 0}

        for st in range(NST):
            tok0 = st * 512

            def wload(ti):
                kind, t, c0 = tiles[ti]
                sl = ti % 2
                S.op("sp", lambda e, sl=sl, c0=c0: e.dma_start(out=wslot[sl].rearrange("p k n -> p (k n)"), in_=wg_in[l][c0 // 512]),
                     r=[("win%d" % l, "g")], w=[("w", sl)], dma="w%d" % sl)

            wload(0)
            for j in range(4):
                xb = xl[j % nxl]
                xk = gv_keys((j % nxl) * DM, (j % nxl + 1) * DM)
                nb = xn[j % nxn]
                nk = vln_keys((j % nxn) * DM, (j % nxn + 1) * DM)
                r0 = tok0 + j * 128
                S.op("sp", lambda e, xb=xb, r0=r0: e.dma_start(out=xb, in_=x_cur[r0:r0 + 128, :]),
                     w=xk, dma="x%d" % (j % nxl))
                S.op("act", lambda e, xb=xb, nb=nb, j=j: e.activation(out=nb, in_=xb, func=AF.Square, accum_out=ssq[:, j:j + 1]),
                     r=xk, w=nk + [("ssq", j)])
                S.op("act", lambda e, j=j: e.activation(out=rstd[:, j:j + 1], in_=ssq[:, j:j + 1], func=AF.Sqrt,
                                                        scale=1.0 / DM, bias=eps_t), r=[("ssq", j), "eps"], w=[("rstd", j)])
                S.op("dve", lambda e, j=j: e.reciprocal(out=rstd[:, j:j + 1], in_=rstd[:, j:j + 1]),
                     r=[("rstd", j)], w=[("rstd", j)])
                S.op("dve", lambda e, xb=xb, nb=nb, j=j: e.tensor_scalar_mul(out=nb, in0=xb, scalar1=rstd[:, j:j + 1]),
                     r=xk + [("rstd", j)], w=nk)
                for k0 in range(0, KC, 8):
                    nk8 = min(8, KC - k0)
                    bank = 6 + (cnt["tr"] % 2)
                    cnt["tr"] += 1
                    for kk in range(nk8):
                        S.op("pe", lambda e, nb=nb, bank=bank, kk=kk, k0=k0: e.transpose(
                            PBb[bank][:, kk * 128:(kk + 1) * 128], nb[:, (k0 + kk) * 128:(k0 + kk + 1) * 128], ident),
                             r=nk + ["ident"], w=[("pb", bank, kk)])
                    S.op("dve", lambda e, bank=bank, k0=k0, nk8=nk8, j=j: e.tensor_tensor(
                        out=hT[:, k0:k0 + nk8, j * 128:(j + 1) * 128],
                        in0=PBb[bank][:, 0:nk8 * 128].rearrange("p (a b) -> p a b", a=nk8),
                        in1=gcol[:, k0:k0 + nk8].unsqueeze(2).broadcast_to([128, nk8, 128]), op=ALU.mult),
                         r=[("pb", bank, kk) for kk in range(nk8)] + ["gcol"], w=[("hT", j)])

            hT_all = [("hT", j) for j in range(4)]
            if st == 0 and l == 0:
                dump("hT", hT, hT_all)
                dump("rstd", rstd, [("rstd", j) for j in range(4)])
                dump("gcol", gcol, ["gcol"])
                dump("w0", wslot[0], [("w", 0)])
                dump("wmT", wmT, ["wmT"])
                dump("ident", ident, ["ident"])
                dump("eps", eps_t, ["eps"])
                dump("ones", ones_bf, ["ones"])
                dump("ssq", ssq, [("ssq", j) for j in range(4)])
                dump("x3", xl[3 % nxl], gv_keys(0, 4 * DA))
                dump("xn3", xn[3 % nxn], vln_keys(0, 4 * DA))
                dump("Cb", Cb, [("Cb", fc) for fc in range(NCA)])
            pend = []

            def flush_pend():
                for f in pend:
                    f()
                del pend[:]

            for ti in range(NT):
                kind, t, c0 = tiles[ti]
                sl = ti % 2
                W = wslot[sl]
                if ti + 1 < NT:
                    wload(ti + 1)
                if kind in ("va", "q", "k", "vb"):
                    for j in range(4):
                        bank = cnt["acc"] % 4
                        cnt["acc"] += 1
                        for kc in range(KC):
                            S.op("pe", lambda e, bank=bank, kc=kc, j=j, W=W: e.matmul(
                                PB[bank][:, 0:512], hT[:, kc, j * 128:(j + 1) * 128], W[:, kc, :],
                                start=(kc == 0), stop=(kc == KC - 1)),
                                 r=[("hT", j), ("w", sl)], w=[("pb", bank)])
                        flush_pend()
                        if kind == "va":
                            S.op("act", lambda e, bank=bank, j=j, t=t: e.activation(
                                out=gv[:, j, t * 512:(t + 1) * 512], in_=PB[bank][:, 0:512], func=AF.Gelu),
                                 r=[("pb", bank)], w=[("gv", j)])
                        elif kind == "vb":
                            zi = cnt["z"] % 2
                            cnt["z"] += 1
                            S.op("act", lambda e, bank=bank, zi=zi: e.activation(out=vst[zi], in_=PB[bank][:, 0:512], func=AF.Copy),
                                 r=[("pb", bank)], w=[("vst", zi)])
                            r0 = tok0 + j * 128
                            S.op("sp", lambda e, zi=zi, r0=r0, t=t: e.dma_start(
                                out=v_send[2 * t:2 * t + 2, r0:r0 + 128, :].rearrange("h p e -> p h e"),
                                in_=vst[zi].rearrange("p (h e) -> p h e", h=2)),
                                 r=[("vst", zi)], w=[("v_send", st, j, t)], dma="vst%d" % zi)
                        else:
                            zi = cnt["z"] % 2
                            cnt["z"] += 1
                            z4 = PB[bank][:, 0:512].rearrange("p (a b) -> p a b", a=4)
                            o4 = qkb[zi].rearrange("p (a b) -> p a b", a=4)
                            blk = st * 4 + j
                            cb = cos_t[:, blk, :].unsqueeze(1).broadcast_to([128, 4, 16])
                            sb_ = sin_t[:, blk, :].unsqueeze(1).broadcast_to([128, 4, 16])
                            T = rt[zi]
                            kb, kr, ko = [("pb", bank)], [("rt", zi)], [("qkb", zi)]
                            S.op("dve", lambda e, z4=z4, T=T, cb=cb: e.tensor_tensor(out=T[:, :, 0:16], in0=z4[:, :, 0:16], in1=cb, op=ALU.mult), r=kb, w=kr)
                            S.op("dve", lambda e, z4=z4, T=T, sb_=sb_: e.tensor_tensor(out=T[:, :, 16:32], in0=z4[:, :, 16:32], in1=sb_, op=ALU.mult), r=kb, w=kr)
                            S.op("dve", lambda e, z4=z4, T=T, cb=cb: e.tensor_tensor(out=T[:, :, 32:48], in0=z4[:, :, 16:32], in1=cb, op=ALU.mult), r=kb, w=kr)
                            S.op("dve", lambda e, z4=z4, T=T, sb_=sb_: e.tensor_tensor(out=T[:, :, 48:64], in0=z4[:, :, 0:16], in1=sb_, op=ALU.mult), r=kb, w=kr)
                            S.op("dve", lambda e, o4=o4, T=T: e.tensor_tensor(out=o4[:, :, 0:16], in0=T[:, :, 0:16], in1=T[:, :, 16:32], op=ALU.subtract), r=kr, w=ko)
                            S.op("dve", lambda e, o4=o4, T=T: e.tensor_tensor(out=o4[:, :, 16:32], in0=T[:, :, 32:48], in1=T[:, :, 48:64], op=ALU.add), r=kr, w=ko)
                            S.op("act", lambda e, o4=o4, z4=z4: e.activation(out=o4[:, :, 32:128], in_=z4[:, :, 32:128], func=AF.Copy), r=kb, w=ko)
                            qi = (0 if kind == "q" else 1)
                            dst = qT_send if kind == "q" else kT_send

                            def do_tr(zi=zi, j=j, qi=qi, t=t, dst=dst, last=(j == 3)):
                                bank2 = 6 + (cnt["tr"] % 2)
                                cnt["tr"] += 1
                                for a in range(4):
                                    S.op("pe", lambda e, a=a, zi=zi, bank2=bank2: e.transpose(
                                        PBb[bank2][:, a * 128:(a + 1) * 128], qkb[zi][:, a * 128:(a + 1) * 128], ident),
                                         r=[("qkb", zi), "ident"], w=[("pb", bank2, a)])
                                S.op("act", lambda e, bank2=bank2, qi=qi, j=j: e.activation(
                                    out=qst[qi][:, :, j * 128:(j + 1) * 128],
                                    in_=PBb[bank2][:, 0:512].rearrange("p (a b) -> p a b", a=4), func=AF.Copy),
                                     r=[("pb", bank2, a) for a in range(4)], w=[("qst", qi, j)])
                                if last:
                                    S.op("sp", lambda e, qi=qi, t=t, dst=dst: e.dma_start(
                                        out=dst[t * 512:(t + 1) * 512, tok0:tok0 + 512].rearrange("(a d) n -> d a n", d=128),
                                        in_=qst[qi]), r=[("qst", qi, jj) for jj in range(4)],
                                         w=[("qk_send", qi, st, t)] + [("qst", qi, jj) for jj in range(4)], dma="qst%d" % qi)
                            pend.append(do_tr)
                    if kind == "va" and t == nA - 1:
                        nch = (DA + 511) // 512
                        for j in range(4):
                            for ch in range(nch):
                                n = min(512, DA - ch * 512)
                                S.op("dve", lambda e, j=j, ch=ch, n=n: e.bn_stats(out=stats[:, j, 6 * ch:6 * ch + 6],
                                                                                  in_=gv[:, j, ch * 512:ch * 512 + n]),
                                     r=[("gv", j)], w=[("stats", j, ch)])
                            S.op("dve", lambda e, j=j: e.bn_aggr(out=mv[:, j, :], in_=stats[:, j, :]),
                                 r=[("stats", j, ch) for ch in range(nch)], w=[("mv", j)])
                            S.op("act", lambda e, j=j: e.activation(out=lrs[:, j:j + 1], in_=mv[:, j, 1:2], func=AF.Sqrt, bias=eps_t),
                                 r=[("mv", j), "eps"], w=[("lrs", j)])
                            S.op("dve", lambda e, j=j: e.reciprocal(out=lrs[:, j:j + 1], in_=lrs[:, j:j + 1]), r=[("lrs", j)], w=[("lrs", j)])
                            S.op("dve", lambda e, j=j: e.tensor_scalar(out=vln[:, j, :], in0=gv[:, j, :], scalar1=mv[:, j, 0:1],
                                                                       scalar2=lrs[:, j:j + 1], op0=ALU.subtract, op1=ALU.mult),
                                 r=[("gv", j), ("mv", j), ("lrs", j)], w=[("vln", j)])
                else:
                    for fcl in range(4):
                        fc = t * 4 + fcl
                        if kind == "u":
                            g = fc // 2
                            mb = 4 + (cnt["m"] % 2)
                            cnt["m"] += 1
                            for j in range(4):
                                S.op("pe", lambda e, mb=mb, j=j, fc=fc, g=g: e.matmul(
                                    PB[mb][:, j * 128:(j + 1) * 128], vln[:, j, fc * 128:(fc + 1) * 128], wmT[:, g, :],
                                    start=True, stop=True), r=[("vln", j), "wmT"], w=[("pbm", mb, j)])
                        bank = cnt["acc"] % 4
                        cnt["acc"] += 1
                        for kc in range(KC):
                            S.op("pe", lambda e, bank=bank, kc=kc, fcl=fcl, W=W: e.matmul(
                                PB[bank][:, 0:512], W[:, kc, fcl * 128:(fcl + 1) * 128], hT[:, kc, :],
                                start=(kc == 0), stop=(kc == KC - 1)), r=hT_all + [("w", sl)], w=[("pb", bank)])
                        flush_pend()
                        zi = cnt["z"] % 2
                        cnt["z"] += 1
                        if kind == "u":
                            S.op("act", lambda e, bank=bank, zi=zi: e.activation(out=gu[zi], in_=PB[bank][:, 0:512], func=AF.Gelu),
                                 r=[("pb", bank)], w=[("gu", zi)])
                            S.op("dve", lambda e, mb=mb, zi=zi, fc=fc: e.scalar_tensor_tensor(
                                out=t1[zi].rearrange("p (a b) -> p a b", a=4),
                                in0=PB[mb][:, 0:512].rearrange("p (a b) -> p a b", a=4), scalar=lgc[:, fc:fc + 1],
                                in1=Cb[:, fc, :].unsqueeze(1).broadcast_to([128, 4, 128]), op0=ALU.mult, op1=ALU.add),
                                 r=[("pbm", mb, j) for j in range(4)] + ["lgc", ("Cb", fc)], w=[("t1", zi)])
                            S.op("pool", lambda e, zi=zi, fcl=fcl: e.tensor_tensor(out=p1[:, fcl, :], in0=t1[zi], in1=gu[zi], op=ALU.mult),
                                 r=[("t1", zi), ("gu", zi)], w=[("p1", fcl)])
                        elif kind == "ga":
                            S.op("act", lambda e, bank=bank, zi=zi: e.activation(out=sg[zi], in_=PB[bank][:, 0:512], func=AF.Silu),
                                 r=[("pb", bank)], w=[("sg", zi)])
                            S.op("dve", lambda e, zi=zi, fcl=fcl: e.tensor_tensor(out=yst[zi], in0=p1[:, fcl, :], in1=sg[zi], op=ALU.mult),
                                 r=[("p1", fcl), ("sg", zi)], w=[("yst", zi)])
                            S.op("sp", lambda e, zi=zi, fc=fc: e.dma_start(out=yaT_sp[fc * 128:(fc + 1) * 128, tok0:tok0 + 512], in_=yst[zi]),
                                 r=[("yst", zi)], w=[("yaT_sp", st, fc)], dma="yst%d" % zi)
                        else:
                            S.op("act", lambda e, bank=bank, zi=zi: e.activation(out=sgst[zi], in_=PB[bank][:, 0:512], func=AF.Silu),
                                 r=[("pb", bank)], w=[("sgst", zi)])
                            S.op("sp", lambda e, zi=zi, fc=fc: e.dma_start(out=sgbT_sp[fc * 128:(fc + 1) * 128, tok0:tok0 + 512], in_=sgst[zi]),
                                 r=[("sgst", zi)], w=[("sgbT_sp", st, fc)], dma="sgst%d" % zi)
            flush_pend()
        S.fence()

        if c.stop < 3:
            continue
        for h in range(H):
            collective(GROUP4, kT_send[h * 256:(h + 1) * 256, :], kT_g[h].rearrange("r a t -> (r a) t"), r=[], w=["kT_g"], semkey="cck")
        for h in range(H):
            collective(GROUP4, v_send[h], v_g[h], r=[], w=["v_g"], semkey="ccv")
        for h in range(H):
            collective(GROUP4, qT_send[h * 256:(h + 1) * 256, :], qT_g[h].rearrange("r a t -> (r a) t"), r=[], w=["qT_g"], semkey="ccq")
        S.fence()

        if c.stop < 4:
            continue
        arena.top = PERSIST_TOP
        kT = arena.alloc((HC, S_), BF16)
        Vsb = arena.alloc((NKT, HPC, 257), BF16)
        qT = [arena.alloc((HC, 512), BF16) for _ in range(2)]
        PT = [arena.alloc(512, BF16) for _ in range(4)]
        PTd = [arena.alloc(512, BF16) for _ in range(4)]
        o1 = arena.alloc((4, 256), F32)
        oo = arena.alloc((4, 256), F32)
        junk = arena.alloc(256, BF16)
        rl = arena.alloc(8, F32)
        oss = arena.alloc(4, F32)
        ors = arena.alloc(4, F32)
        onb = arena.alloc((4, 256), BF16)
        onst = [arena.alloc((HC, 512), BF16) for _ in range(2)]

        def hp_of(e):
            if "hp" not in hp_cache:
                hp_cache["hp"] = e.partition_id() % RANKS
            return hp_cache["hp"]

        def private_copy(dst, src_g, key_src, key_dst, sem):
            nd = len(src_g.shape)
            names = " ".join("d%d" % i for i in range(nd))
            dn = " ".join("e%d" % i for i in range(len(dst.shape)))
            S.op("sp", lambda e: e.dma_start(
                out=dst.rearrange("%s -> (%s)" % (dn, dn)).rearrange("(p f) -> p f", f=4096),
                in_=src_g[bass.ds(hp_of(e), 1)].rearrange("%s -> (%s)" % (names, names)).rearrange("(p f) -> p f", f=4096)),
                 r=[key_src], w=[key_dst], dma=sem)

        private_copy(kT_my, kT_g.rearrange("(a b) r c t -> a b r c t", b=HPC), "kT_g", "kT_my", "pck")
        private_copy(v_my, v_g.rearrange("(a b) n e -> a b n e", b=HPC), "v_g", "v_my", "pcv")
        private_copy(qT_my, qT_g.rearrange("(a b) r c t -> a b r c t", b=HPC), "qT_g", "qT_my", "pcq")
        for hl in range(HPC):
            for r in range(RANKS):
                S.op("sp", lambda e, r=r, hl=hl: e.dma_start(
                    out=kT[:, hl * 2:hl * 2 + 2, r * TPC:(r + 1) * TPC],
                    in_=kT_my[hl, r].rearrange("(c d) t -> d c t", d=128)),
                     r=["kT_my"], w=[("kT", r, hl)], dma="ldk%d" % hl)
        for hl in range(HPC):
            S.op("sp", lambda e, hl=hl: e.dma_start(
                out=Vsb[:, :, hl, 0:256], in_=v_my[hl].rearrange("(kt p) e -> p kt e", p=128)),
                 r=["v_my"], w=[("V", hl)], dma="ldv%d" % hl)
        S.op("pool", lambda e: e.memset(Vsb[:, :, :, 256:257], 1.0), w=["Vone"])
        for kb in range(4):
            S.op("pool", lambda e, kb=kb: e.memset(PTd[kb], 0.0), w=[("PTd", kb)])
        scale = float(128 ** -0.5)
        ptc = 0
        sbank = 0
        deferred = []
        for Gq in range(NQG):
            rq, oq = Gq // NST, (Gq % NST) * 512
            qb_ = qT[Gq % 2]
            for hl in range(HPC):
                S.op("sp", lambda e, qb_=qb_, rq=rq, oq=oq, hl=hl: e.dma_start(
                    out=qb_[:, hl * 2:hl * 2 + 2, :],
                    in_=qT_my[hl, rq, :, oq:oq + 512].rearrange("(c d) t -> d c t", d=128)),
                     r=["qT_my"], w=[("qT", Gq % 2, hl)], dma="ldq%d_%d" % (Gq % 2, hl))
            ntile = 4 * Gq + 4
            ost = onst[Gq % 2]
            for hl in range(HPC):
                for cpt in range(2):
                    hc = hl * 2 + cpt
                    prev = None
                    for t in range(ntile + 1):
                        if t < ntile:
                            kb = t - 4 * Gq
                            c_lo = max(kb, 0) * 128
                            sb = 4 + (sbank % 2)
                            sbank += 1
                            S.op("pe", lambda e, sb=sb, hc=hc, t=t, c_lo=c_lo, qb_=qb_: e.matmul(
                                PB[sb][:, c_lo:512], kT[:, hc, t * 128:(t + 1) * 128], qb_[:, hc, c_lo:512],
                                start=True, stop=True), r=[("kT", rr_, hl) for rr_ in range(RANKS)] + [("qT", Gq % 2, hl)], w=[("pbs", sb)])
                            if kb >= 0:
                                P = PTd[kb]
                                pk = ("PTd", kb)
                                S.op("act", lambda e, P=P, sb=sb, c_lo=c_lo: e.activation(
                                    out=P[:, c_lo + 64:512], in_=PB[sb][:, c_lo + 64:512], func=AF.Exp, scale=scale),
                                     r=[("pbs", sb)], w=[pk])
                                S.op("act", lambda e, P=P, sb=sb, c_lo=c_lo: e.activation(
                                    out=P[0:64, c_lo:c_lo + 64], in_=PB[sb][0:64, c_lo:c_lo + 64], func=AF.Exp, scale=scale),
                                     r=[("pbs", sb)], w=[pk])
                            else:
                                P = PT[ptc % 4]
                                pk = ("PT", ptc % 4)
                                ptc += 1
                                S.op("act", lambda e, P=P, sb=sb: e.activation(out=P, in_=PB[sb][:, 0:512], func=AF.Exp, scale=scale),
                                     r=[("pbs", sb)], w=[pk])
                            cur = (t, max(kb, 0), P, pk)
                        else:
                            cur = None
                        if prev is not None:
                            pt_, qlo, P_, pk_ = prev
                            for qb in range(qlo, 4):
                                last_t = 4 * Gq + qb
                                S.op("pe", lambda e, qb=qb, P_=P_, pt_=pt_, hl=hl, last_t=last_t: e.matmul(
                                    PB[qb][:, 0:257], P_[:, qb * 128:(qb + 1) * 128], Vsb[:, pt_, hl, :],
                                    start=(pt_ == 0), stop=(pt_ == last_t)),
                                     r=[pk_, ("V", hl), "Vone"], w=[("pbo", qb)])
                        prev = cur
                        if t == 1 and deferred:
                            for f in deferred:
                                f()
                            del deferred[:]
                    for qb in range(4):
                        if cpt == 0:
                            S.op("dve", lambda e, qb=qb: e.reciprocal(out=rl[:, qb:qb + 1], in_=PB[qb][:, 256:257]),
                                 r=[("pbo", qb)], w=[("rl", qb)])
                            S.op("dve", lambda e, qb=qb: e.tensor_scalar_mul(out=o1[:, qb, :], in0=PB[qb][:, 0:256], scalar1=rl[:, qb:qb + 1]),
                                 r=[("pbo", qb), ("rl", qb)], w=[("o1", qb)])
                        else:
                            S.op("dve", lambda e, qb=qb: e.reciprocal(out=rl[:, 4 + qb:5 + qb], in_=PB[qb][:, 256:257]),
                                 r=[("pbo", qb)], w=[("rl2", qb)])
                            S.op("dve", lambda e, qb=qb: e.tensor_tensor(out=rl[:, 4 + qb:5 + qb], in0=rl[:, 4 + qb:5 + qb], in1=neg_lam, op=ALU.mult),
                                 r=[("rl2", qb), "neg_lam"], w=[("rl2", qb)])
                            S.op("dve", lambda e, qb=qb: e.scalar_tensor_tensor(
                                out=oo[:, qb, :], in0=PB[qb][:, 0:256], scalar=rl[:, 4 + qb:5 + qb], in1=o1[:, qb, :],
                                op0=ALU.mult, op1=ALU.add), r=[("pbo", qb), ("rl2", qb), ("o1", qb)], w=[("oo", qb)])
                            S.op("act", lambda e, qb=qb: e.activation(out=junk, in_=oo[:, qb, :], func=AF.Square, accum_out=oss[:, qb:qb + 1]),
                                 r=[("oo", qb)], w=["junk", ("oss", qb)])
                            S.op("act", lambda e, qb=qb: e.activation(out=ors[:, qb:qb + 1], in_=oss[:, qb:qb + 1], func=AF.Ln,
                                                                      scale=1.0 / 256, bias=eps_t), r=[("oss", qb), "eps"], w=[("ors", qb)])
                            S.op("act", lambda e, qb=qb: e.activation(out=ors[:, qb:qb + 1], in_=ors[:, qb:qb + 1], func=AF.Exp, scale=-0.5),
                                 r=[("ors", qb)], w=[("ors", qb)])
                            S.op("dve", lambda e, qb=qb: e.scalar_tensor_tensor(
                                out=onb[:, qb, :], in0=oo[:, qb, :], scalar=ors[:, qb:qb + 1], in1=gsub, op0=ALU.mult, op1=ALU.mult),
                                 r=[("oo", qb), ("ors", qb), "gsub"], w=[("onb", qb)])
                    if cpt == 1:
                        def make_deferred(Gq=Gq, hl=hl, ost=ost):
                            def f():
                                for qb in range(4):
                                    for half in range(2):
                                        S.op("pe", lambda e, qb=qb, half=half: e.transpose(
                                            PBb[6][:, (qb * 2 + half) * 128:(qb * 2 + half + 1) * 128],
                                            onb[:, qb, half * 128:(half + 1) * 128], ident),
                                             r=[("onb", qb), "ident"], w=[("pbt", qb, half)])
                                for half in range(2):
                                    S.op("dve", lambda e, half=half: e.tensor_copy(
                                        out=ost[:, hl * 2 + half, :].rearrange("p (a b) -> p a b", a=4),
                                        in_=PBb[6][:, 0:1024].rearrange("p (a h b) -> p a h b", a=4, h=2)[:, :, half, :]),
                                         r=[("pbt", qb, half) for qb in range(4)], w=[("onst", Gq % 2, hl, half)])
                                if hl == HPC - 1:
                                    okeys = [("onst", Gq % 2, h_, f_) for h_ in range(HPC) for f_ in range(2)]
                                    S.op("sp", lambda e: e.dma_start(
                                        out=onT_send[Gq // NST, :, (Gq % NST) * 512:(Gq % NST + 1) * 512].rearrange("(a d) n -> d a n", d=128), in_=ost),
                                         r=okeys, w=[("onT_send", Gq)] + okeys, dma="onst%d" % (Gq % 2))
                            return f
                        deferred.append(make_deferred())
        for f in deferred:
            f()
        del deferred[:]
        S.fence()
        if c.stop < 5:
            continue
        for tq_ in range(RANKS):
            for hl in range(HPC):
                collective(GROUP4, onT_send[tq_, hl * 256:(hl + 1) * 256, :], onT_g[tq_, hl].rearrange("r a t -> (r a) t"),
                           r=[], w=["onT_g"], semkey="ccon")
        S.fence()
        if c.stop < 6:
            continue

        private_copy(on_my, onT_g, "onT_g", "on_my", "pco")
        arena.top = PERSIST_TOP
        wos = [arena.alloc((NCM, 512), BF16) for _ in range(2)]
        yT = arena.alloc((NCM, 512), BF16)
        sgT = arena.alloc((NCB, 512), BF16)
        xres = [arena.alloc(512, F32) for _ in range(4)]
        xo = [arena.alloc(512, F32) for _ in range(2)]
        ND = DM // 512
        xc = 0
        acc = 0
        for st in range(NST):
            tok0 = st * 512
            S.op("sp", lambda e, tok0=tok0: e.dma_start(
                out=yT[:, 0:NCA, :], in_=yaT_sp[:, tok0:tok0 + 512].rearrange("(a d) n -> d a n", d=128)),
                 w=["yaT"], dma="ldya")
            for hl in range(HPC):
                for r in range(RANKS):
                    c0 = NCA + (r * HPC + hl) * 2
                    S.op("sp", lambda e, tok0=tok0, hl=hl, r=r, c0=c0: e.dma_start(
                        out=yT[:, c0:c0 + 2, :],
                        in_=on_my[hl, r, :, tok0:tok0 + 512].rearrange("(c d) t -> d c t", d=128)),
                         r=["on_my"], w=[("onT", c0 - NCA), ("onT", c0 - NCA + 1), ("ybT", c0 - NCA), ("ybT", c0 - NCA + 1)], dma="ldon%d" % (r * HPC + hl))
            S.op("sp", lambda e, tok0=tok0: e.dma_start(
                out=sgT, in_=sgbT_sp[:, tok0:tok0 + 512].rearrange("(a d) n -> d a n", d=128)),
                 w=["sgT"], dma="ldsg")
            for cb in range(NCB):
                eng = "dve" if cb % 2 == 0 else "pool"
                S.op(eng, lambda e, cb=cb: e.tensor_tensor(out=yT[:, NCA + cb, :], in0=yT[:, NCA + cb, :], in1=sgT[:, cb, :], op=ALU.mult),
                     r=[("onT", cb), "sgT"], w=[("ybT", cb)])
            ykeys = ["yaT"] + [("ybT", cb) for cb in range(NCB)]
            for dg in range(ND):
                sl = (st * ND + dg) % 2
                S.op("sp", lambda e, sl=sl, dg=dg: e.dma_start(out=wos[sl].rearrange("p k n -> p (k n)"), in_=wg_out[l][dg]),
                     r=[("wout%d" % l, "g")], w=[("wo", sl)], dma="wo%d" % sl)
                for j in range(4):
                    r0 = tok0 + j * 128
                    xi = xc % 4
                    xoi = xc % 2
                    xc += 1
                    S.op("sp", lambda e, xi=xi, r0=r0, dg=dg: e.dma_start(out=xres[xi], in_=x_cur[r0:r0 + 128, dg * 512:(dg + 1) * 512]),
                         w=[("xres", xi)], dma="xres%d" % xi)
                    bank = acc % 4
                    acc += 1
                    for kc in range(NCM):
                        S.op("pe", lambda e, bank=bank, kc=kc, j=j, sl=sl: e.matmul(
                            PB[bank][:, 0:512], yT[:, kc, j * 128:(j + 1) * 128], wos[sl][:, kc, :],
                            start=(kc == 0), stop=(kc == NCM - 1)), r=ykeys + [("wo", sl)], w=[("pb", bank)])
                    S.op("dve", lambda e, bank=bank, xi=xi, xoi=xoi: e.tensor_tensor(out=xo[xoi], in0=PB[bank][:, 0:512], in1=xres[xi], op=ALU.add),
                         r=[("pb", bank), ("xres", xi)], w=[("xo", xoi)])
                    S.op("sp", lambda e, xoi=xoi, r0=r0, dg=dg: e.dma_start(out=x_next[r0:r0 + 128, dg * 512:(dg + 1) * 512], in_=xo[xoi]),
                         r=[("xo", xoi)], w=[("x_next", st, j, dg)], dma="xo%d" % xoi)
        S.fence()

    arena.top = PERSIST_TOP
    fg = arena.alloc(DM, F32)
    xf = [arena.alloc(DM, F32) for _ in range(2)]
    yf = [arena.alloc(DM, F32) for _ in range(2)]
    fss = arena.alloc(2, F32)
    frs = arena.alloc(2, F32)
    S.op("sp", lambda e: e.dma_start(out=fg, in_=fing_in.partition_broadcast(128)), w=["fg"], dma="ldc0")
    x_last = xs[DEPTH - 1]
    for blk in range(NBLK):
        i = blk % 2
        r0 = blk * 128
        S.op("sp", lambda e, i=i, r0=r0: e.dma_start(out=xf[i], in_=x_last[r0:r0 + 128, :]), w=[("xf", i)], dma="xf%d" % i)
        S.op("act", lambda e, i=i: e.activation(out=yf[i], in_=xf[i], func=AF.Square, accum_out=fss[:, i:i + 1]),
             r=[("xf", i)], w=[("yf", i), ("fss", i)])
        S.op("act", lambda e, i=i: e.activation(out=frs[:, i:i + 1], in_=fss[:, i:i + 1], func=AF.Sqrt, scale=1.0 / DM, bias=eps_t),
             r=[("fss", i), "eps"], w=[("frs", i)])
        S.op("dve", lambda e, i=i: e.reciprocal(out=frs[:, i:i + 1], in_=frs[:, i:i + 1]), r=[("frs", i)], w=[("frs", i)])
        S.op("dve", lambda e, i=i: e.scalar_tensor_tensor(out=yf[i], in0=xf[i], scalar=frs[:, i:i + 1], in1=fg, op0=ALU.mult, op1=ALU.mult),
             r=[("xf", i), ("frs", i), "fg"], w=[("yf", i)])
        S.op("sp", lambda e, i=i, r0=r0: e.dma_start(out=out_d[r0:r0 + 128, :], in_=yf[i]), r=[("yf", i)], w=[("out", blk)], dma="yf%d" % i)
    if c.debug:
        S.fence()
        for nm, src in (("qT", qT_send), ("kT", kT_send), ("v", v_send), ("on", onT_send), ("ya", yaT_sp),
                        ("sgb", sgbT_sp), ("x0", xs[0]), ("ong", onT_g)):
            if isinstance(c.debug, (list, tuple)) and nm not in c.debug:
                continue
            nd = len(src.shape)
            names = " ".join("d%d" % i for i in range(nd))
            tot = 1
            for v_ in src.shape:
                tot *= v_
            flat = lambda ap_, names=names: ap_.rearrange("%s -> (%s)" % (names, names)).rearrange("(p f) -> p f", p=128)
            dd = nc.dram_tensor("dbg_" + nm, [128, tot // 128], src.dtype, kind="ExternalOutput").ap()
            S.op("pool", lambda e, dd=dd, src=src, flat=flat: e.dma_start(out=dd, in_=flat(src)),
                 w=[("dbg", nm)], dma="dbg_" + nm)
    S.fence(final=True)

    keys = S.finalize()
    with nc.cleanup_on_exit():
        sems = {k: nc.alloc_semaphore(name="s%d" % i) for i, k in enumerate(keys)}
        for k in keys:
            nc.gpsimd.sem_clear(sems[k])
        nc.all_engine_barrier()
        with nc.Block() as block:
            S.emit(block, sems)
        nc.all_engine_barrier()
    for cx in reversed(ctxs):
        cx.__exit__(None, None, None)
    return nc, len(S.ops), len(keys)


def make_in_maps(cfg, x, positions, norm_g, w_in, ln_g, ln_b, sgu_w, sgu_b,
                 lam_q1, lam_k1, lam_q2, lam_k2, subln_g, w_out, final_g):
    c = cfg
    f = np.float32
    x = np.asarray(x, f)
    positions = np.asarray(positions, np.int32)
    DEPTH = c.DEPTH
    invf = (np.float32(ROPE_THETA) ** (-np.arange(0, 32, 2, dtype=np.float32) / np.float32(32))).astype(f)[None, :]
    ngc = np.ascontiguousarray(np.asarray(norm_g, f).reshape(DEPTH, c.KC, 128).transpose(0, 2, 1))
    lgc = np.ascontiguousarray(np.asarray(ln_g, f).reshape(DEPTH, c.NCA, 128).transpose(0, 2, 1))
    lbc = np.ascontiguousarray(np.asarray(ln_b, f).reshape(DEPTH, c.NCA, 128).transpose(0, 2, 1))
    sguw = np.ascontiguousarray(np.asarray(sgu_w, f))
    sgub = np.ascontiguousarray(np.asarray(sgu_b, f).reshape(DEPTH, 1, c.G * 128))
    lam = np.ascontiguousarray(np.concatenate([np.asarray(a, f) for a in (lam_q1, lam_k1, lam_q2, lam_k2)], axis=1)
                               .reshape(DEPTH, 1, 4 * 128))
    subg = np.ascontiguousarray(np.asarray(subln_g, f).reshape(DEPTH, 1, 256))
    fing = np.ascontiguousarray(np.asarray(final_g, f).reshape(1, c.DM))
    w_in = np.asarray(w_in, f)
    w_out = np.asarray(w_out, f)
    maps = []
    ri, ro = c.DM // RANKS, c.DMIX // RANKS
    for core in range(NCORES):
        b, r = core // RANKS, core % RANKS
        t0 = r * c.TPC
        maps.append({
            "x": np.ascontiguousarray(x[b, t0:t0 + c.TPC, :]),
            "pos_t": np.ascontiguousarray(positions[b, t0:t0 + c.TPC].reshape(c.NBLK, 128).T),
            "invf": invf, "norm_gc": ngc, "ln_gc": lgc, "ln_bc": lbc, "sgu_w": sguw, "sgu_b": sgub,
            "lam": lam, "subln_g": subg, "final_g": fing,
            "w_in_sh": np.ascontiguousarray(w_in.reshape(DEPTH, c.KC, RANKS, 32, c.DIN)[:, :, r].reshape(DEPTH, c.KC * 32, c.DIN)),
            "w_out_sh": np.ascontiguousarray(w_out.reshape(DEPTH, c.NCM, RANKS, 32, c.DM)[:, :, r].reshape(DEPTH, c.NCM * 32, c.DM)),
        })
    return maps


_CACHE = {}


def run(cfg, inputs):
    key = (cfg.S, cfg.DM, cfg.G, cfg.H, cfg.DEPTH, cfg.stop)
    if key not in _CACHE:
        _CACHE[key] = build_program(cfg)[0]
    nc = _CACHE[key]
    maps = make_in_maps(cfg, **inputs)
    res = run_bass_kernel_spmd(nc, maps, core_ids=list(range(NCORES)))
    out = np.empty((2, cfg.S, cfg.DM), np.float32)
    for core in range(NCORES):
        b, r = core // RANKS, core % RANKS
        out[b, r * cfg.TPC:(r + 1) * cfg.TPC, :] = res.results[core]["out"]
    if cfg.debug:
        return out, res.results
    return out


def kernel(**inputs):
    return run(Cfg(), inputs)
```

```python
import math
import os
import numpy as np
import concourse.bass as bass
import concourse.mybir as mybir
from concourse.bass_utils import run_bass_kernel_spmd

F32 = mybir.dt.float32
BF16 = mybir.dt.bfloat16
I32 = mybir.dt.int32
AF = mybir.ActivationFunctionType
ALU = mybir.AluOpType
AX = mybir.AxisListType

EPS = 1e-5
ROPE_THETA = 500000.0
NCORES = 8
RANKS = 4


class Cfg:
    def __init__(self, S=8192, DM=4096, G=8, H=8, DEPTH=2, debug=False, stop=99):
        self.S, self.DM, self.G, self.H, self.DEPTH = S, DM, G, H, DEPTH
        self.debug = debug
        self.stop = stop
        self.DA = G * 256
        self.DB = H * 256
        self.DMIX = self.DA + self.DB
        self.DIN = 3 * self.DA + 4 * self.DB
        self.TPC = S // RANKS
        self.NST = self.TPC // 512
        self.NBLK = self.TPC // 128
        self.KC = DM // 128
        self.HPC = H // RANKS
        self.NCA = self.DA // 128
        self.NCB = self.DB // 128
        self.NCM = self.DMIX // 128


import types


def _freeze(fn):
    if fn.__closure__ is None:
        return fn
    cells = []
    for cl in fn.__closure__:
        try:
            cells.append(types.CellType(cl.cell_contents))
        except ValueError:
            cells.append(cl)
    return types.FunctionType(fn.__code__, fn.__globals__, fn.__name__, fn.__defaults__, tuple(cells))


class _Op:
    __slots__ = ("eng", "fn", "deps", "kind", "semkey", "count", "signalled", "inc", "idx")

    def __init__(self, eng, fn, kind, semkey, inc):
        self.eng, self.fn, self.kind, self.semkey, self.inc = eng, fn, kind, semkey, inc
        self.deps = []
        self.count = None
        self.signalled = False


class Sched:
    ENGS = ("pe", "act", "dve", "pool", "sp")

    def __init__(self):
        self.ops = []
        self.last_writer = {}
        self.readers = {}
        self.epoch = 0
        self.last_eng_op = {}
        self.async_since_fence = []
        self.ncc = 0
        self.persist = {}
        self.persist_ops = []

    def op(self, eng, fn, r=(), w=(), dma=None, inc=None, persist=False):
        kind = "d" if dma is not None else "c"
        semkey = ("dma", dma) if dma is not None else ("eng", eng, self.epoch)
        o = _Op(eng, _freeze(fn), kind, semkey, inc if inc is not None else (16 if kind == "d" else 1))
        deps = {}
        for k in r:
            lw = self.last_writer.get(k)
            if lw is None:
                lw = self.persist.get(k)
            if lw is not None:
                deps[id(lw)] = lw
        for k in w:
            lw = self.last_writer.get(k)
            if lw is not None:
                deps[id(lw)] = lw
            for rd in self.readers.get(k, ()):
                deps[id(rd)] = rd
        o.deps = list(deps.values())
        for k in w:
            self.last_writer[k] = o
            self.readers[k] = []
        for k in r:
            self.readers.setdefault(k, []).append(o)
        self.ops.append(o)
        if persist:
            for k in w:
                self.persist[k] = o
            self.persist_ops.append(o)
        elif kind == "d":
            self.async_since_fence.append(o)
        else:
            self.last_eng_op[eng] = o
        return o

    def fence(self, final=False, skip=()):
        tails = [o for e_, o in self.last_eng_op.items() if e_ not in skip] + list(self.async_since_fence)
        if final:
            tails += self.persist_ops
        new_last = {e_: o for e_, o in self.last_eng_op.items() if e_ in skip}
        for e in self.ENGS:
            if e in skip:
                continue
            o = _Op(e, lambda eng: eng.nop(), "c", ("eng", e, self.epoch), 1)
            o.deps = [t for t in tails]
            self.ops.append(o)
            new_last[e] = o
        self.last_eng_op = new_last
        self.async_since_fence = []
        self.last_writer = {}
        self.readers = {}

    def finalize(self):
        for i, o in enumerate(self.ops):
            o.idx = i
        for o in self.ops:
            best = {}
            for d in o.deps:
                b = best.get(d.semkey)
                if b is None or d.idx > b.idx:
                    best[d.semkey] = d
            o.deps = list(best.values())
        for o in self.ops:
            for d in o.deps:
                if d.kind == "d":
                    d.signalled = True
                elif d.eng != o.eng or o.kind == "d" or d.eng != "pe":
                    d.signalled = True
        counts = {}
        for o in self.ops:
            if o.kind == "d":
                o.signalled = True
            if o.signalled:
                c = counts.get(o.semkey, 0) + o.inc
                counts[o.semkey] = c
                o.count = c
        self.counts = counts
        return list(counts.keys())

    def emit(self, block, sems):
        per_eng = {e: [] for e in self.ENGS}
        for o in self.ops:
            per_eng[o.eng].append(o)

        def run(eng_name):
            def body(e):
                waited = {}
                for o in per_eng[eng_name]:
                    need = {}
                    for d in o.deps:
                        if not d.signalled:
                            continue
                        if d.kind == "c" and d.eng == eng_name and eng_name == "pe" and o.kind == "c":
                            continue
                        if need.get(d.semkey, 0) < d.count:
                            need[d.semkey] = d.count
                    for sk, c in need.items():
                        if waited.get(sk, 0) >= c:
                            continue
                        e.wait_ge(sems[sk], c)
                        waited[sk] = c
                    ins = o.fn(e)
                    if o.signalled:
                        ins.then_inc(sems[o.semkey], o.inc)
            return body

        block.tensor(run("pe"))
        block.scalar(run("act"))
        block.vector(run("dve"))
        block.gpsimd(run("pool"))
        block.sync(run("sp"))


class Arena:
    def __init__(self, ap, nbytes):
        self.ap, self.cap, self.top = ap, nbytes, 0

    def alloc(self, shape, dt):
        if isinstance(shape, int):
            shape = (shape,)
        n = 1
        for s in shape:
            n *= s
        nb = n * (2 if dt == BF16 else 4)
        off = self.top
        self.top = off + (nb + 63) // 64 * 64
        assert self.top <= self.cap, ("SBUF arena overflow", self.top, self.cap)
        v = self.ap[:, off // 2:(off + nb) // 2]
        if dt != BF16:
            v = v.bitcast(dt)
        if len(shape) == 2:
            v = v.rearrange("p (a b) -> p a b", a=shape[0], b=shape[1])
        elif len(shape) == 3:
            v = v.rearrange("p (a b c) -> p a b c", a=shape[0], b=shape[1], c=shape[2])
        return v


def lambda_init_fn(layer_idx):
    return 0.8 - 0.6 * math.exp(-0.3 * layer_idx)


def build_program(cfg):
    c = cfg
    S_, DM, G, H, DEPTH = c.S, c.DM, c.G, c.H, c.DEPTH
    DA, DB, DMIX, DIN = c.DA, c.DB, c.DMIX, c.DIN
    TPC, NST, NBLK, KC, HPC = c.TPC, c.NST, c.NBLK, c.KC, c.HPC
    NCA, NCB, NCM = c.NCA, c.NCB, c.NCM
    NKT = S_ // 128
    NQG = S_ // 512
    HC = HPC * 2
    OFF_U, OFF_V, OFF_GA = 0, DA, 2 * DA
    OFF_Q, OFF_K, OFF_VB, OFF_GB = 3 * DA, 3 * DA + DB, 3 * DA + 2 * DB, 3 * DA + 3 * DB

    nc = bass.Bass("TRN2", target_bir_lowering=False)

    def din(name, shape, dt=F32):
        return nc.dram_tensor(name, list(shape), dt, kind="ExternalInput").ap()

    x_in = din("x", [TPC, DM])
    pos_in = din("pos_t", [128, NBLK], I32)
    invf_in = din("invf", [1, 16])
    ngc_in = din("norm_gc", [DEPTH, 128, KC])
    lgc_in = din("ln_gc", [DEPTH, 128, NCA])
    lbc_in = din("ln_bc", [DEPTH, 128, NCA])
    sguw_in = din("sgu_w", [DEPTH, G, 128, 128])
    sgub_in = din("sgu_b", [DEPTH, 1, G * 128])
    lam_in = din("lam", [DEPTH, 1, 4 * 128])
    subg_in = din("subln_g", [DEPTH, 1, 256])
    fing_in = din("final_g", [1, DM])
    win_in = din("w_in_sh", [DEPTH, KC * 32, DIN])
    wout_in = din("w_out_sh", [DEPTH, NCM * 32, DM])
    out_d = nc.dram_tensor("out", [TPC, DM], F32, kind="ExternalOutput").ap()

    def dscr(name, shape, dt=BF16):
        return nc.dram_tensor(name, list(shape), dt).ap()

    wsh_in = [dscr("wsh_in%d" % l, [DIN // 512, 32, KC, 512]) for l in range(DEPTH)]
    wsh_out = [dscr("wsh_out%d" % l, [DM // 512, 32, NCM, 512]) for l in range(DEPTH)]
    wg_in = [dscr("wg_in%d" % l, [DIN // 512, 128, KC * 512]) for l in range(DEPTH)]
    wg_out = [dscr("wg_out%d" % l, [DM // 512, 128, NCM * 512]) for l in range(DEPTH)]
    xs = [dscr("xs%d" % l, [TPC, DM], F32) for l in range(DEPTH)]
    qT_send = dscr("qT_send", [DB, TPC])
    kT_send = dscr("kT_send", [DB, TPC])
    v_send = dscr("v_send", [H, TPC, 256])
    qT_g = dscr("qT_g", [H, RANKS, 256, TPC])
    kT_g = dscr("kT_g", [H, RANKS, 256, TPC])
    v_g = dscr("v_g", [H, RANKS * TPC, 256])
    onT_send = dscr("onT_send", [RANKS, HPC * 256, TPC])
    onT_g = dscr("onT_g", [RANKS, HPC, RANKS, 256, TPC])
    kT_my = dscr("kT_my", [HPC, RANKS, 256, TPC])
    qT_my = dscr("qT_my", [HPC, RANKS, 256, TPC])
    v_my = dscr("v_my", [HPC, RANKS * TPC, 256])
    on_my = dscr("on_my", [HPC, RANKS, 256, TPC])
    yaT_sp = dscr("yaT_sp", [DA, TPC])
    sgbT_sp = dscr("sgbT_sp", [DB, TPC])

    ARENA_BYTES = int(os.environ.get("ARENA_KB", "206")) * 1024
    ctxs = [nc.sbuf_tensor("arena", [128, ARENA_BYTES // 2], BF16)]
    ctxs += [nc.psum_tensor("pb%d" % i, [128, 512], F32) for i in range(8)]
    ents = [cx.__enter__() for cx in ctxs]
    arena = Arena(ents[0], ARENA_BYTES)
    PB = ents[1:9]
    PBb = [p[:, 0:512].bitcast(BF16) for p in PB]

    S = Sched()
    hp_cache = {}
    GROUP4 = [[0, 1, 2, 3], [4, 5, 6, 7]]
    GROUP8 = [list(range(8))]

    def collective(groups, src, dst, r, w, semkey, persist=False):
        S.op("pool", lambda e: e.collective_compute("AllGather", ALU.bypass, replica_groups=groups,
                                                    ins=[src.opt()], outs=[dst.opt()]),
             r=r, w=w, dma=semkey, inc=1, persist=persist)

    def dump(name, ap, rkeys):
        if not (isinstance(c.debug, (list, tuple)) and ("dump_" + name) in c.debug):
            return
        shp = list(ap.shape)
        n = 1
        for v_ in shp[1:]:
            n *= v_
        dd = nc.dram_tensor("dump_" + name, [shp[0], n], ap.dtype, kind="ExternalOutput").ap()
        src = ap
        if len(shp) == 3:
            dd = dd.rearrange("p (a b) -> p a b", a=shp[1])
        elif len(shp) == 4:
            dd = dd.rearrange("p (a b c) -> p a b c", a=shp[1], b=shp[2])
        S.op("sp", lambda e: e.dma_start(out=dd, in_=src), r=rkeys, w=[("dump", name)], dma="dump_" + name)

    ident = arena.alloc(128, BF16)
    ones_bf = arena.alloc(128, BF16)
    eps_t = arena.alloc(1, F32)
    cos_t = arena.alloc((NBLK, 16), F32)
    sin_t = arena.alloc((NBLK, 16), F32)
    gcol = arena.alloc(KC, F32)
    lgc = arena.alloc(NCA, F32)
    lbc = arena.alloc(NCA, F32)
    wmT = arena.alloc((G, 128), BF16)
    Cb = arena.alloc((NCA, 128), F32)
    gsub = arena.alloc(256, F32)
    neg_lam = arena.alloc(1, F32)
    PERSIST_TOP = arena.top

    def cast_gather(src3, l, sh, g, nchunk, cols, tag):
        src = src3[l]
        ntile = cols // 512
        ckeys = []
        for kc in range(nchunk):
            S.op("pool", lambda e, kc=kc: e.dma_start(
                out=sh[:, :, kc, :].rearrange("t i n -> i t n"),
                in_=src[kc * 32:(kc + 1) * 32, :].rearrange("i (t n) -> i t n", n=512)),
                 w=[(tag, "sh", kc)], dma="cast", persist=True)
            ckeys.append((tag, "sh", kc))
        for t in range(ntile):
            collective(GROUP4, sh[t].rearrange("i k n -> i (k n)"), g[t],
                       r=(ckeys if t == 0 else []) + ([(tag, "thr", t - 8)] if t >= 8 else []),
                       w=[(tag, "g", t), (tag, "thr", t)], semkey="cc_" + tag, persist=True)

    S.op("pool", lambda e: e.memset(ident, 0.0), w=["ident0"], persist=True)
    S.op("pool", lambda e: e.affine_select(out=ident, in_=ident, compare_op=ALU.not_equal, fill=1.0,
                                           base=0, pattern=[[-1, 128]], channel_multiplier=1),
         r=["ident0"], w=["ident"], persist=True)
    S.op("pool", lambda e: e.memset(ones_bf, 1.0), w=["ones"], persist=True)
    S.op("pool", lambda e: e.memset(eps_t, EPS), w=["eps"], persist=True)

    def gather_layer(l):
        cast_gather(win_in, l, wsh_in[l], wg_in[l], KC, DIN, "win%d" % l)
        cast_gather(wout_in, l, wsh_out[l], wg_out[l], NCM, DM, "wout%d" % l)

    gather_layer(0)

    mark = arena.top
    pos_sb = arena.alloc(NBLK, I32)
    posf = arena.alloc(NBLK, F32)
    invf = arena.alloc(16, F32)
    ang = arena.alloc((NBLK, 16), F32)
    tq = arena.alloc((NBLK, 16), F32)
    ki = arena.alloc((NBLK, 16), I32)
    kf = arena.alloc((NBLK, 16), F32)
    rr = arena.alloc((NBLK, 16), F32)
    r2 = arena.alloc((NBLK, 16), F32)
    mk = arena.alloc((NBLK, 16), F32)
    S.op("sp", lambda e: e.dma_start(out=pos_sb, in_=pos_in[:, :]), w=["pos"], dma="ldc0")
    S.op("sp", lambda e: e.dma_start(out=invf, in_=invf_in.partition_broadcast(128)), w=["invf"], dma="ldc1")
    S.op("dve", lambda e: e.tensor_copy(out=posf, in_=pos_sb), r=["pos"], w=["posf"])
    S.op("dve", lambda e: e.tensor_tensor(out=ang, in0=posf.unsqueeze(2).broadcast_to([128, NBLK, 16]),
                                          in1=invf.unsqueeze(1).broadcast_to([128, NBLK, 16]), op=ALU.mult),
         r=["posf", "invf"], w=["ang"])
    TWO_PI = float(2 * np.pi)
    C1 = 6.28125
    C2 = float(2 * np.pi - 6.28125)
    S.op("dve", lambda e: e.tensor_scalar(out=tq, in0=ang, scalar1=float(1 / (2 * np.pi)), scalar2=None, op0=ALU.mult),
         r=["ang"], w=["tq"])
    S.op("dve", lambda e: e.tensor_copy(out=ki, in_=tq), r=["tq"], w=["ki"])
    S.op("dve", lambda e: e.tensor_copy(out=kf, in_=ki), r=["ki"], w=["kf"])
    S.op("dve", lambda e: e.scalar_tensor_tensor(out=rr, in0=kf, scalar=-C1, in1=ang, op0=ALU.mult, op1=ALU.add),
         r=["kf", "ang"], w=["rr"])
    S.op("dve", lambda e: e.scalar_tensor_tensor(out=rr, in0=kf, scalar=-C2, in1=rr, op0=ALU.mult, op1=ALU.add),
         r=["kf", "rr"], w=["rr"])

    def wrap(buf, key):
        S.op("dve", lambda e: e.tensor_single_scalar(out=mk, in_=buf, scalar=float(np.pi), op=ALU.is_gt), r=[key], w=["mk"])
        S.op("dve", lambda e: e.scalar_tensor_tensor(out=buf, in0=mk, scalar=-TWO_PI, in1=buf, op0=ALU.mult, op1=ALU.add),
             r=["mk", key], w=[key])
        S.op("dve", lambda e: e.tensor_single_scalar(out=mk, in_=buf, scalar=-float(np.pi), op=ALU.is_lt), r=[key], w=["mk"])
        S.op("dve", lambda e: e.scalar_tensor_tensor(out=buf, in0=mk, scalar=TWO_PI, in1=buf, op0=ALU.mult, op1=ALU.add),
             r=["mk", key], w=[key])

    wrap(rr, "rr")
    S.op("dve", lambda e: e.tensor_scalar(out=r2, in0=rr, scalar1=float(np.pi / 2), scalar2=None, op0=ALU.add), r=["rr"], w=["r2"])
    wrap(r2, "r2")
    S.op("act", lambda e: e.activation(out=sin_t, in_=rr, func=AF.Sin), r=["rr"], w=["sin"])
    S.op("act", lambda e: e.activation(out=cos_t, in_=r2, func=AF.Sin), r=["r2"], w=["cos"])
    S.persist["sin"] = S.last_writer["sin"]
    S.persist["cos"] = S.last_writer["cos"]
    dump("ident0", ident, ["ident"])
    dump("eps0", eps_t, ["eps"])
    dump("ones0", ones_bf, ["ones"])
    dump("cos", cos_t, ["cos"])
    dump("sin", sin_t, ["sin"])
    S.fence(skip=("pool",))
    arena.top = mark

    for l in range(DEPTH if c.stop >= 1 else 0):
        x_cur = x_in if l == 0 else xs[l - 1]
        x_next = xs[l]
        S.epoch = l + 1
        lam_init = lambda_init_fn(l)

        arena.top = PERSIST_TOP
        w_sb = arena.alloc((G, 128), F32)
        wm_b = arena.alloc((G, 128), BF16)
        bs_bc = arena.alloc((G, 128), F32)
        lamv = arena.alloc((4, 128), F32)
        lprod = arena.alloc((2, 128), F32)
        lsum = arena.alloc(2, F32)
        lexp = arena.alloc(2, F32)
        S.op("sp", lambda e, l=l: e.dma_start(out=gcol, in_=ngc_in[l]), w=["gcol"], dma="ldc0")
        S.op("sp", lambda e, l=l: e.dma_start(out=lgc, in_=lgc_in[l]), w=["lgc"], dma="ldc1")
        S.op("sp", lambda e, l=l: e.dma_start(out=lbc, in_=lbc_in[l]), w=["lbc"], dma="ldc2")
        S.op("sp", lambda e, l=l: e.dma_start(out=w_sb, in_=sguw_in[l].rearrange("g i j -> i g j")), w=["w_sb"], dma="ldc3")
        S.op("sp", lambda e, l=l: e.dma_start(out=bs_bc.rearrange("p g i -> p (g i)"), in_=sgub_in[l].partition_broadcast(128)),
             w=["bs_bc"], dma="ldc4")
        S.op("sp", lambda e, l=l: e.dma_start(out=lamv.rearrange("p a b -> p (a b)"), in_=lam_in[l].partition_broadcast(128)),
             w=["lamv"], dma="ldc5")
        S.op("sp", lambda e, l=l: e.dma_start(out=gsub, in_=subg_in[l].partition_broadcast(128)), w=["gsub"], dma="ldc6")
        S.op("dve", lambda e: e.memset(w_sb[0:64, :, 64:128], 0.0), r=["w_sb"], w=["w_sb"])
        S.op("dve", lambda e: e.tensor_copy(out=wm_b, in_=w_sb), r=["w_sb"], w=["wm_b"])
        for g in range(G):
            S.op("pe", lambda e, g=g: e.transpose(PBb[6][:, g * 128:(g + 1) * 128], wm_b[:, g, :], ident),
                 r=["wm_b", "ident"], w=[("pb6", g)])
        S.op("dve", lambda e: e.tensor_copy(out=wmT.rearrange("p g i -> p (g i)"), in_=PBb[6][:, 0:G * 128]),
             r=[("pb6", g) for g in range(G)], w=["wmT"])
        if l == 0:
            dump("w_sb", w_sb, ["w_sb"])
            dump("wm_b", wm_b, ["wm_b"])
            dump("wmT2", wmT, ["wmT"])
        wmT_f = wmT.rearrange("p g i -> p (g i)")
        npiece = (G * 128 + 511) // 512
        for pc in range(npiece):
            n = min(512, G * 128 - pc * 512)
            S.op("pe", lambda e, pc=pc, n=n: e.matmul(PB[4 + pc][:, 0:n], ones_bf, wmT_f[:, pc * 512:pc * 512 + n],
                                                      start=True, stop=True),
                 r=["wmT", "ones"], w=[("rw", pc)])
        for fc in range(NCA):
            g = fc // 2
            pc, o = (g * 128) // 512, (g * 128) % 512
            S.op("dve", lambda e, fc=fc, g=g, pc=pc, o=o: e.scalar_tensor_tensor(
                out=Cb[:, fc, :], in0=PB[4 + pc][:, o:o + 128], scalar=lbc[:, fc:fc + 1], in1=bs_bc[:, g, :],
                op0=ALU.mult, op1=ALU.add), r=[("rw", pc), "lbc", "bs_bc"], w=[("Cb", fc)])
        S.op("dve", lambda e: e.tensor_tensor(out=lprod[:, 0, :], in0=lamv[:, 0, :], in1=lamv[:, 1, :], op=ALU.mult),
             r=["lamv"], w=["lp0"])
        S.op("dve", lambda e: e.tensor_tensor(out=lprod[:, 1, :], in0=lamv[:, 2, :], in1=lamv[:, 3, :], op=ALU.mult),
             r=["lamv"], w=["lp1"])
        S.op("dve", lambda e: e.reduce_sum(out=lsum, in_=lprod, axis=AX.X), r=["lp0", "lp1"], w=["lsum"])
        S.op("act", lambda e: e.activation(out=lexp, in_=lsum, func=AF.Exp), r=["lsum"], w=["lexp"])
        S.op("dve", lambda e: e.tensor_tensor(out=neg_lam, in0=lexp[:, 1:2], in1=lexp[:, 0:1], op=ALU.subtract),
             r=["lexp"], w=["neg_lam"])
        S.op("dve", lambda e, li=lam_init: e.tensor_scalar(out=neg_lam, in0=neg_lam, scalar1=-float(li), scalar2=None, op0=ALU.add),
             r=["neg_lam"], w=["neg_lam"])
        S.op("dve", lambda e, li=lam_init: e.tensor_scalar(out=gsub, in0=gsub, scalar1=float(1.0 - li), scalar2=None, op0=ALU.mult),
             r=["gsub"], w=["gsub"])
        S.fence(skip=("pool",))

        if c.stop < 2:
            continue
        arena.top = PERSIST_TOP
        wslot = [arena.alloc((KC, 512), BF16) for _ in range(2)]
        hT = arena.alloc((KC, 512), BF16)
        gv = arena.alloc((4, DA), F32)
        gv_flat = gv.rearrange("p a b -> p (a b)")
        vln = arena.alloc((4, DA), BF16)
        vln_flat = vln.rearrange("p a b -> p (a b)")
        nxl = 2 if 2 * DM <= 4 * DA else 1
        xl = [gv_flat[:, i * DM:(i + 1) * DM] for i in range(nxl)]
        nxn = 2 if 2 * DM <= 4 * DA else 1
        xn = [vln_flat[:, i * DM:(i + 1) * DM] for i in range(nxn)]

        def gv_keys(lo, hi):
            return [("gv", b) for b in range(lo // DA, (hi - 1) // DA + 1)]

        def vln_keys(lo, hi):
            return [("vln", b) for b in range(lo // DA, (hi - 1) // DA + 1)]

        ssq = arena.alloc(4, F32)
        rstd = arena.alloc(4, F32)
        stats = arena.alloc((4, 6 * ((DA + 511) // 512)), F32)
        mv = arena.alloc((4, 2), F32)
        lrs = arena.alloc(4, F32)
        qkb = [arena.alloc(512, BF16) for _ in range(2)]
        rt = [arena.alloc((4, 64), F32) for _ in range(2)]
        qst = [arena.alloc((4, 512), BF16) for _ in range(2)]
        vst = [arena.alloc(512, BF16) for _ in range(2)]
        gu = [arena.alloc(512, BF16) for _ in range(2)]
        t1 = [arena.alloc(512, F32) for _ in range(2)]
        p1 = arena.alloc((4, 512), BF16)
        sg = [arena.alloc(512, BF16) for _ in range(2)]
        yst = [arena.alloc(512, BF16) for _ in range(2)]
        sgst = [arena.alloc(512, BF16) for _ in range(2)]

        nA, nB = DA // 512, DB // 512
        tiles = []
        for t in range(nA):
            tiles.append(("va", t, OFF_V + t * 512))
        for t in range(nB):
            tiles.append(("q", t, OFF_Q + t * 512))
        for t in range(nB):
            tiles.append(("k", t, OFF_K + t * 512))
        for t in range(nB):
            tiles.append(("vb", t, OFF_VB + t * 512))
        for t in range(nA):
            tiles.append(("u", t, OFF_U + t * 512))
            tiles.append(("ga", t, OFF_GA + t * 512))
        for t in range(nB):
            tiles.append(("gb", t, OFF_GB + t * 512))
        NT = len(tiles)

        cnt = {"acc": 0, "tr": 0, "m": 0, "z": 0}

        for st in range(NST):
            tok0 = st * 512

            def wload(ti):
                kind, t, c0 = tiles[ti]
                sl = ti % 2
                S.op("sp", lambda e, sl=sl, c0=c0: e.dma_start(out=wslot[sl].rearrange("p k n -> p (k n)"), in_=wg_in[l][c0 // 512]),
                     r=[("win%d" % l, "g", c0 // 512)], w=[("w", sl)], dma="w%d" % sl)

            wload(0)
            for j in range(4):
                xb = xl[j % nxl]
                xk = gv_keys((j % nxl) * DM, (j % nxl + 1) * DM)
                nb = xn[j % nxn]
                nk = vln_keys((j % nxn) * DM, (j % nxn + 1) * DM)
                r0 = tok0 + j * 128
                S.op("sp", lambda e, xb=xb, r0=r0: e.dma_start(out=xb, in_=x_cur[r0:r0 + 128, :]),
                     w=xk, dma="x%d" % (j % nxl))
                S.op("act", lambda e, xb=xb, nb=nb, j=j: e.activation(out=nb, in_=xb, func=AF.Square, accum_out=ssq[:, j:j + 1]),
                     r=xk, w=nk + [("ssq", j)])
                S.op("act", lambda e, j=j: e.activation(out=rstd[:, j:j + 1], in_=ssq[:, j:j + 1], func=AF.Sqrt,
                                                        scale=1.0 / DM, bias=eps_t), r=[("ssq", j), "eps"], w=[("rstd", j)])
                S.op("dve", lambda e, j=j: e.reciprocal(out=rstd[:, j:j + 1], in_=rstd[:, j:j + 1]),
                     r=[("rstd", j)], w=[("rstd", j)])
                S.op("dve", lambda e, xb=xb, nb=nb, j=j: e.tensor_scalar_mul(out=nb, in0=xb, scalar1=rstd[:, j:j + 1]),
                     r=xk + [("rstd", j)], w=nk)
                for k0 in range(0, KC, 8):
                    nk8 = min(8, KC - k0)
                    bank = 6 + (cnt["tr"] % 2)
                    cnt["tr"] += 1
                    for kk in range(nk8):
                        S.op("pe", lambda e, nb=nb, bank=bank, kk=kk, k0=k0: e.transpose(
                            PBb[bank][:, kk * 128:(kk + 1) * 128], nb[:, (k0 + kk) * 128:(k0 + kk + 1) * 128], ident),
                             r=nk + ["ident"], w=[("pb", bank, kk)])
                    S.op("dve", lambda e, bank=bank, k0=k0, nk8=nk8, j=j: e.tensor_tensor(
                        out=hT[:, k0:k0 + nk8, j * 128:(j + 1) * 128],
                        in0=PBb[bank][:, 0:nk8 * 128].rearrange("p (a b) -> p a b", a=nk8),
                        in1=gcol[:, k0:k0 + nk8].unsqueeze(2).broadcast_to([128, nk8, 128]), op=ALU.mult),
                         r=[("pb", bank, kk) for kk in range(nk8)] + ["gcol"], w=[("hT", j)])

            hT_all = [("hT", j) for j in range(4)]
            if st == 0 and l == 0:
                dump("hT", hT, hT_all)
                dump("rstd", rstd, [("rstd", j) for j in range(4)])
                dump("gcol", gcol, ["gcol"])
                dump("w0", wslot[0], [("w", 0)])
                dump("wmT", wmT, ["wmT"])
                dump("ident", ident, ["ident"])
                dump("eps", eps_t, ["eps"])
                dump("ones", ones_bf, ["ones"])
                dump("ssq", ssq, [("ssq", j) for j in range(4)])
                dump("x3", xl[3 % nxl], gv_keys(0, 4 * DA))
                dump("xn3", xn[3 % nxn], vln_keys(0, 4 * DA))
                dump("Cb", Cb, [("Cb", fc) for fc in range(NCA)])
            pend = []

            def flush_pend():
                for f in pend:
                    f()
                del pend[:]

            for ti in range(NT):
                kind, t, c0 = tiles[ti]
                sl = ti % 2
                W = wslot[sl]
                if ti + 1 < NT:
                    wload(ti + 1)
                if kind in ("va", "q", "k", "vb"):
                    for j in range(4):
                        bank = cnt["acc"] % 4
                        cnt["acc"] += 1
                        for kc in range(KC):
                            S.op("pe", lambda e, bank=bank, kc=kc, j=j, W=W: e.matmul(
                                PB[bank][:, 0:512], hT[:, kc, j * 128:(j + 1) * 128], W[:, kc, :],
                                start=(kc == 0), stop=(kc == KC - 1)),
                                 r=[("hT", j), ("w", sl)], w=[("pb", bank)])
                        flush_pend()
                        if kind == "va":
                            S.op("act", lambda e, bank=bank, j=j, t=t: e.activation(
                                out=gv[:, j, t * 512:(t + 1) * 512], in_=PB[bank][:, 0:512], func=AF.Gelu),
                                 r=[("pb", bank)], w=[("gv", j)])
                        elif kind == "vb":
                            zi = cnt["z"] % 2
                            cnt["z"] += 1
                            S.op("act", lambda e, bank=bank, zi=zi: e.activation(out=vst[zi], in_=PB[bank][:, 0:512], func=AF.Copy),
                                 r=[("pb", bank)], w=[("vst", zi)])
                            r0 = tok0 + j * 128
                            S.op("sp", lambda e, zi=zi, r0=r0, t=t: e.dma_start(
                                out=v_send[2 * t:2 * t + 2, r0:r0 + 128, :].rearrange("h p e -> p h e"),
                                in_=vst[zi].rearrange("p (h e) -> p h e", h=2)),
                                 r=[("vst", zi)], w=[("v_send", st, j, t)], dma="vst%d" % zi)
                        else:
                            zi = cnt["z"] % 2
                            cnt["z"] += 1
                            z4 = PB[bank][:, 0:512].rearrange("p (a b) -> p a b", a=4)
                            o4 = qkb[zi].rearrange("p (a b) -> p a b", a=4)
                            blk = st * 4 + j
                            cb = cos_t[:, blk, :].unsqueeze(1).broadcast_to([128, 4, 16])
                            sb_ = sin_t[:, blk, :].unsqueeze(1).broadcast_to([128, 4, 16])
                            T = rt[zi]
                            kb, kr, ko = [("pb", bank)], [("rt", zi)], [("qkb", zi)]
                            S.op("dve", lambda e, z4=z4, T=T, cb=cb: e.tensor_tensor(out=T[:, :, 0:16], in0=z4[:, :, 0:16], in1=cb, op=ALU.mult), r=kb, w=kr)
                            S.op("dve", lambda e, z4=z4, T=T, sb_=sb_: e.tensor_tensor(out=T[:, :, 16:32], in0=z4[:, :, 16:32], in1=sb_, op=ALU.mult), r=kb, w=kr)
                            S.op("dve", lambda e, z4=z4, T=T, cb=cb: e.tensor_tensor(out=T[:, :, 32:48], in0=z4[:, :, 16:32], in1=cb, op=ALU.mult), r=kb, w=kr)
                            S.op("dve", lambda e, z4=z4, T=T, sb_=sb_: e.tensor_tensor(out=T[:, :, 48:64], in0=z4[:, :, 0:16], in1=sb_, op=ALU.mult), r=kb, w=kr)
                            S.op("dve", lambda e, o4=o4, T=T: e.tensor_tensor(out=o4[:, :, 0:16], in0=T[:, :, 0:16], in1=T[:, :, 16:32], op=ALU.subtract), r=kr, w=ko)
                            S.op("dve", lambda e, o4=o4, T=T: e.tensor_tensor(out=o4[:, :, 16:32], in0=T[:, :, 32:48], in1=T[:, :, 48:64], op=ALU.add), r=kr, w=ko)
                            S.op("act", lambda e, o4=o4, z4=z4: e.activation(out=o4[:, :, 32:128], in_=z4[:, :, 32:128], func=AF.Copy), r=kb, w=ko)
                            qi = (0 if kind == "q" else 1)
                            dst = qT_send if kind == "q" else kT_send

                            def do_tr(zi=zi, j=j, qi=qi, t=t, dst=dst, last=(j == 3)):
                                bank2 = 6 + (cnt["tr"] % 2)
                                cnt["tr"] += 1
                                for a in range(4):
                                    S.op("pe", lambda e, a=a, zi=zi, bank2=bank2: e.transpose(
                                        PBb[bank2][:, a * 128:(a + 1) * 128], qkb[zi][:, a * 128:(a + 1) * 128], ident),
                                         r=[("qkb", zi), "ident"], w=[("pb", bank2, a)])
                                S.op("act", lambda e, bank2=bank2, qi=qi, j=j: e.activation(
                                    out=qst[qi][:, :, j * 128:(j + 1) * 128],
                                    in_=PBb[bank2][:, 0:512].rearrange("p (a b) -> p a b", a=4), func=AF.Copy),
                                     r=[("pb", bank2, a) for a in range(4)], w=[("qst", qi, j)])
                                if last:
                                    S.op("sp", lambda e, qi=qi, t=t, dst=dst: e.dma_start(
                                        out=dst[t * 512:(t + 1) * 512, tok0:tok0 + 512].rearrange("(a d) n -> d a n", d=128),
                                        in_=qst[qi]), r=[("qst", qi, jj) for jj in range(4)],
                                         w=[("qk_send", qi, st, t)] + [("qst", qi, jj) for jj in range(4)], dma="qst%d" % qi)
                            pend.append(do_tr)
                    if kind == "va" and t == nA - 1:
                        nch = (DA + 511) // 512
                        for j in range(4):
                            for ch in range(nch):
                                n = min(512, DA - ch * 512)
                                S.op("dve", lambda e, j=j, ch=ch, n=n: e.bn_stats(out=stats[:, j, 6 * ch:6 * ch + 6],
                                                                                  in_=gv[:, j, ch * 512:ch * 512 + n]),
                                     r=[("gv", j)], w=[("stats", j, ch)])
                            S.op("dve", lambda e, j=j: e.bn_aggr(out=mv[:, j, :], in_=stats[:, j, :]),
                                 r=[("stats", j, ch) for ch in range(nch)], w=[("mv", j)])
                            S.op("act", lambda e, j=j: e.activation(out=lrs[:, j:j + 1], in_=mv[:, j, 1:2], func=AF.Sqrt, bias=eps_t),
                                 r=[("mv", j), "eps"], w=[("lrs", j)])
                            S.op("dve", lambda e, j=j: e.reciprocal(out=lrs[:, j:j + 1], in_=lrs[:, j:j + 1]), r=[("lrs", j)], w=[("lrs", j)])
                            S.op("dve", lambda e, j=j: e.tensor_scalar(out=vln[:, j, :], in0=gv[:, j, :], scalar1=mv[:, j, 0:1],
                                                                       scalar2=lrs[:, j:j + 1], op0=ALU.subtract, op1=ALU.mult),
                                 r=[("gv", j), ("mv", j), ("lrs", j)], w=[("vln", j)])
                else:
                    for fcl in range(4):
                        fc = t * 4 + fcl
                        if kind == "u":
                            g = fc // 2
                            mb = 4 + (cnt["m"] % 2)
                            cnt["m"] += 1
                            for j in range(4):
                                S.op("pe", lambda e, mb=mb, j=j, fc=fc, g=g: e.matmul(
                                    PB[mb][:, j * 128:(j + 1) * 128], vln[:, j, fc * 128:(fc + 1) * 128], wmT[:, g, :],
                                    start=True, stop=True), r=[("vln", j), "wmT"], w=[("pbm", mb, j)])
                        bank = cnt["acc"] % 4
                        cnt["acc"] += 1
                        for kc in range(KC):
                            S.op("pe", lambda e, bank=bank, kc=kc, fcl=fcl, W=W: e.matmul(
                                PB[bank][:, 0:512], W[:, kc, fcl * 128:(fcl + 1) * 128], hT[:, kc, :],
                                start=(kc == 0), stop=(kc == KC - 1)), r=hT_all + [("w", sl)], w=[("pb", bank)])
                        flush_pend()
                        zi = cnt["z"] % 2
                        cnt["z"] += 1
                        if kind == "u":
                            S.op("act", lambda e, bank=bank, zi=zi: e.activation(out=gu[zi], in_=PB[bank][:, 0:512], func=AF.Gelu),
                                 r=[("pb", bank)], w=[("gu", zi)])
                            S.op("dve", lambda e, mb=mb, zi=zi, fc=fc: e.scalar_tensor_tensor(
                                out=t1[zi].rearrange("p (a b) -> p a b", a=4),
                                in0=PB[mb][:, 0:512].rearrange("p (a b) -> p a b", a=4), scalar=lgc[:, fc:fc + 1],
                                in1=Cb[:, fc, :].unsqueeze(1).broadcast_to([128, 4, 128]), op0=ALU.mult, op1=ALU.add),
                                 r=[("pbm", mb, j) for j in range(4)] + ["lgc", ("Cb", fc)], w=[("t1", zi)])
                            S.op("dve", lambda e, zi=zi, fcl=fcl: e.tensor_tensor(out=p1[:, fcl, :], in0=t1[zi], in1=gu[zi], op=ALU.mult),
                                 r=[("t1", zi), ("gu", zi)], w=[("p1", fcl)])
                        elif kind == "ga":
                            S.op("act", lambda e, bank=bank, zi=zi: e.activation(out=sg[zi], in_=PB[bank][:, 0:512], func=AF.Silu),
                                 r=[("pb", bank)], w=[("sg", zi)])
                            S.op("dve", lambda e, zi=zi, fcl=fcl: e.tensor_tensor(out=yst[zi], in0=p1[:, fcl, :], in1=sg[zi], op=ALU.mult),
                                 r=[("p1", fcl), ("sg", zi)], w=[("yst", zi)])
                            S.op("sp", lambda e, zi=zi, fc=fc: e.dma_start(out=yaT_sp[fc * 128:(fc + 1) * 128, tok0:tok0 + 512], in_=yst[zi]),
                                 r=[("yst", zi)], w=[("yaT_sp", st, fc)], dma="yst%d" % zi)
                        else:
                            S.op("act", lambda e, bank=bank, zi=zi: e.activation(out=sgst[zi], in_=PB[bank][:, 0:512], func=AF.Silu),
                                 r=[("pb", bank)], w=[("sgst", zi)])
                            S.op("sp", lambda e, zi=zi, fc=fc: e.dma_start(out=sgbT_sp[fc * 128:(fc + 1) * 128, tok0:tok0 + 512], in_=sgst[zi]),
                                 r=[("sgst", zi)], w=[("sgbT_sp", st, fc)], dma="sgst%d" % zi)
            flush_pend()
        S.fence()

        if c.stop < 3:
            continue
        for h in range(H):
            collective(GROUP4, kT_send[h * 256:(h + 1) * 256, :], kT_g[h].rearrange("r a t -> (r a) t"), r=[], w=["kT_g"], semkey="cck")
        for h in range(H):
            collective(GROUP4, v_send[h], v_g[h], r=[], w=["v_g"], semkey="ccv")
        for h in range(H):
            collective(GROUP4, qT_send[h * 256:(h + 1) * 256, :], qT_g[h].rearrange("r a t -> (r a) t"), r=[], w=["qT_g"], semkey="ccq")
        S.fence()

        if c.stop < 4:
            continue
        arena.top = PERSIST_TOP
        kT = arena.alloc((HC, S_), BF16)
        Vsb = arena.alloc((NKT, HPC, 257), BF16)
        qT = [arena.alloc((HC, 512), BF16) for _ in range(2)]
        PT = [arena.alloc(512, BF16) for _ in range(4)]
        PTd = [arena.alloc(512, BF16) for _ in range(4)]
        o1 = arena.alloc((4, 256), F32)
        oo = arena.alloc((4, 256), F32)
        junk = arena.alloc(256, BF16)
        rl = arena.alloc(8, F32)
        oss = arena.alloc(4, F32)
        ors = arena.alloc(4, F32)
        onb = arena.alloc((4, 256), BF16)
        onst = [arena.alloc((HC, 512), BF16) for _ in range(2)]

        def hp_of(e):
            if "hp" not in hp_cache:
                hp_cache["hp"] = e.partition_id() % RANKS
            return hp_cache["hp"]

        def private_copy(dst, src_g, key_src, key_dst, sem):
            nd = len(src_g.shape)
            names = " ".join("d%d" % i for i in range(nd))
            dn = " ".join("e%d" % i for i in range(len(dst.shape)))
            S.op("sp", lambda e: e.dma_start(
                out=dst.rearrange("%s -> (%s)" % (dn, dn)).rearrange("(p f) -> p f", f=4096),
                in_=src_g[bass.ds(hp_of(e), 1)].rearrange("%s -> (%s)" % (names, names)).rearrange("(p f) -> p f", f=4096)),
                 r=[key_src], w=[key_dst], dma=sem)

        private_copy(kT_my, kT_g.rearrange("(a b) r c t -> a b r c t", b=HPC), "kT_g", "kT_my", "pck")
        private_copy(v_my, v_g.rearrange("(a b) n e -> a b n e", b=HPC), "v_g", "v_my", "pcv")
        private_copy(qT_my, qT_g.rearrange("(a b) r c t -> a b r c t", b=HPC), "qT_g", "qT_my", "pcq")
        for hl in range(HPC):
            for r in range(RANKS):
                S.op("sp", lambda e, r=r, hl=hl: e.dma_start(
                    out=kT[:, hl * 2:hl * 2 + 2, r * TPC:(r + 1) * TPC],
                    in_=kT_my[hl, r].rearrange("(c d) t -> d c t", d=128)),
                     r=["kT_my"], w=[("kT", r, hl)], dma="ldk%d" % hl)
        for hl in range(HPC):
            S.op("sp", lambda e, hl=hl: e.dma_start(
                out=Vsb[:, :, hl, 0:256], in_=v_my[hl].rearrange("(kt p) e -> p kt e", p=128)),
                 r=["v_my"], w=[("V", hl)], dma="ldv%d" % hl)
        S.op("pool", lambda e: e.memset(Vsb[:, :, :, 256:257], 1.0), w=["Vone"])
        for kb in range(4):
            S.op("pool", lambda e, kb=kb: e.memset(PTd[kb], 0.0), w=[("PTd", kb)])
        if l + 1 < DEPTH:
            gather_layer(l + 1)
        scale = float(128 ** -0.5)
        ptc = 0
        sbank = 0
        deferred = []
        for Gq in range(NQG):
            rq, oq = Gq // NST, (Gq % NST) * 512
            qb_ = qT[Gq % 2]
            for hl in range(HPC):
                S.op("sp", lambda e, qb_=qb_, rq=rq, oq=oq, hl=hl: e.dma_start(
                    out=qb_[:, hl * 2:hl * 2 + 2, :],
                    in_=qT_my[hl, rq, :, oq:oq + 512].rearrange("(c d) t -> d c t", d=128)),
                     r=["qT_my"], w=[("qT", Gq % 2, hl)], dma="ldq%d_%d" % (Gq % 2, hl))
            ntile = 4 * Gq + 4
            ost = onst[Gq % 2]
            for hl in range(HPC):
                for cpt in range(2):
                    hc = hl * 2 + cpt
                    prev = None
                    for t in range(ntile + 1):
                        if t < ntile:
                            kb = t - 4 * Gq
                            c_lo = max(kb, 0) * 128
                            sb = 4 + (sbank % 2)
                            sbank += 1
                            S.op("pe", lambda e, sb=sb, hc=hc, t=t, c_lo=c_lo, qb_=qb_: e.matmul(
                                PB[sb][:, c_lo:512], kT[:, hc, t * 128:(t + 1) * 128], qb_[:, hc, c_lo:512],
                                start=True, stop=True), r=[("kT", rr_, hl) for rr_ in range(RANKS)] + [("qT", Gq % 2, hl)], w=[("pbs", sb)])
                            if kb >= 0:
                                P = PTd[kb]
                                pk = ("PTd", kb)
                                S.op("act", lambda e, P=P, sb=sb, c_lo=c_lo: e.activation(
                                    out=P[:, c_lo + 64:512], in_=PB[sb][:, c_lo + 64:512], func=AF.Exp, scale=scale),
                                     r=[("pbs", sb)], w=[pk])
                                S.op("act", lambda e, P=P, sb=sb, c_lo=c_lo: e.activation(
                                    out=P[0:64, c_lo:c_lo + 64], in_=PB[sb][0:64, c_lo:c_lo + 64], func=AF.Exp, scale=scale),
                                     r=[("pbs", sb)], w=[pk])
                            else:
                                P = PT[ptc % 4]
                                pk = ("PT", ptc % 4)
                                ptc += 1
                                S.op("act", lambda e, P=P, sb=sb: e.activation(out=P, in_=PB[sb][:, 0:512], func=AF.Exp, scale=scale),
                                     r=[("pbs", sb)], w=[pk])
                            cur = (t, max(kb, 0), P, pk)
                        else:
                            cur = None
                        if prev is not None:
                            pt_, qlo, P_, pk_ = prev
                            for qb in range(qlo, 4):
                                last_t = 4 * Gq + qb
                                S.op("pe", lambda e, qb=qb, P_=P_, pt_=pt_, hl=hl, last_t=last_t: e.matmul(
                                    PB[qb][:, 0:257], P_[:, qb * 128:(qb + 1) * 128], Vsb[:, pt_, hl, :],
                                    start=(pt_ == 0), stop=(pt_ == last_t)),
                                     r=[pk_, ("V", hl), "Vone"], w=[("pbo", qb)])
                        prev = cur
                        if t == 1 and deferred:
                            for f in deferred:
                                f()
                            del deferred[:]
                    for qb in range(4):
                        if cpt == 0:
                            S.op("dve", lambda e, qb=qb: e.reciprocal(out=rl[:, qb:qb + 1], in_=PB[qb][:, 256:257]),
                                 r=[("pbo", qb)], w=[("rl", qb)])
                            S.op("dve", lambda e, qb=qb: e.tensor_scalar_mul(out=o1[:, qb, :], in0=PB[qb][:, 0:256], scalar1=rl[:, qb:qb + 1]),
                                 r=[("pbo", qb), ("rl", qb)], w=[("o1", qb)])
                        else:
                            S.op("dve", lambda e, qb=qb: e.reciprocal(out=rl[:, 4 + qb:5 + qb], in_=PB[qb][:, 256:257]),
                                 r=[("pbo", qb)], w=[("rl2", qb)])
                            S.op("dve", lambda e, qb=qb: e.tensor_tensor(out=rl[:, 4 + qb:5 + qb], in0=rl[:, 4 + qb:5 + qb], in1=neg_lam, op=ALU.mult),
                                 r=[("rl2", qb), "neg_lam"], w=[("rl2", qb)])
                            S.op("dve", lambda e, qb=qb: e.scalar_tensor_tensor(
                                out=oo[:, qb, :], in0=PB[qb][:, 0:256], scalar=rl[:, 4 + qb:5 + qb], in1=o1[:, qb, :],
                                op0=ALU.mult, op1=ALU.add), r=[("pbo", qb), ("rl2", qb), ("o1", qb)], w=[("oo", qb)])
                            S.op("act", lambda e, qb=qb: e.activation(out=junk, in_=oo[:, qb, :], func=AF.Square, accum_out=oss[:, qb:qb + 1]),
                                 r=[("oo", qb)], w=["junk", ("oss", qb)])
                            S.op("act", lambda e, qb=qb: e.activation(out=ors[:, qb:qb + 1], in_=oss[:, qb:qb + 1], func=AF.Ln,
                                                                      scale=1.0 / 256, bias=eps_t), r=[("oss", qb), "eps"], w=[("ors", qb)])
                            S.op("act", lambda e, qb=qb: e.activation(out=ors[:, qb:qb + 1], in_=ors[:, qb:qb + 1], func=AF.Exp, scale=-0.5),
                                 r=[("ors", qb)], w=[("ors", qb)])
                            S.op("dve", lambda e, qb=qb: e.scalar_tensor_tensor(
                                out=onb[:, qb, :], in0=oo[:, qb, :], scalar=ors[:, qb:qb + 1], in1=gsub, op0=ALU.mult, op1=ALU.mult),
                                 r=[("oo", qb), ("ors", qb), "gsub"], w=[("onb", qb)])
                    if cpt == 1:
                        def make_deferred(Gq=Gq, hl=hl, ost=ost):
                            def f():
                                for qb in range(4):
                                    for half in range(2):
                                        S.op("pe", lambda e, qb=qb, half=half: e.transpose(
                                            PBb[6][:, (qb * 2 + half) * 128:(qb * 2 + half + 1) * 128],
                                            onb[:, qb, half * 128:(half + 1) * 128], ident),
                                             r=[("onb", qb), "ident"], w=[("pbt", qb, half)])
                                for half in range(2):
                                    S.op("dve", lambda e, half=half: e.tensor_copy(
                                        out=ost[:, hl * 2 + half, :].rearrange("p (a b) -> p a b", a=4),
                                        in_=PBb[6][:, 0:1024].rearrange("p (a h b) -> p a h b", a=4, h=2)[:, :, half, :]),
                                         r=[("pbt", qb, half) for qb in range(4)], w=[("onst", Gq % 2, hl, half)])
                                if hl == HPC - 1:
                                    okeys = [("onst", Gq % 2, h_, f_) for h_ in range(HPC) for f_ in range(2)]
                                    S.op("sp", lambda e: e.dma_start(
                                        out=onT_send[Gq // NST, :, (Gq % NST) * 512:(Gq % NST + 1) * 512].rearrange("(a d) n -> d a n", d=128), in_=ost),
                                         r=okeys, w=[("onT_send", Gq)] + okeys, dma="onst%d" % (Gq % 2))
                            return f
                        deferred.append(make_deferred())
        for f in deferred:
            f()
        del deferred[:]
        S.fence()
        if c.stop < 5:
            continue
        for tq_ in range(RANKS):
            for hl in range(HPC):
                collective(GROUP4, onT_send[tq_, hl * 256:(hl + 1) * 256, :], onT_g[tq_, hl].rearrange("r a t -> (r a) t"),
                           r=[], w=["onT_g"], semkey="ccon")
        S.fence()
        if c.stop < 6:
            continue

        private_copy(on_my, onT_g, "onT_g", "on_my", "pco")
        arena.top = PERSIST_TOP
        wos = [arena.alloc((NCM, 512), BF16) for _ in range(2)]
        yT = arena.alloc((NCM, 512), BF16)
        sgT = arena.alloc((NCB, 512), BF16)
        xres = [arena.alloc(512, F32) for _ in range(4)]
        xo = [arena.alloc(512, F32) for _ in range(2)]
        ND = DM // 512
        xc = 0
        acc = 0
        for st in range(NST):
            tok0 = st * 512
            S.op("sp", lambda e, tok0=tok0: e.dma_start(
                out=yT[:, 0:NCA, :], in_=yaT_sp[:, tok0:tok0 + 512].rearrange("(a d) n -> d a n", d=128)),
                 w=["yaT"], dma="ldya")
            for hl in range(HPC):
                for r in range(RANKS):
                    c0 = NCA + (r * HPC + hl) * 2
                    S.op("sp", lambda e, tok0=tok0, hl=hl, r=r, c0=c0: e.dma_start(
                        out=yT[:, c0:c0 + 2, :],
                        in_=on_my[hl, r, :, tok0:tok0 + 512].rearrange("(c d) t -> d c t", d=128)),
                         r=["on_my"], w=[("onT", c0 - NCA), ("onT", c0 - NCA + 1), ("ybT", c0 - NCA), ("ybT", c0 - NCA + 1)], dma="ldon%d" % (r * HPC + hl))
            S.op("sp", lambda e, tok0=tok0: e.dma_start(
                out=sgT, in_=sgbT_sp[:, tok0:tok0 + 512].rearrange("(a d) n -> d a n", d=128)),
                 w=["sgT"], dma="ldsg")
            for cb in range(NCB):
                eng = "dve"
                S.op(eng, lambda e, cb=cb: e.tensor_tensor(out=yT[:, NCA + cb, :], in0=yT[:, NCA + cb, :], in1=sgT[:, cb, :], op=ALU.mult),
                     r=[("onT", cb), "sgT"], w=[("ybT", cb)])
            ykeys = ["yaT"] + [("ybT", cb) for cb in range(NCB)]
            for dg in range(ND):
                sl = (st * ND + dg) % 2
                S.op("sp", lambda e, sl=sl, dg=dg: e.dma_start(out=wos[sl].rearrange("p k n -> p (k n)"), in_=wg_out[l][dg]),
                     r=[("wout%d" % l, "g", dg)], w=[("wo", sl)], dma="wo%d" % sl)
                for j in range(4):
                    r0 = tok0 + j * 128
                    xi = xc % 4
                    xoi = xc % 2
                    xc += 1
                    S.op("sp", lambda e, xi=xi, r0=r0, dg=dg: e.dma_start(out=xres[xi], in_=x_cur[r0:r0 + 128, dg * 512:(dg + 1) * 512]),
                         w=[("xres", xi)], dma="xres%d" % xi)
                    bank = acc % 4
                    acc += 1
                    for kc in range(NCM):
                        S.op("pe", lambda e, bank=bank, kc=kc, j=j, sl=sl: e.matmul(
                            PB[bank][:, 0:512], yT[:, kc, j * 128:(j + 1) * 128], wos[sl][:, kc, :],
                            start=(kc == 0), stop=(kc == NCM - 1)), r=ykeys + [("wo", sl)], w=[("pb", bank)])
                    S.op("dve", lambda e, bank=bank, xi=xi, xoi=xoi: e.tensor_tensor(out=xo[xoi], in0=PB[bank][:, 0:512], in1=xres[xi], op=ALU.add),
                         r=[("pb", bank), ("xres", xi)], w=[("xo", xoi)])
                    S.op("sp", lambda e, xoi=xoi, r0=r0, dg=dg: e.dma_start(out=x_next[r0:r0 + 128, dg * 512:(dg + 1) * 512], in_=xo[xoi]),
                         r=[("xo", xoi)], w=[("x_next", st, j, dg)], dma="xo%d" % xoi)
        S.fence()

    arena.top = PERSIST_TOP
    fg = arena.alloc(DM, F32)
    NFB = 4
    xf = [arena.alloc(DM, F32) for _ in range(NFB)]
    yf = [arena.alloc(DM, F32) for _ in range(NFB)]
    fss = arena.alloc(NFB, F32)
    frs = arena.alloc(NFB, F32)
    S.op("sp", lambda e: e.dma_start(out=fg, in_=fing_in.partition_broadcast(128)), w=["fg"], dma="ldc0")
    x_last = xs[DEPTH - 1]
    for blk in range(NBLK):
        i = blk % NFB
        r0 = blk * 128
        S.op("sp", lambda e, i=i, r0=r0: e.dma_start(out=xf[i], in_=x_last[r0:r0 + 128, :]), w=[("xf", i)], dma="xf%d" % i)
        S.op("act", lambda e, i=i: e.activation(out=yf[i], in_=xf[i], func=AF.Square, accum_out=fss[:, i:i + 1]),
             r=[("xf", i)], w=[("yf", i), ("fss", i)])
        S.op("act", lambda e, i=i: e.activation(out=frs[:, i:i + 1], in_=fss[:, i:i + 1], func=AF.Sqrt, scale=1.0 / DM, bias=eps_t),
             r=[("fss", i), "eps"], w=[("frs", i)])
        S.op("dve", lambda e, i=i: e.reciprocal(out=frs[:, i:i + 1], in_=frs[:, i:i + 1]), r=[("frs", i)], w=[("frs", i)])
        S.op("dve", lambda e, i=i: e.scalar_tensor_tensor(out=yf[i], in0=xf[i], scalar=frs[:, i:i + 1], in1=fg, op0=ALU.mult, op1=ALU.mult),
             r=[("xf", i), ("frs", i), "fg"], w=[("yf", i)])
        S.op("sp", lambda e, i=i, r0=r0: e.dma_start(out=out_d[r0:r0 + 128, :], in_=yf[i]), r=[("yf", i)], w=[("out", blk)], dma="yf%d" % i)
    if c.debug:
        S.fence()
        for nm, src in (("qT", qT_send), ("kT", kT_send), ("v", v_send), ("on", onT_send), ("ya", yaT_sp),
                        ("sgb", sgbT_sp), ("x0", xs[0]), ("ong", onT_g)):
            if isinstance(c.debug, (list, tuple)) and nm not in c.debug:
                continue
            nd = len(src.shape)
            names = " ".join("d%d" % i for i in range(nd))
            tot = 1
            for v_ in src.shape:
                tot *= v_
            flat = lambda ap_, names=names: ap_.rearrange("%s -> (%s)" % (names, names)).rearrange("(p f) -> p f", p=128)
            dd = nc.dram_tensor("dbg_" + nm, [128, tot // 128], src.dtype, kind="ExternalOutput").ap()
            S.op("pool", lambda e, dd=dd, src=src, flat=flat: e.dma_start(out=dd, in_=flat(src)),
                 w=[("dbg", nm)], dma="dbg_" + nm)
    S.fence(final=True)

    keys = S.finalize()
    with nc.cleanup_on_exit():
        sems = {k: nc.alloc_semaphore(name="s%d" % i) for i, k in enumerate(keys)}
        for k in keys:
            nc.gpsimd.sem_clear(sems[k])
        nc.all_engine_barrier()
        with nc.Block() as block:
            S.emit(block, sems)
        nc.all_engine_barrier()
    for cx in reversed(ctxs):
        cx.__exit__(None, None, None)
    return nc, len(S.ops), len(keys)


def make_in_maps(cfg, x, positions, norm_g, w_in, ln_g, ln_b, sgu_w, sgu_b,
                 lam_q1, lam_k1, lam_q2, lam_k2, subln_g, w_out, final_g):
    c = cfg
    f = np.float32
    x = np.asarray(x, f)
    positions = np.asarray(positions, np.int32)
    DEPTH = c.DEPTH
    invf = (np.float32(ROPE_THETA) ** (-np.arange(0, 32, 2, dtype=np.float32) / np.float32(32))).astype(f)[None, :]
    ngc = np.ascontiguousarray(np.asarray(norm_g, f).reshape(DEPTH, c.KC, 128).transpose(0, 2, 1))
    lgc = np.ascontiguousarray(np.asarray(ln_g, f).reshape(DEPTH, c.NCA, 128).transpose(0, 2, 1))
    lbc = np.ascontiguousarray(np.asarray(ln_b, f).reshape(DEPTH, c.NCA, 128).transpose(0, 2, 1))
    sguw = np.ascontiguousarray(np.asarray(sgu_w, f))
    sgub = np.ascontiguousarray(np.asarray(sgu_b, f).reshape(DEPTH, 1, c.G * 128))
    lam = np.ascontiguousarray(np.concatenate([np.asarray(a, f) for a in (lam_q1, lam_k1, lam_q2, lam_k2)], axis=1)
                               .reshape(DEPTH, 1, 4 * 128))
    subg = np.ascontiguousarray(np.asarray(subln_g, f).reshape(DEPTH, 1, 256))
    fing = np.ascontiguousarray(np.asarray(final_g, f).reshape(1, c.DM))
    w_in = np.asarray(w_in, f)
    w_out = np.asarray(w_out, f)
    maps = []
    ri, ro = c.DM // RANKS, c.DMIX // RANKS
    for core in range(NCORES):
        b, r = core // RANKS, core % RANKS
        t0 = r * c.TPC
        maps.append({
            "x": np.ascontiguousarray(x[b, t0:t0 + c.TPC, :]),
            "pos_t": np.ascontiguousarray(positions[b, t0:t0 + c.TPC].reshape(c.NBLK, 128).T),
            "invf": invf, "norm_gc": ngc, "ln_gc": lgc, "ln_bc": lbc, "sgu_w": sguw, "sgu_b": sgub,
            "lam": lam, "subln_g": subg, "final_g": fing,
            "w_in_sh": np.ascontiguousarray(w_in.reshape(DEPTH, c.KC, RANKS, 32, c.DIN)[:, :, r].reshape(DEPTH, c.KC * 32, c.DIN)),
            "w_out_sh": np.ascontiguousarray(w_out.reshape(DEPTH, c.NCM, RANKS, 32, c.DM)[:, :, r].reshape(DEPTH, c.NCM * 32, c.DM)),
        })
    return maps


_CACHE = {}


def run(cfg, inputs):
    key = (cfg.S, cfg.DM, cfg.G, cfg.H, cfg.DEPTH, cfg.stop)
    if key not in _CACHE:
        _CACHE[key] = build_program(cfg)[0]
    nc = _CACHE[key]
    maps = make_in_maps(cfg, **inputs)
    res = run_bass_kernel_spmd(nc, maps, core_ids=list(range(NCORES)))
    out = np.empty((2, cfg.S, cfg.DM), np.float32)
    for core in range(NCORES):
        b, r = core // RANKS, core % RANKS
        out[b, r * cfg.TPC:(r + 1) * cfg.TPC, :] = res.results[core]["out"]
    if cfg.debug:
        return out, res.results
    return out


def kernel(**inputs):
    return run(Cfg(), inputs)
```

```python
import math
import os
import numpy as np
import concourse.bass as bass
import concourse.mybir as mybir
from concourse.bass_utils import run_bass_kernel_spmd

F32 = mybir.dt.float32
BF16 = mybir.dt.bfloat16
I32 = mybir.dt.int32
AF = mybir.ActivationFunctionType
ALU = mybir.AluOpType
AX = mybir.AxisListType

EPS = 1e-5
ROPE_THETA = 500000.0
NCORES = 8
RANKS = 4


class Cfg:
    def __init__(self, S=8192, DM=4096, G=8, H=8, DEPTH=2, debug=False, stop=99):
        self.S, self.DM, self.G, self.H, self.DEPTH = S, DM, G, H, DEPTH
        self.debug = debug
        self.stop = stop
        self.DA = G * 256
        self.DB = H * 256
        self.DMIX = self.DA + self.DB
        self.DIN = 3 * self.DA + 4 * self.DB
        self.TPC = S // RANKS
        self.NST = self.TPC // 512
        self.NBLK = self.TPC // 128
        self.KC = DM // 128
        self.HPC = H // RANKS
        self.NCA = self.DA // 128
        self.NCB = self.DB // 128
        self.NCM = self.DMIX // 128


import types


def _freeze(fn):
    if fn.__closure__ is None:
        return fn
    cells = []
    for cl in fn.__closure__:
        try:
            cells.append(types.CellType(cl.cell_contents))
        except ValueError:
            cells.append(cl)
    return types.FunctionType(fn.__code__, fn.__globals__, fn.__name__, fn.__defaults__, tuple(cells))


class _Op:
    __slots__ = ("eng", "fn", "deps", "kind", "semkey", "count", "signalled", "inc", "idx")

    def __init__(self, eng, fn, kind, semkey, inc):
        self.eng, self.fn, self.kind, self.semkey, self.inc = eng, fn, kind, semkey, inc
        self.deps = []
        self.count = None
        self.signalled = False


class Sched:
    ENGS = ("pe", "act", "dve", "pool", "sp")

    def __init__(self):
        self.ops = []
        self.last_writer = {}
        self.readers = {}
        self.epoch = 0
        self.last_eng_op = {}
        self.async_since_fence = []
        self.ncc = 0
        self.persist = {}
        self.persist_ops = []

    def op(self, eng, fn, r=(), w=(), dma=None, inc=None, persist=False):
        kind = "d" if dma is not None else "c"
        semkey = ("dma", dma) if dma is not None else ("eng", eng, self.epoch)
        o = _Op(eng, _freeze(fn), kind, semkey, inc if inc is not None else (16 if kind == "d" else 1))
        deps = {}
        for k in r:
            lw = self.last_writer.get(k)
            if lw is None:
                lw = self.persist.get(k)
            if lw is not None:
                deps[id(lw)] = lw
        for k in w:
            lw = self.last_writer.get(k)
            if lw is not None:
                deps[id(lw)] = lw
            for rd in self.readers.get(k, ()):
                deps[id(rd)] = rd
        o.deps = list(deps.values())
        for k in w:
            self.last_writer[k] = o
            self.readers[k] = []
        for k in r:
            self.readers.setdefault(k, []).append(o)
        self.ops.append(o)
        if persist:
            for k in w:
                self.persist[k] = o
            self.persist_ops.append(o)
        elif kind == "d":
            self.async_since_fence.append(o)
        else:
            self.last_eng_op[eng] = o
        return o

    def fence(self, final=False, skip=()):
        tails = [o for e_, o in self.last_eng_op.items() if e_ not in skip] + list(self.async_since_fence)
        if final:
            tails += self.persist_ops
        new_last = {e_: o for e_, o in self.last_eng_op.items() if e_ in skip}
        for e in self.ENGS:
            if e in skip:
                continue
            o = _Op(e, lambda eng: eng.nop(), "c", ("eng", e, self.epoch), 1)
            o.deps = [t for t in tails]
            self.ops.append(o)
            new_last[e] = o
        self.last_eng_op = new_last
        self.async_since_fence = []
        self.last_writer = {}
        self.readers = {}

    def finalize(self):
        for i, o in enumerate(self.ops):
            o.idx = i
        for o in self.ops:
            best = {}
            for d in o.deps:
                b = best.get(d.semkey)
                if b is None or d.idx > b.idx:
                    best[d.semkey] = d
            o.deps = list(best.values())
        for o in self.ops:
            for d in o.deps:
                if d.kind == "d":
                    d.signalled = True
                elif d.eng != o.eng or o.kind == "d" or d.eng != "pe":
                    d.signalled = True
        counts = {}
        for o in self.ops:
            if o.kind == "d":
                o.signalled = True
            if o.signalled:
                c = counts.get(o.semkey, 0) + o.inc
                counts[o.semkey] = c
                o.count = c
        self.counts = counts
        return list(counts.keys())

    def emit(self, block, sems):
        per_eng = {e: [] for e in self.ENGS}
        for o in self.ops:
            per_eng[o.eng].append(o)

        def run(eng_name):
            def body(e):
                waited = {}
                for o in per_eng[eng_name]:
                    need = {}
                    for d in o.deps:
                        if not d.signalled:
                            continue
                        if d.kind == "c" and d.eng == eng_name and eng_name == "pe" and o.kind == "c":
                            continue
                        if need.get(d.semkey, 0) < d.count:
                            need[d.semkey] = d.count
                    for sk, c in need.items():
                        if waited.get(sk, 0) >= c:
                            continue
                        e.wait_ge(sems[sk], c)
                        waited[sk] = c
                    ins = o.fn(e)
                    if o.signalled:
                        ins.then_inc(sems[o.semkey], o.inc)
            return body

        block.tensor(run("pe"))
        block.scalar(run("act"))
        block.vector(run("dve"))
        block.gpsimd(run("pool"))
        block.sync(run("sp"))


class Arena:
    def __init__(self, ap, nbytes):
        self.ap, self.cap, self.top = ap, nbytes, 0

    def alloc(self, shape, dt):
        if isinstance(shape, int):
            shape = (shape,)
        n = 1
        for s in shape:
            n *= s
        nb = n * (2 if dt == BF16 else 4)
        off = self.top
        self.top = off + (nb + 63) // 64 * 64
        assert self.top <= self.cap, ("SBUF arena overflow", self.top, self.cap)
        v = self.ap[:, off // 2:(off + nb) // 2]
        if dt != BF16:
            v = v.bitcast(dt)
        if len(shape) == 2:
            v = v.rearrange("p (a b) -> p a b", a=shape[0], b=shape[1])
        elif len(shape) == 3:
            v = v.rearrange("p (a b c) -> p a b c", a=shape[0], b=shape[1], c=shape[2])
        return v


def lambda_init_fn(layer_idx):
    return 0.8 - 0.6 * math.exp(-0.3 * layer_idx)


def build_program(cfg):
    c = cfg
    S_, DM, G, H, DEPTH = c.S, c.DM, c.G, c.H, c.DEPTH
    DA, DB, DMIX, DIN = c.DA, c.DB, c.DMIX, c.DIN
    TPC, NST, NBLK, KC, HPC = c.TPC, c.NST, c.NBLK, c.KC, c.HPC
    NCA, NCB, NCM = c.NCA, c.NCB, c.NCM
    NKT = S_ // 128
    NQG = S_ // 512
    HC = HPC * 2
    OFF_U, OFF_V, OFF_GA = 0, DA, 2 * DA
    OFF_Q, OFF_K, OFF_VB, OFF_GB = 3 * DA, 3 * DA + DB, 3 * DA + 2 * DB, 3 * DA + 3 * DB

    nc = bass.Bass("TRN2", target_bir_lowering=False)

    def din(name, shape, dt=F32):
        return nc.dram_tensor(name, list(shape), dt, kind="ExternalInput").ap()

    x_in = din("x", [TPC, DM])
    pos_in = din("pos_t", [128, NBLK], I32)
    invf_in = din("invf", [1, 16])
    ngc_in = din("norm_gc", [DEPTH, 128, KC])
    lgc_in = din("ln_gc", [DEPTH, 128, NCA])
    lbc_in = din("ln_bc", [DEPTH, 128, NCA])
    sguw_in = din("sgu_w", [DEPTH, G, 128, 128])
    sgub_in = din("sgu_b", [DEPTH, 1, G * 128])
    lam_in = din("lam", [DEPTH, 1, 4 * 128])
    subg_in = din("subln_g", [DEPTH, 1, 256])
    fing_in = din("final_g", [1, DM])
    win_in = din("w_in_sh", [DEPTH, KC * 32, DIN])
    wout_in = din("w_out_sh", [DEPTH, NCM * 32, DM])
    out_d = nc.dram_tensor("out", [TPC, DM], F32, kind="ExternalOutput").ap()

    def dscr(name, shape, dt=BF16):
        return nc.dram_tensor(name, list(shape), dt).ap()

    wsh_in = [dscr("wsh_in%d" % l, [DIN // 512, 32, KC, 512]) for l in range(DEPTH)]
    wsh_out = [dscr("wsh_out%d" % l, [DM // 512, 32, NCM, 512]) for l in range(DEPTH)]
    wg_in = [dscr("wg_in%d" % l, [DIN // 512, 128, KC * 512]) for l in range(DEPTH)]
    wg_out = [dscr("wg_out%d" % l, [DM // 512, 128, NCM * 512]) for l in range(DEPTH)]
    xs = [dscr("xs%d" % l, [TPC, DM], F32) for l in range(DEPTH)]
    qT_send = dscr("qT_send", [DB, TPC])
    kT_send = dscr("kT_send", [DB, TPC])
    v_send = dscr("v_send", [H, TPC, 256])
    qT_g = dscr("qT_g", [H, RANKS, 256, TPC])
    kT_g = dscr("kT_g", [H, RANKS, 256, TPC])
    v_g = dscr("v_g", [H, RANKS * TPC, 256])
    onT_send = dscr("onT_send", [RANKS, HPC * 256, TPC])
    onT_g = dscr("onT_g", [RANKS, HPC, RANKS, 256, TPC])
    kT_my = dscr("kT_my", [HPC, RANKS, 256, TPC])
    qT_my = dscr("qT_my", [HPC, RANKS, 256, TPC])
    v_my = dscr("v_my", [HPC, RANKS * TPC, 256])
    on_my = dscr("on_my", [HPC, RANKS, 256, TPC])
    yaT_sp = dscr("yaT_sp", [DA, TPC])
    sgbT_sp = dscr("sgbT_sp", [DB, TPC])

    ARENA_BYTES = int(os.environ.get("ARENA_KB", "206")) * 1024
    ctxs = [nc.sbuf_tensor("arena", [128, ARENA_BYTES // 2], BF16)]
    ctxs += [nc.psum_tensor("pb%d" % i, [128, 512], F32) for i in range(8)]
    ents = [cx.__enter__() for cx in ctxs]
    arena = Arena(ents[0], ARENA_BYTES)
    PB = ents[1:9]
    PBb = [p[:, 0:512].bitcast(BF16) for p in PB]

    S = Sched()
    hp_cache = {}
    GROUP4 = [[0, 1, 2, 3], [4, 5, 6, 7]]
    GROUP8 = [list(range(8))]

    def collective(groups, src, dst, r, w, semkey, persist=False):
        S.op("pool", lambda e: e.collective_compute("AllGather", ALU.bypass, replica_groups=groups,
                                                    ins=[src.opt()], outs=[dst.opt()]),
             r=r, w=w, dma=semkey, inc=1, persist=persist)

    def dump(name, ap, rkeys):
        if not (isinstance(c.debug, (list, tuple)) and ("dump_" + name) in c.debug):
            return
        shp = list(ap.shape)
        n = 1
        for v_ in shp[1:]:
            n *= v_
        dd = nc.dram_tensor("dump_" + name, [shp[0], n], ap.dtype, kind="ExternalOutput").ap()
        src = ap
        if len(shp) == 3:
            dd = dd.rearrange("p (a b) -> p a b", a=shp[1])
        elif len(shp) == 4:
            dd = dd.rearrange("p (a b c) -> p a b c", a=shp[1], b=shp[2])
        S.op("sp", lambda e: e.dma_start(out=dd, in_=src), r=rkeys, w=[("dump", name)], dma="dump_" + name)

    ident = arena.alloc(128, BF16)
    ones_bf = arena.alloc(128, BF16)
    eps_t = arena.alloc(1, F32)
    cos_t = arena.alloc((NBLK, 16), F32)
    sin_t = arena.alloc((NBLK, 16), F32)
    gcol = arena.alloc(KC, F32)
    lgc = arena.alloc(NCA, F32)
    lbc = arena.alloc(NCA, F32)
    wmT = arena.alloc((G, 128), BF16)
    Cb = arena.alloc((NCA, 128), F32)
    gsub = arena.alloc(256, F32)
    neg_lam = arena.alloc(1, F32)
    PERSIST_TOP = arena.top

    def cast_gather(src3, l, sh, g, nchunk, cols, tag):
        src = src3[l]
        ntile = cols // 512
        ckeys = []
        for kc in range(nchunk):
            S.op("pool", lambda e, kc=kc: e.dma_start(
                out=sh[:, :, kc, :].rearrange("t i n -> i t n"),
                in_=src[kc * 32:(kc + 1) * 32, :].rearrange("i (t n) -> i t n", n=512)),
                 w=[(tag, "sh", kc)], dma="cast", persist=True)
            ckeys.append((tag, "sh", kc))
        for t in range(ntile):
            collective(GROUP4, sh[t].rearrange("i k n -> i (k n)"), g[t],
                       r=(ckeys if t == 0 else []) + ([(tag, "thr", t - 8)] if t >= 8 else []),
                       w=[(tag, "g", t), (tag, "thr", t)], semkey="cc_" + tag, persist=True)

    S.op("pool", lambda e: e.memset(ident, 0.0), w=["ident0"], persist=True)
    S.op("pool", lambda e: e.affine_select(out=ident, in_=ident, compare_op=ALU.not_equal, fill=1.0,
                                           base=0, pattern=[[-1, 128]], channel_multiplier=1),
         r=["ident0"], w=["ident"], persist=True)
    S.op("pool", lambda e: e.memset(ones_bf, 1.0), w=["ones"], persist=True)
    S.op("pool", lambda e: e.memset(eps_t, EPS), w=["eps"], persist=True)

    def gather_layer(l):
        cast_gather(win_in, l, wsh_in[l], wg_in[l], KC, DIN, "win%d" % l)
        cast_gather(wout_in, l, wsh_out[l], wg_out[l], NCM, DM, "wout%d" % l)

    gather_layer(0)

    mark = arena.top
    pos_sb = arena.alloc(NBLK, I32)
    posf = arena.alloc(NBLK, F32)
    invf = arena.alloc(16, F32)
    ang = arena.alloc((NBLK, 16), F32)
    tq = arena.alloc((NBLK, 16), F32)
    ki = arena.alloc((NBLK, 16), I32)
    kf = arena.alloc((NBLK, 16), F32)
    rr = arena.alloc((NBLK, 16), F32)
    r2 = arena.alloc((NBLK, 16), F32)
    mk = arena.alloc((NBLK, 16), F32)
    S.op("sp", lambda e: e.dma_start(out=pos_sb, in_=pos_in[:, :]), w=["pos"], dma="ldc0")
    S.op("sp", lambda e: e.dma_start(out=invf, in_=invf_in.partition_broadcast(128)), w=["invf"], dma="ldc1")
    S.op("dve", lambda e: e.tensor_copy(out=posf, in_=pos_sb), r=["pos"], w=["posf"])
    S.op("dve", lambda e: e.tensor_tensor(out=ang, in0=posf.unsqueeze(2).broadcast_to([128, NBLK, 16]),
                                          in1=invf.unsqueeze(1).broadcast_to([128, NBLK, 16]), op=ALU.mult),
         r=["posf", "invf"], w=["ang"])
    TWO_PI = float(2 * np.pi)
    C1 = 6.28125
    C2 = float(2 * np.pi - 6.28125)
    S.op("dve", lambda e: e.tensor_scalar(out=tq, in0=ang, scalar1=float(1 / (2 * np.pi)), scalar2=None, op0=ALU.mult),
         r=["ang"], w=["tq"])
    S.op("dve", lambda e: e.tensor_copy(out=ki, in_=tq), r=["tq"], w=["ki"])
    S.op("dve", lambda e: e.tensor_copy(out=kf, in_=ki), r=["ki"], w=["kf"])
    S.op("dve", lambda e: e.scalar_tensor_tensor(out=rr, in0=kf, scalar=-C1, in1=ang, op0=ALU.mult, op1=ALU.add),
         r=["kf", "ang"], w=["rr"])
    S.op("dve", lambda e: e.scalar_tensor_tensor(out=rr, in0=kf, scalar=-C2, in1=rr, op0=ALU.mult, op1=ALU.add),
         r=["kf", "rr"], w=["rr"])

    def wrap(buf, key):
        S.op("dve", lambda e: e.tensor_single_scalar(out=mk, in_=buf, scalar=float(np.pi), op=ALU.is_gt), r=[key], w=["mk"])
        S.op("dve", lambda e: e.scalar_tensor_tensor(out=buf, in0=mk, scalar=-TWO_PI, in1=buf, op0=ALU.mult, op1=ALU.add),
             r=["mk", key], w=[key])
        S.op("dve", lambda e: e.tensor_single_scalar(out=mk, in_=buf, scalar=-float(np.pi), op=ALU.is_lt), r=[key], w=["mk"])
        S.op("dve", lambda e: e.scalar_tensor_tensor(out=buf, in0=mk, scalar=TWO_PI, in1=buf, op0=ALU.mult, op1=ALU.add),
             r=["mk", key], w=[key])

    wrap(rr, "rr")
    S.op("dve", lambda e: e.tensor_scalar(out=r2, in0=rr, scalar1=float(np.pi / 2), scalar2=None, op0=ALU.add), r=["rr"], w=["r2"])
    wrap(r2, "r2")
    S.op("act", lambda e: e.activation(out=sin_t, in_=rr, func=AF.Sin), r=["rr"], w=["sin"])
    S.op("act", lambda e: e.activation(out=cos_t, in_=r2, func=AF.Sin), r=["r2"], w=["cos"])
    S.persist["sin"] = S.last_writer["sin"]
    S.persist["cos"] = S.last_writer["cos"]
    dump("ident0", ident, ["ident"])
    dump("eps0", eps_t, ["eps"])
    dump("ones0", ones_bf, ["ones"])
    dump("cos", cos_t, ["cos"])
    dump("sin", sin_t, ["sin"])
    S.fence(skip=("pool",))
    arena.top = mark

    for l in range(DEPTH if c.stop >= 1 else 0):
        x_cur = x_in if l == 0 else xs[l - 1]
        x_next = xs[l]
        S.epoch = l + 1
        lam_init = lambda_init_fn(l)

        arena.top = PERSIST_TOP
        w_sb = arena.alloc((G, 128), F32)
        wm_b = arena.alloc((G, 128), BF16)
        bs_bc = arena.alloc((G, 128), F32)
        lamv = arena.alloc((4, 128), F32)
        lprod = arena.alloc((2, 128), F32)
        lsum = arena.alloc(2, F32)
        lexp = arena.alloc(2, F32)
        S.op("sp", lambda e, l=l: e.dma_start(out=gcol, in_=ngc_in[l]), w=["gcol"], dma="ldc0")
        S.op("sp", lambda e, l=l: e.dma_start(out=lgc, in_=lgc_in[l]), w=["lgc"], dma="ldc1")
        S.op("sp", lambda e, l=l: e.dma_start(out=lbc, in_=lbc_in[l]), w=["lbc"], dma="ldc2")
        S.op("sp", lambda e, l=l: e.dma_start(out=w_sb, in_=sguw_in[l].rearrange("g i j -> i g j")), w=["w_sb"], dma="ldc3")
        S.op("sp", lambda e, l=l: e.dma_start(out=bs_bc.rearrange("p g i -> p (g i)"), in_=sgub_in[l].partition_broadcast(128)),
             w=["bs_bc"], dma="ldc4")
        S.op("sp", lambda e, l=l: e.dma_start(out=lamv.rearrange("p a b -> p (a b)"), in_=lam_in[l].partition_broadcast(128)),
             w=["lamv"], dma="ldc5")
        S.op("sp", lambda e, l=l: e.dma_start(out=gsub, in_=subg_in[l].partition_broadcast(128)), w=["gsub"], dma="ldc6")
        S.op("dve", lambda e: e.memset(w_sb[0:64, :, 64:128], 0.0), r=["w_sb"], w=["w_sb"])
        S.op("dve", lambda e: e.tensor_copy(out=wm_b, in_=w_sb), r=["w_sb"], w=["wm_b"])
        for g in range(G):
            S.op("pe", lambda e, g=g: e.transpose(PBb[6][:, g * 128:(g + 1) * 128], wm_b[:, g, :], ident),
                 r=["wm_b", "ident"], w=[("pb6", g)])
        S.op("dve", lambda e: e.tensor_copy(out=wmT.rearrange("p g i -> p (g i)"), in_=PBb[6][:, 0:G * 128]),
             r=[("pb6", g) for g in range(G)], w=["wmT"])
        if l == 0:
            dump("w_sb", w_sb, ["w_sb"])
            dump("wm_b", wm_b, ["wm_b"])
            dump("wmT2", wmT, ["wmT"])
        wmT_f = wmT.rearrange("p g i -> p (g i)")
        npiece = (G * 128 + 511) // 512
        for pc in range(npiece):
            n = min(512, G * 128 - pc * 512)
            S.op("pe", lambda e, pc=pc, n=n: e.matmul(PB[4 + pc][:, 0:n], ones_bf, wmT_f[:, pc * 512:pc * 512 + n],
                                                      start=True, stop=True),
                 r=["wmT", "ones"], w=[("rw", pc)])
        for fc in range(NCA):
            g = fc // 2
            pc, o = (g * 128) // 512, (g * 128) % 512
            S.op("dve", lambda e, fc=fc, g=g, pc=pc, o=o: e.scalar_tensor_tensor(
                out=Cb[:, fc, :], in0=PB[4 + pc][:, o:o + 128], scalar=lbc[:, fc:fc + 1], in1=bs_bc[:, g, :],
                op0=ALU.mult, op1=ALU.add), r=[("rw", pc), "lbc", "bs_bc"], w=[("Cb", fc)])
        S.op("dve", lambda e: e.tensor_tensor(out=lprod[:, 0, :], in0=lamv[:, 0, :], in1=lamv[:, 1, :], op=ALU.mult),
             r=["lamv"], w=["lp0"])
        S.op("dve", lambda e: e.tensor_tensor(out=lprod[:, 1, :], in0=lamv[:, 2, :], in1=lamv[:, 3, :], op=ALU.mult),
             r=["lamv"], w=["lp1"])
        S.op("dve", lambda e: e.reduce_sum(out=lsum, in_=lprod, axis=AX.X), r=["lp0", "lp1"], w=["lsum"])
        S.op("act", lambda e: e.activation(out=lexp, in_=lsum, func=AF.Exp), r=["lsum"], w=["lexp"])
        S.op("dve", lambda e: e.tensor_tensor(out=neg_lam, in0=lexp[:, 1:2], in1=lexp[:, 0:1], op=ALU.subtract),
             r=["lexp"], w=["neg_lam"])
        S.op("dve", lambda e, li=lam_init: e.tensor_scalar(out=neg_lam, in0=neg_lam, scalar1=-float(li), scalar2=None, op0=ALU.add),
             r=["neg_lam"], w=["neg_lam"])
        S.op("dve", lambda e, li=lam_init: e.tensor_scalar(out=gsub, in0=gsub, scalar1=float(1.0 - li), scalar2=None, op0=ALU.mult),
             r=["gsub"], w=["gsub"])
        S.fence(skip=("pool",))

        if c.stop < 2:
            continue
        arena.top = PERSIST_TOP
        wslot = [arena.alloc((KC, 512), BF16) for _ in range(2)]
        hT = arena.alloc((KC, 512), BF16)
        gv = arena.alloc((4, DA), F32)
        gv_flat = gv.rearrange("p a b -> p (a b)")
        vln = arena.alloc((4, DA), BF16)
        vln_flat = vln.rearrange("p a b -> p (a b)")
        nxl = 2 if 2 * DM <= 4 * DA else 1
        xl = [gv_flat[:, i * DM:(i + 1) * DM] for i in range(nxl)]
        nxn = 2 if 2 * DM <= 4 * DA else 1
        xn = [vln_flat[:, i * DM:(i + 1) * DM] for i in range(nxn)]

        def gv_keys(lo, hi):
            return [("gv", b) for b in range(lo // DA, (hi - 1) // DA + 1)]

        def vln_keys(lo, hi):
            return [("vln", b) for b in range(lo // DA, (hi - 1) // DA + 1)]

        ssq = arena.alloc(4, F32)
        rstd = arena.alloc(4, F32)
        stats = arena.alloc((4, 6 * ((DA + 511) // 512)), F32)
        mv = arena.alloc((4, 2), F32)
        lrs = arena.alloc(4, F32)
        qkb = [arena.alloc(512, BF16) for _ in range(2)]
        rt = [arena.alloc((4, 64), F32) for _ in range(2)]
        qst = [arena.alloc((4, 512), BF16) for _ in range(2)]
        vst = [arena.alloc(512, BF16) for _ in range(2)]
        gu = [arena.alloc(512, BF16) for _ in range(2)]
        t1 = [arena.alloc(512, F32) for _ in range(2)]
        p1 = arena.alloc((4, 512), BF16)
        sg = [arena.alloc(512, BF16) for _ in range(2)]
        yst = [arena.alloc(512, BF16) for _ in range(2)]
        sgst = [arena.alloc(512, BF16) for _ in range(2)]

        nA, nB = DA // 512, DB // 512
        tiles = []
        for t in range(nA):
            tiles.append(("va", t, OFF_V + t * 512))
        for t in range(nB):
            tiles.append(("q", t, OFF_Q + t * 512))
        for t in range(nB):
            tiles.append(("k", t, OFF_K + t * 512))
        for t in range(nB):
            tiles.append(("vb", t, OFF_VB + t * 512))
        for t in range(nA):
            tiles.append(("u", t, OFF_U + t * 512))
            tiles.append(("ga", t, OFF_GA + t * 512))
        for t in range(nB):
            tiles.append(("gb", t, OFF_GB + t * 512))
        NT = len(tiles)

        cnt = {"acc": 0, "tr": 0, "m": 0, "z": 0}

        tiles_a = [tl for tl in tiles if tl[0] in ("q", "k", "vb")]
        tiles_b = [tl for tl in tiles if tl[0] not in ("q", "k", "vb")]
        seq = [(st_, tl) for st_ in range(NST) for tl in tiles_a] + [(st_, tl) for st_ in range(NST) for tl in tiles_b]

        def wload_seq(gidx):
            _, (kind_, t_, c0_) = seq[gidx]
            sl_ = gidx % 2
            S.op("sp", lambda e, sl_=sl_, c0_=c0_: e.dma_start(out=wslot[sl_].rearrange("p k n -> p (k n)"), in_=wg_in[l][c0_ // 512]),
                 r=[("win%d" % l, "g", c0_ // 512)], w=[("w", sl_)], dma="w%d" % sl_)

        wload_seq(0)

        def run_supertile(st, sub_tiles, first_idx):
            tok0 = st * 512
            for j in range(4):
                xb = xl[j % nxl]
                xk = gv_keys((j % nxl) * DM, (j % nxl + 1) * DM)
                nb = xn[j % nxn]
                nk = vln_keys((j % nxn) * DM, (j % nxn + 1) * DM)
                r0 = tok0 + j * 128
                S.op("sp", lambda e, xb=xb, r0=r0: e.dma_start(out=xb, in_=x_cur[r0:r0 + 128, :]),
                     w=xk, dma="x%d" % (j % nxl))
                S.op("act", lambda e, xb=xb, nb=nb, j=j: e.activation(out=nb, in_=xb, func=AF.Square, accum_out=ssq[:, j:j + 1]),
                     r=xk, w=nk + [("ssq", j)])
                S.op("act", lambda e, j=j: e.activation(out=rstd[:, j:j + 1], in_=ssq[:, j:j + 1], func=AF.Sqrt,
                                                        scale=1.0 / DM, bias=eps_t), r=[("ssq", j), "eps"], w=[("rstd", j)])
                S.op("dve", lambda e, j=j: e.reciprocal(out=rstd[:, j:j + 1], in_=rstd[:, j:j + 1]),
                     r=[("rstd", j)], w=[("rstd", j)])
                S.op("dve", lambda e, xb=xb, nb=nb, j=j: e.tensor_scalar_mul(out=nb, in0=xb, scalar1=rstd[:, j:j + 1]),
                     r=xk + [("rstd", j)], w=nk)
                for k0 in range(0, KC, 8):
                    nk8 = min(8, KC - k0)
                    bank = 6 + (cnt["tr"] % 2)
                    cnt["tr"] += 1
                    for kk in range(nk8):
                        S.op("pe", lambda e, nb=nb, bank=bank, kk=kk, k0=k0: e.transpose(
                            PBb[bank][:, kk * 128:(kk + 1) * 128], nb[:, (k0 + kk) * 128:(k0 + kk + 1) * 128], ident),
                             r=nk + ["ident"], w=[("pb", bank, kk)])
                    S.op("dve", lambda e, bank=bank, k0=k0, nk8=nk8, j=j: e.tensor_tensor(
                        out=hT[:, k0:k0 + nk8, j * 128:(j + 1) * 128],
                        in0=PBb[bank][:, 0:nk8 * 128].rearrange("p (a b) -> p a b", a=nk8),
                        in1=gcol[:, k0:k0 + nk8].unsqueeze(2).broadcast_to([128, nk8, 128]), op=ALU.mult),
                         r=[("pb", bank, kk) for kk in range(nk8)] + ["gcol"], w=[("hT", j)])

            hT_all = [("hT", j) for j in range(4)]
            if st == 0 and l == 0 and first_idx == 0:
                dump("hT", hT, hT_all)
                dump("rstd", rstd, [("rstd", j) for j in range(4)])
                dump("gcol", gcol, ["gcol"])
                dump("w0", wslot[0], [("w", 0)])
                dump("wmT", wmT, ["wmT"])
                dump("ident", ident, ["ident"])
                dump("eps", eps_t, ["eps"])
                dump("ones", ones_bf, ["ones"])
                dump("ssq", ssq, [("ssq", j) for j in range(4)])
                dump("x3", xl[3 % nxl], gv_keys(0, 4 * DA))
                dump("xn3", xn[3 % nxn], vln_keys(0, 4 * DA))
                dump("Cb", Cb, [("Cb", fc) for fc in range(NCA)])
            pend = []

            def flush_pend():
                for f in pend:
                    f()
                del pend[:]

            for k_, (kind, t, c0) in enumerate(sub_tiles):
                gidx = first_idx + k_
                sl = gidx % 2
                W = wslot[sl]
                if gidx + 1 < len(seq):
                    wload_seq(gidx + 1)
                if kind in ("va", "q", "k", "vb"):
                    for j in range(4):
                        bank = cnt["acc"] % 4
                        cnt["acc"] += 1
                        for kc in range(KC):
                            S.op("pe", lambda e, bank=bank, kc=kc, j=j, W=W: e.matmul(
                                PB[bank][:, 0:512], hT[:, kc, j * 128:(j + 1) * 128], W[:, kc, :],
                                start=(kc == 0), stop=(kc == KC - 1)),
                                 r=[("hT", j), ("w", sl)], w=[("pb", bank)])
                        flush_pend()
                        if kind == "va":
                            S.op("act", lambda e, bank=bank, j=j, t=t: e.activation(
                                out=gv[:, j, t * 512:(t + 1) * 512], in_=PB[bank][:, 0:512], func=AF.Gelu),
                                 r=[("pb", bank)], w=[("gv", j)])
                        elif kind == "vb":
                            zi = cnt["z"] % 2
                            cnt["z"] += 1
                            S.op("act", lambda e, bank=bank, zi=zi: e.activation(out=vst[zi], in_=PB[bank][:, 0:512], func=AF.Copy),
                                 r=[("pb", bank)], w=[("vst", zi)])
                            r0 = tok0 + j * 128
                            S.op("sp", lambda e, zi=zi, r0=r0, t=t: e.dma_start(
                                out=v_send[2 * t:2 * t + 2, r0:r0 + 128, :].rearrange("h p e -> p h e"),
                                in_=vst[zi].rearrange("p (h e) -> p h e", h=2)),
                                 r=[("vst", zi)], w=[("v_send", st, j, t)], dma="vst%d" % zi)
                        else:
                            zi = cnt["z"] % 2
                            cnt["z"] += 1
                            z4 = PB[bank][:, 0:512].rearrange("p (a b) -> p a b", a=4)
                            o4 = qkb[zi].rearrange("p (a b) -> p a b", a=4)
                            blk = st * 4 + j
                            cb = cos_t[:, blk, :].unsqueeze(1).broadcast_to([128, 4, 16])
                            sb_ = sin_t[:, blk, :].unsqueeze(1).broadcast_to([128, 4, 16])
                            T = rt[zi]
                            kb, kr, ko = [("pb", bank)], [("rt", zi)], [("qkb", zi)]
                            S.op("dve", lambda e, z4=z4, T=T, cb=cb: e.tensor_tensor(out=T[:, :, 0:16], in0=z4[:, :, 0:16], in1=cb, op=ALU.mult), r=kb, w=kr)
                            S.op("dve", lambda e, z4=z4, T=T, sb_=sb_: e.tensor_tensor(out=T[:, :, 16:32], in0=z4[:, :, 16:32], in1=sb_, op=ALU.mult), r=kb, w=kr)
                            S.op("dve", lambda e, z4=z4, T=T, cb=cb: e.tensor_tensor(out=T[:, :, 32:48], in0=z4[:, :, 16:32], in1=cb, op=ALU.mult), r=kb, w=kr)
                            S.op("dve", lambda e, z4=z4, T=T, sb_=sb_: e.tensor_tensor(out=T[:, :, 48:64], in0=z4[:, :, 0:16], in1=sb_, op=ALU.mult), r=kb, w=kr)
                            S.op("dve", lambda e, o4=o4, T=T: e.tensor_tensor(out=o4[:, :, 0:16], in0=T[:, :, 0:16], in1=T[:, :, 16:32], op=ALU.subtract), r=kr, w=ko)
                            S.op("dve", lambda e, o4=o4, T=T: e.tensor_tensor(out=o4[:, :, 16:32], in0=T[:, :, 32:48], in1=T[:, :, 48:64], op=ALU.add), r=kr, w=ko)
                            S.op("act", lambda e, o4=o4, z4=z4: e.activation(out=o4[:, :, 32:128], in_=z4[:, :, 32:128], func=AF.Copy), r=kb, w=ko)
                            qi = (0 if kind == "q" else 1)
                            dst = qT_send if kind == "q" else kT_send

                            def do_tr(zi=zi, j=j, qi=qi, t=t, dst=dst, last=(j == 3)):
                                bank2 = 6 + (cnt["tr"] % 2)
                                cnt["tr"] += 1
                                for a in range(4):
                                    S.op("pe", lambda e, a=a, zi=zi, bank2=bank2: e.transpose(
                                        PBb[bank2][:, a * 128:(a + 1) * 128], qkb[zi][:, a * 128:(a + 1) * 128], ident),
                                         r=[("qkb", zi), "ident"], w=[("pb", bank2, a)])
                                S.op("act", lambda e, bank2=bank2, qi=qi, j=j: e.activation(
                                    out=qst[qi][:, :, j * 128:(j + 1) * 128],
                                    in_=PBb[bank2][:, 0:512].rearrange("p (a b) -> p a b", a=4), func=AF.Copy),
                                     r=[("pb", bank2, a) for a in range(4)], w=[("qst", qi, j)])
                                if last:
                                    S.op("sp", lambda e, qi=qi, t=t, dst=dst: e.dma_start(
                                        out=dst[t * 512:(t + 1) * 512, tok0:tok0 + 512].rearrange("(a d) n -> d a n", d=128),
                                        in_=qst[qi]), r=[("qst", qi, jj) for jj in range(4)],
                                         w=[("qk_send", qi, st, t)] + [("qst", qi, jj) for jj in range(4)], dma="qst%d" % qi)
                            pend.append(do_tr)
                    if kind == "va" and t == nA - 1:
                        nch = (DA + 511) // 512
                        for j in range(4):
                            for ch in range(nch):
                                n = min(512, DA - ch * 512)
                                S.op("dve", lambda e, j=j, ch=ch, n=n: e.bn_stats(out=stats[:, j, 6 * ch:6 * ch + 6],
                                                                                  in_=gv[:, j, ch * 512:ch * 512 + n]),
                                     r=[("gv", j)], w=[("stats", j, ch)])
                            S.op("dve", lambda e, j=j: e.bn_aggr(out=mv[:, j, :], in_=stats[:, j, :]),
                                 r=[("stats", j, ch) for ch in range(nch)], w=[("mv", j)])
                            S.op("act", lambda e, j=j: e.activation(out=lrs[:, j:j + 1], in_=mv[:, j, 1:2], func=AF.Sqrt, bias=eps_t),
                                 r=[("mv", j), "eps"], w=[("lrs", j)])
                            S.op("dve", lambda e, j=j: e.reciprocal(out=lrs[:, j:j + 1], in_=lrs[:, j:j + 1]), r=[("lrs", j)], w=[("lrs", j)])
                            S.op("dve", lambda e, j=j: e.tensor_scalar(out=vln[:, j, :], in0=gv[:, j, :], scalar1=mv[:, j, 0:1],
                                                                       scalar2=lrs[:, j:j + 1], op0=ALU.subtract, op1=ALU.mult),
                                 r=[("gv", j), ("mv", j), ("lrs", j)], w=[("vln", j)])
                else:
                    for fcl in range(4):
                        fc = t * 4 + fcl
                        if kind == "u":
                            g = fc // 2
                            mb = 4 + (cnt["m"] % 2)
                            cnt["m"] += 1
                            for j in range(4):
                                S.op("pe", lambda e, mb=mb, j=j, fc=fc, g=g: e.matmul(
                                    PB[mb][:, j * 128:(j + 1) * 128], vln[:, j, fc * 128:(fc + 1) * 128], wmT[:, g, :],
                                    start=True, stop=True), r=[("vln", j), "wmT"], w=[("pbm", mb, j)])
                        bank = cnt["acc"] % 4
                        cnt["acc"] += 1
                        for kc in range(KC):
                            S.op("pe", lambda e, bank=bank, kc=kc, fcl=fcl, W=W: e.matmul(
                                PB[bank][:, 0:512], W[:, kc, fcl * 128:(fcl + 1) * 128], hT[:, kc, :],
                                start=(kc == 0), stop=(kc == KC - 1)), r=hT_all + [("w", sl)], w=[("pb", bank)])
                        flush_pend()
                        zi = cnt["z"] % 2
                        cnt["z"] += 1
                        if kind == "u":
                            S.op("act", lambda e, bank=bank, zi=zi: e.activation(out=gu[zi], in_=PB[bank][:, 0:512], func=AF.Gelu),
                                 r=[("pb", bank)], w=[("gu", zi)])
                            S.op("dve", lambda e, mb=mb, zi=zi, fc=fc: e.scalar_tensor_tensor(
                                out=t1[zi].rearrange("p (a b) -> p a b", a=4),
                                in0=PB[mb][:, 0:512].rearrange("p (a b) -> p a b", a=4), scalar=lgc[:, fc:fc + 1],
                                in1=Cb[:, fc, :].unsqueeze(1).broadcast_to([128, 4, 128]), op0=ALU.mult, op1=ALU.add),
                                 r=[("pbm", mb, j) for j in range(4)] + ["lgc", ("Cb", fc)], w=[("t1", zi)])
                            S.op("dve", lambda e, zi=zi, fcl=fcl: e.tensor_tensor(out=p1[:, fcl, :], in0=t1[zi], in1=gu[zi], op=ALU.mult),
                                 r=[("t1", zi), ("gu", zi)], w=[("p1", fcl)])
                        elif kind == "ga":
                            S.op("act", lambda e, bank=bank, zi=zi: e.activation(out=sg[zi], in_=PB[bank][:, 0:512], func=AF.Silu),
                                 r=[("pb", bank)], w=[("sg", zi)])
                            S.op("dve", lambda e, zi=zi, fcl=fcl: e.tensor_tensor(out=yst[zi], in0=p1[:, fcl, :], in1=sg[zi], op=ALU.mult),
                                 r=[("p1", fcl), ("sg", zi)], w=[("yst", zi)])
                            S.op("sp", lambda e, zi=zi, fc=fc: e.dma_start(out=yaT_sp[fc * 128:(fc + 1) * 128, tok0:tok0 + 512], in_=yst[zi]),
                                 r=[("yst", zi)], w=[("yaT_sp", st, fc)], dma="yst%d" % zi)
                        else:
                            S.op("act", lambda e, bank=bank, zi=zi: e.activation(out=sgst[zi], in_=PB[bank][:, 0:512], func=AF.Silu),
                                 r=[("pb", bank)], w=[("sgst", zi)])
                            S.op("sp", lambda e, zi=zi, fc=fc: e.dma_start(out=sgbT_sp[fc * 128:(fc + 1) * 128, tok0:tok0 + 512], in_=sgst[zi]),
                                 r=[("sgst", zi)], w=[("sgbT_sp", st, fc)], dma="sgst%d" % zi)
            flush_pend()

        g0 = 0
        for st_ in range(NST):
            run_supertile(st_, tiles_a, g0)
            g0 += len(tiles_a)
        kkeys = [("qk_send", 1, st_, t_) for st_ in range(NST) for t_ in range(nB)]
        qkeys = [("qk_send", 0, st_, t_) for st_ in range(NST) for t_ in range(nB)]
        vkeys = [("v_send", st_, j_, t_) for st_ in range(NST) for j_ in range(4) for t_ in range(nB)]
        for h in range(H):
            collective(GROUP4, kT_send[h * 256:(h + 1) * 256, :], kT_g[h].rearrange("r a t -> (r a) t"), r=kkeys, w=[("kT_g", h)], semkey="cck")
        for h in range(H):
            collective(GROUP4, v_send[h], v_g[h], r=vkeys, w=[("v_g", h)], semkey="ccv")
        for h in range(H):
            collective(GROUP4, qT_send[h * 256:(h + 1) * 256, :], qT_g[h].rearrange("r a t -> (r a) t"), r=qkeys, w=[("qT_g", h)], semkey="ccq")
        for st_ in range(NST):
            run_supertile(st_, tiles_b, g0)
            g0 += len(tiles_b)
        S.fence()
        if c.stop < 3:
            continue

        if c.stop < 4:
            continue
        arena.top = PERSIST_TOP
        kT = arena.alloc((HC, S_), BF16)
        Vsb = arena.alloc((NKT, HPC, 257), BF16)
        qT = [arena.alloc((HC, 512), BF16) for _ in range(2)]
        PT = [arena.alloc(512, BF16) for _ in range(4)]
        PTd = [arena.alloc(512, BF16) for _ in range(4)]
        o1 = arena.alloc((4, 256), F32)
        oo = arena.alloc((4, 256), F32)
        junk = arena.alloc(256, BF16)
        rl = arena.alloc(8, F32)
        oss = arena.alloc(4, F32)
        ors = arena.alloc(4, F32)
        onb = arena.alloc((4, 256), BF16)
        onst = [arena.alloc((HC, 512), BF16) for _ in range(2)]

        def hp_of(e):
            if "hp" not in hp_cache:
                hp_cache["hp"] = e.partition_id() % RANKS
            return hp_cache["hp"]

        def private_copy(dst, src_g, key_src, key_dst, sem):
            nd = len(src_g.shape)
            names = " ".join("d%d" % i for i in range(nd))
            dn = " ".join("e%d" % i for i in range(len(dst.shape)))
            S.op("sp", lambda e: e.dma_start(
                out=dst.rearrange("%s -> (%s)" % (dn, dn)).rearrange("(p f) -> p f", f=4096),
                in_=src_g[bass.ds(hp_of(e), 1)].rearrange("%s -> (%s)" % (names, names)).rearrange("(p f) -> p f", f=4096)),
                 r=[key_src], w=[key_dst], dma=sem)

        private_copy(kT_my, kT_g.rearrange("(a b) r c t -> a b r c t", b=HPC), "kT_g", "kT_my", "pck")
        private_copy(v_my, v_g.rearrange("(a b) n e -> a b n e", b=HPC), "v_g", "v_my", "pcv")
        private_copy(qT_my, qT_g.rearrange("(a b) r c t -> a b r c t", b=HPC), "qT_g", "qT_my", "pcq")
        for hl in range(HPC):
            for r in range(RANKS):
                S.op("sp", lambda e, r=r, hl=hl: e.dma_start(
                    out=kT[:, hl * 2:hl * 2 + 2, r * TPC:(r + 1) * TPC],
                    in_=kT_my[hl, r].rearrange("(c d) t -> d c t", d=128)),
                     r=["kT_my"], w=[("kT", r, hl)], dma="ldk%d" % hl)
        for hl in range(HPC):
            S.op("sp", lambda e, hl=hl: e.dma_start(
                out=Vsb[:, :, hl, 0:256], in_=v_my[hl].rearrange("(kt p) e -> p kt e", p=128)),
                 r=["v_my"], w=[("V", hl)], dma="ldv%d" % hl)
        S.op("pool", lambda e: e.memset(Vsb[:, :, :, 256:257], 1.0), w=["Vone"])
        for kb in range(4):
            S.op("pool", lambda e, kb=kb: e.memset(PTd[kb], 0.0), w=[("PTd", kb)])
        if l + 1 < DEPTH:
            gather_layer(l + 1)
        scale = float(128 ** -0.5)
        ptc = 0
        sbank = 0
        deferred = []
        for Gq in range(NQG):
            rq, oq = Gq // NST, (Gq % NST) * 512
            qb_ = qT[Gq % 2]
            for hl in range(HPC):
                S.op("sp", lambda e, qb_=qb_, rq=rq, oq=oq, hl=hl: e.dma_start(
                    out=qb_[:, hl * 2:hl * 2 + 2, :],
                    in_=qT_my[hl, rq, :, oq:oq + 512].rearrange("(c d) t -> d c t", d=128)),
                     r=["qT_my"], w=[("qT", Gq % 2, hl)], dma="ldq%d_%d" % (Gq % 2, hl))
            ntile = 4 * Gq + 4
            ost = onst[Gq % 2]
            for hl in range(HPC):
                for cpt in range(2):
                    hc = hl * 2 + cpt
                    prev = None
                    for t in range(ntile + 1):
                        if t < ntile:
                            kb = t - 4 * Gq
                            c_lo = max(kb, 0) * 128
                            sb = 4 + (sbank % 2)
                            sbank += 1
                            S.op("pe", lambda e, sb=sb, hc=hc, t=t, c_lo=c_lo, qb_=qb_: e.matmul(
                                PB[sb][:, c_lo:512], kT[:, hc, t * 128:(t + 1) * 128], qb_[:, hc, c_lo:512],
                                start=True, stop=True), r=[("kT", rr_, hl) for rr_ in range(RANKS)] + [("qT", Gq % 2, hl)], w=[("pbs", sb)])
                            if kb >= 0:
                                P = PTd[kb]
                                pk = ("PTd", kb)
                                S.op("act", lambda e, P=P, sb=sb, c_lo=c_lo: e.activation(
                                    out=P[:, c_lo + 64:512], in_=PB[sb][:, c_lo + 64:512], func=AF.Exp, scale=scale),
                                     r=[("pbs", sb)], w=[pk])
                                S.op("act", lambda e, P=P, sb=sb, c_lo=c_lo: e.activation(
                                    out=P[0:64, c_lo:c_lo + 64], in_=PB[sb][0:64, c_lo:c_lo + 64], func=AF.Exp, scale=scale),
                                     r=[("pbs", sb)], w=[pk])
                            else:
                                P = PT[ptc % 4]
                                pk = ("PT", ptc % 4)
                                ptc += 1
                                S.op("act", lambda e, P=P, sb=sb: e.activation(out=P, in_=PB[sb][:, 0:512], func=AF.Exp, scale=scale),
                                     r=[("pbs", sb)], w=[pk])
                            cur = (t, max(kb, 0), P, pk)
                        else:
                            cur = None
                        if prev is not None:
                            pt_, qlo, P_, pk_ = prev
                            for qb in range(qlo, 4):
                                last_t = 4 * Gq + qb
                                S.op("pe", lambda e, qb=qb, P_=P_, pt_=pt_, hl=hl, last_t=last_t: e.matmul(
                                    PB[qb][:, 0:257], P_[:, qb * 128:(qb + 1) * 128], Vsb[:, pt_, hl, :],
                                    start=(pt_ == 0), stop=(pt_ == last_t)),
                                     r=[pk_, ("V", hl), "Vone"], w=[("pbo", qb)])
                        prev = cur
                        if t == 1 and deferred:
                            for f in deferred:
                                f()
                            del deferred[:]
                    for qb in range(4):
                        if cpt == 0:
                            S.op("dve", lambda e, qb=qb: e.reciprocal(out=rl[:, qb:qb + 1], in_=PB[qb][:, 256:257]),
                                 r=[("pbo", qb)], w=[("rl", qb)])
                            S.op("dve", lambda e, qb=qb: e.tensor_scalar_mul(out=o1[:, qb, :], in0=PB[qb][:, 0:256], scalar1=rl[:, qb:qb + 1]),
                                 r=[("pbo", qb), ("rl", qb)], w=[("o1", qb)])
                        else:
                            S.op("dve", lambda e, qb=qb: e.reciprocal(out=rl[:, 4 + qb:5 + qb], in_=PB[qb][:, 256:257]),
                                 r=[("pbo", qb)], w=[("rl2", qb)])
                            S.op("dve", lambda e, qb=qb: e.tensor_tensor(out=rl[:, 4 + qb:5 + qb], in0=rl[:, 4 + qb:5 + qb], in1=neg_lam, op=ALU.mult),
                                 r=[("rl2", qb), "neg_lam"], w=[("rl2", qb)])
                            S.op("dve", lambda e, qb=qb: e.scalar_tensor_tensor(
                                out=oo[:, qb, :], in0=PB[qb][:, 0:256], scalar=rl[:, 4 + qb:5 + qb], in1=o1[:, qb, :],
                                op0=ALU.mult, op1=ALU.add), r=[("pbo", qb), ("rl2", qb), ("o1", qb)], w=[("oo", qb)])
                            S.op("act", lambda e, qb=qb: e.activation(out=junk, in_=oo[:, qb, :], func=AF.Square, accum_out=oss[:, qb:qb + 1]),
                                 r=[("oo", qb)], w=["junk", ("oss", qb)])
                            S.op("act", lambda e, qb=qb: e.activation(out=ors[:, qb:qb + 1], in_=oss[:, qb:qb + 1], func=AF.Ln,
                                                                      scale=1.0 / 256, bias=eps_t), r=[("oss", qb), "eps"], w=[("ors", qb)])
                            S.op("act", lambda e, qb=qb: e.activation(out=ors[:, qb:qb + 1], in_=ors[:, qb:qb + 1], func=AF.Exp, scale=-0.5),
                                 r=[("ors", qb)], w=[("ors", qb)])
                            S.op("dve", lambda e, qb=qb: e.scalar_tensor_tensor(
                                out=onb[:, qb, :], in0=oo[:, qb, :], scalar=ors[:, qb:qb + 1], in1=gsub, op0=ALU.mult, op1=ALU.mult),
                                 r=[("oo", qb), ("ors", qb), "gsub"], w=[("onb", qb)])
                    if cpt == 1:
                        def make_deferred(Gq=Gq, hl=hl, ost=ost):
                            def f():
                                for qb in range(4):
                                    for half in range(2):
                                        S.op("pe", lambda e, qb=qb, half=half: e.transpose(
                                            PBb[6][:, (qb * 2 + half) * 128:(qb * 2 + half + 1) * 128],
                                            onb[:, qb, half * 128:(half + 1) * 128], ident),
                                             r=[("onb", qb), "ident"], w=[("pbt", qb, half)])
                                for half in range(2):
                                    S.op("dve", lambda e, half=half: e.tensor_copy(
                                        out=ost[:, hl * 2 + half, :].rearrange("p (a b) -> p a b", a=4),
                                        in_=PBb[6][:, 0:1024].rearrange("p (a h b) -> p a h b", a=4, h=2)[:, :, half, :]),
                                         r=[("pbt", qb, half) for qb in range(4)], w=[("onst", Gq % 2, hl, half)])
                                if hl == HPC - 1:
                                    okeys = [("onst", Gq % 2, h_, f_) for h_ in range(HPC) for f_ in range(2)]
                                    S.op("sp", lambda e: e.dma_start(
                                        out=onT_send[Gq // NST, :, (Gq % NST) * 512:(Gq % NST + 1) * 512].rearrange("(a d) n -> d a n", d=128), in_=ost),
                                         r=okeys, w=[("onT_send", Gq)] + okeys, dma="onst%d" % (Gq % 2))
                            return f
                        deferred.append(make_deferred())
        for f in deferred:
            f()
        del deferred[:]
        S.fence()
        if c.stop < 5:
            continue
        for tq_ in range(RANKS):
            for hl in range(HPC):
                collective(GROUP4, onT_send[tq_, hl * 256:(hl + 1) * 256, :], onT_g[tq_, hl].rearrange("r a t -> (r a) t"),
                           r=[], w=["onT_g"], semkey="ccon")
        S.fence()
        if c.stop < 6:
            continue

        private_copy(on_my, onT_g, "onT_g", "on_my", "pco")
        arena.top = PERSIST_TOP
        wos = [arena.alloc((NCM, 512), BF16) for _ in range(2)]
        yT = arena.alloc((NCM, 512), BF16)
        sgT = arena.alloc((NCB, 512), BF16)
        xres = [arena.alloc(512, F32) for _ in range(4)]
        xo = [arena.alloc(512, F32) for _ in range(2)]
        ND = DM // 512
        xc = 0
        acc = 0
        for st in range(NST):
            tok0 = st * 512
            S.op("sp", lambda e, tok0=tok0: e.dma_start(
                out=yT[:, 0:NCA, :], in_=yaT_sp[:, tok0:tok0 + 512].rearrange("(a d) n -> d a n", d=128)),
                 w=["yaT"], dma="ldya")
            for hl in range(HPC):
                for r in range(RANKS):
                    c0 = NCA + (r * HPC + hl) * 2
                    S.op("sp", lambda e, tok0=tok0, hl=hl, r=r, c0=c0: e.dma_start(
                        out=yT[:, c0:c0 + 2, :],
                        in_=on_my[hl, r, :, tok0:tok0 + 512].rearrange("(c d) t -> d c t", d=128)),
                         r=["on_my"], w=[("onT", c0 - NCA), ("onT", c0 - NCA + 1), ("ybT", c0 - NCA), ("ybT", c0 - NCA + 1)], dma="ldon%d" % (r * HPC + hl))
            S.op("sp", lambda e, tok0=tok0: e.dma_start(
                out=sgT, in_=sgbT_sp[:, tok0:tok0 + 512].rearrange("(a d) n -> d a n", d=128)),
                 w=["sgT"], dma="ldsg")
            for cb in range(NCB):
                eng = "dve"
                S.op(eng, lambda e, cb=cb: e.tensor_tensor(out=yT[:, NCA + cb, :], in0=yT[:, NCA + cb, :], in1=sgT[:, cb, :], op=ALU.mult),
                     r=[("onT", cb), "sgT"], w=[("ybT", cb)])
            ykeys = ["yaT"] + [("ybT", cb) for cb in range(NCB)]
            for dg in range(ND):
                sl = (st * ND + dg) % 2
                S.op("sp", lambda e, sl=sl, dg=dg: e.dma_start(out=wos[sl].rearrange("p k n -> p (k n)"), in_=wg_out[l][dg]),
                     r=[("wout%d" % l, "g", dg)], w=[("wo", sl)], dma="wo%d" % sl)
                for j in range(4):
                    r0 = tok0 + j * 128
                    xi = xc % 4
                    xoi = xc % 2
                    xc += 1
                    S.op("sp", lambda e, xi=xi, r0=r0, dg=dg: e.dma_start(out=xres[xi], in_=x_cur[r0:r0 + 128, dg * 512:(dg + 1) * 512]),
                         w=[("xres", xi)], dma="xres%d" % xi)
                    bank = acc % 4
                    acc += 1
                    for kc in range(NCM):
                        S.op("pe", lambda e, bank=bank, kc=kc, j=j, sl=sl: e.matmul(
                            PB[bank][:, 0:512], yT[:, kc, j * 128:(j + 1) * 128], wos[sl][:, kc, :],
                            start=(kc == 0), stop=(kc == NCM - 1)), r=ykeys + [("wo", sl)], w=[("pb", bank)])
                    S.op("dve", lambda e, bank=bank, xi=xi, xoi=xoi: e.tensor_tensor(out=xo[xoi], in0=PB[bank][:, 0:512], in1=xres[xi], op=ALU.add),
                         r=[("pb", bank), ("xres", xi)], w=[("xo", xoi)])
                    S.op("sp", lambda e, xoi=xoi, r0=r0, dg=dg: e.dma_start(out=x_next[r0:r0 + 128, dg * 512:(dg + 1) * 512], in_=xo[xoi]),
                         r=[("xo", xoi)], w=[("x_next", st, j, dg)], dma="xo%d" % xoi)
        S.fence()

    arena.top = PERSIST_TOP
    fg = arena.alloc(DM, F32)
    NFB = 4
    xf = [arena.alloc(DM, F32) for _ in range(NFB)]
    yf = [arena.alloc(DM, F32) for _ in range(NFB)]
    fss = arena.alloc(NFB, F32)
    frs = arena.alloc(NFB, F32)
    S.op("sp", lambda e: e.dma_start(out=fg, in_=fing_in.partition_broadcast(128)), w=["fg"], dma="ldc0")
    x_last = xs[DEPTH - 1]
    for blk in range(NBLK):
        i = blk % NFB
        r0 = blk * 128
        S.op("sp", lambda e, i=i, r0=r0: e.dma_start(out=xf[i], in_=x_last[r0:r0 + 128, :]), w=[("xf", i)], dma="xf%d" % i)
        S.op("act", lambda e, i=i: e.activation(out=yf[i], in_=xf[i], func=AF.Square, accum_out=fss[:, i:i + 1]),
             r=[("xf", i)], w=[("yf", i), ("fss", i)])
        S.op("act", lambda e, i=i: e.activation(out=frs[:, i:i + 1], in_=fss[:, i:i + 1], func=AF.Sqrt, scale=1.0 / DM, bias=eps_t),
             r=[("fss", i), "eps"], w=[("frs", i)])
        S.op("dve", lambda e, i=i: e.reciprocal(out=frs[:, i:i + 1], in_=frs[:, i:i + 1]), r=[("frs", i)], w=[("frs", i)])
        S.op("dve", lambda e, i=i: e.scalar_tensor_tensor(out=yf[i], in0=xf[i], scalar=frs[:, i:i + 1], in1=fg, op0=ALU.mult, op1=ALU.mult),
             r=[("xf", i), ("frs", i), "fg"], w=[("yf", i)])
        S.op("sp", lambda e, i=i, r0=r0: e.dma_start(out=out_d[r0:r0 + 128, :], in_=yf[i]), r=[("yf", i)], w=[("out", blk)], dma="yf%d" % i)
    if c.debug:
        S.fence()
        for nm, src in (("qT", qT_send), ("kT", kT_send), ("v", v_send), ("on", onT_send), ("ya", yaT_sp),
                        ("sgb", sgbT_sp), ("x0", xs[0]), ("ong", onT_g)):
            if isinstance(c.debug, (list, tuple)) and nm not in c.debug:
                continue
            nd = len(src.shape)
            names = " ".join("d%d" % i for i in range(nd))
            tot = 1
            for v_ in src.shape:
                tot *= v_
            flat = lambda ap_, names=names: ap_.rearrange("%s -> (%s)" % (names, names)).rearrange("(p f) -> p f", p=128)
            dd = nc.dram_tensor("dbg_" + nm, [128, tot // 128], src.dtype, kind="ExternalOutput").ap()
            S.op("pool", lambda e, dd=dd, src=src, flat=flat: e.dma_start(out=dd, in_=flat(src)),
                 w=[("dbg", nm)], dma="dbg_" + nm)
    S.fence(final=True)

    keys = S.finalize()
    with nc.cleanup_on_exit():
        sems = {k: nc.alloc_semaphore(name="s%d" % i) for i, k in enumerate(keys)}
        for k in keys:
            nc.gpsimd.sem_clear(sems[k])
        nc.all_engine_barrier()
        with nc.Block() as block:
            S.emit(block, sems)
        nc.all_engine_barrier()
    for cx in reversed(ctxs):
        cx.__exit__(None, None, None)
    return nc, len(S.ops), len(keys)


def make_in_maps(cfg, x, positions, norm_g, w_in, ln_g, ln_b, sgu_w, sgu_b,
                 lam_q1, lam_k1, lam_q2, lam_k2, subln_g, w_out, final_g):
    c = cfg
    f = np.float32
    x = np.asarray(x, f)
    positions = np.asarray(positions, np.int32)
    DEPTH = c.DEPTH
    invf = (np.float32(ROPE_THETA) ** (-np.arange(0, 32, 2, dtype=np.float32) / np.float32(32))).astype(f)[None, :]
    ngc = np.ascontiguousarray(np.asarray(norm_g, f).reshape(DEPTH, c.KC, 128).transpose(0, 2, 1))
    lgc = np.ascontiguousarray(np.asarray(ln_g, f).reshape(DEPTH, c.NCA, 128).transpose(0, 2, 1))
    lbc = np.ascontiguousarray(np.asarray(ln_b, f).reshape(DEPTH, c.NCA, 128).transpose(0, 2, 1))
    sguw = np.ascontiguousarray(np.asarray(sgu_w, f))
    sgub = np.ascontiguousarray(np.asarray(sgu_b, f).reshape(DEPTH, 1, c.G * 128))
    lam = np.ascontiguousarray(np.concatenate([np.asarray(a, f) for a in (lam_q1, lam_k1, lam_q2, lam_k2)], axis=1)
                               .reshape(DEPTH, 1, 4 * 128))
    subg = np.ascontiguousarray(np.asarray(subln_g, f).reshape(DEPTH, 1, 256))
    fing = np.ascontiguousarray(np.asarray(final_g, f).reshape(1, c.DM))
    w_in = np.asarray(w_in, f)
    w_out = np.asarray(w_out, f)
    maps = []
    ri, ro = c.DM // RANKS, c.DMIX // RANKS
    for core in range(NCORES):
        b, r = core // RANKS, core % RANKS
        t0 = r * c.TPC
        maps.append({
            "x": np.ascontiguousarray(x[b, t0:t0 + c.TPC, :]),
            "pos_t": np.ascontiguousarray(positions[b, t0:t0 + c.TPC].reshape(c.NBLK, 128).T),
            "invf": invf, "norm_gc": ngc, "ln_gc": lgc, "ln_bc": lbc, "sgu_w": sguw, "sgu_b": sgub,
            "lam": lam, "subln_g": subg, "final_g": fing,
            "w_in_sh": np.ascontiguousarray(w_in.reshape(DEPTH, c.KC, RANKS, 32, c.DIN)[:, :, r].reshape(DEPTH, c.KC * 32, c.DIN)),
            "w_out_sh": np.ascontiguousarray(w_out.reshape(DEPTH, c.NCM, RANKS, 32, c.DM)[:, :, r].reshape(DEPTH, c.NCM * 32, c.DM)),
        })
    return maps


_CACHE = {}


def run(cfg, inputs):
    key = (cfg.S, cfg.DM, cfg.G, cfg.H, cfg.DEPTH, cfg.stop)
    if key not in _CACHE:
        _CACHE[key] = build_program(cfg)[0]
    nc = _CACHE[key]
    maps = make_in_maps(cfg, **inputs)
    res = run_bass_kernel_spmd(nc, maps, core_ids=list(range(NCORES)))
    out = np.empty((2, cfg.S, cfg.DM), np.float32)
    for core in range(NCORES):
        b, r = core // RANKS, core % RANKS
        out[b, r * cfg.TPC:(r + 1) * cfg.TPC, :] = res.results[core]["out"]
    if cfg.debug:
        return out, res.results
    return out


def kernel(**inputs):
    return run(Cfg(), inputs)
```

```python
import math
import os
import numpy as np
import concourse.bass as bass
import concourse.mybir as mybir
from concourse.bass_utils import run_bass_kernel_spmd

F32 = mybir.dt.float32
BF16 = mybir.dt.bfloat16
I32 = mybir.dt.int32
AF = mybir.ActivationFunctionType
ALU = mybir.AluOpType
AX = mybir.AxisListType

EPS = 1e-5
ROPE_THETA = 500000.0
NCORES = 8
RANKS = 4


class Cfg:
    def __init__(self, S=8192, DM=4096, G=8, H=8, DEPTH=2, debug=False, stop=99):
        self.S, self.DM, self.G, self.H, self.DEPTH = S, DM, G, H, DEPTH
        self.debug = debug
        self.stop = stop
        self.DA = G * 256
        self.DB = H * 256
        self.DMIX = self.DA + self.DB
        self.DIN = 3 * self.DA + 4 * self.DB
        self.TPC = S // RANKS
        self.NST = self.TPC // 512
        self.NBLK = self.TPC // 128
        self.KC = DM // 128
        self.HPC = H // RANKS
        self.NCA = self.DA // 128
        self.NCB = self.DB // 128
        self.NCM = self.DMIX // 128


import types


def _freeze(fn):
    if fn.__closure__ is None:
        return fn
    cells = []
    for cl in fn.__closure__:
        try:
            cells.append(types.CellType(cl.cell_contents))
        except ValueError:
            cells.append(cl)
    return types.FunctionType(fn.__code__, fn.__globals__, fn.__name__, fn.__defaults__, tuple(cells))


class _Op:
    __slots__ = ("eng", "fn", "deps", "kind", "semkey", "count", "signalled", "inc", "idx")

    def __init__(self, eng, fn, kind, semkey, inc):
        self.eng, self.fn, self.kind, self.semkey, self.inc = eng, fn, kind, semkey, inc
        self.deps = []
        self.count = None
        self.signalled = False


class Sched:
    ENGS = ("pe", "act", "dve", "pool", "sp")

    def __init__(self):
        self.ops = []
        self.last_writer = {}
        self.readers = {}
        self.epoch = 0
        self.last_eng_op = {}
        self.async_since_fence = []
        self.ncc = 0
        self.persist = {}
        self.persist_ops = []

    def op(self, eng, fn, r=(), w=(), dma=None, inc=None, persist=False):
        kind = "d" if dma is not None else "c"
        semkey = ("dma", dma) if dma is not None else ("eng", eng, self.epoch)
        o = _Op(eng, _freeze(fn), kind, semkey, inc if inc is not None else (16 if kind == "d" else 1))
        deps = {}
        for k in r:
            lw = self.last_writer.get(k)
            if lw is None:
                lw = self.persist.get(k)
            if lw is not None:
                deps[id(lw)] = lw
        for k in w:
            lw = self.last_writer.get(k)
            if lw is not None:
                deps[id(lw)] = lw
            for rd in self.readers.get(k, ()):
                deps[id(rd)] = rd
        o.deps = list(deps.values())
        for k in w:
            self.last_writer[k] = o
            self.readers[k] = []
        for k in r:
            self.readers.setdefault(k, []).append(o)
        self.ops.append(o)
        if persist:
            for k in w:
                self.persist[k] = o
            self.persist_ops.append(o)
        elif kind == "d":
            self.async_since_fence.append(o)
        else:
            self.last_eng_op[eng] = o
        return o

    def fence(self, final=False, skip=()):
        tails = [o for e_, o in self.last_eng_op.items() if e_ not in skip] + list(self.async_since_fence)
        if final:
            tails += self.persist_ops
        new_last = {e_: o for e_, o in self.last_eng_op.items() if e_ in skip}
        for e in self.ENGS:
            if e in skip:
                continue
            o = _Op(e, lambda eng: eng.nop(), "c", ("eng", e, self.epoch), 1)
            o.deps = [t for t in tails]
            self.ops.append(o)
            new_last[e] = o
        self.last_eng_op = new_last
        self.async_since_fence = []
        self.last_writer = {}
        self.readers = {}

    def finalize(self):
        for i, o in enumerate(self.ops):
            o.idx = i
        for o in self.ops:
            best = {}
            for d in o.deps:
                b = best.get(d.semkey)
                if b is None or d.idx > b.idx:
                    best[d.semkey] = d
            o.deps = list(best.values())
        for o in self.ops:
            for d in o.deps:
                if d.kind == "d":
                    d.signalled = True
                elif d.eng != o.eng or o.kind == "d" or d.eng != "pe":
                    d.signalled = True
        counts = {}
        for o in self.ops:
            if o.kind == "d":
                o.signalled = True
            if o.signalled:
                c = counts.get(o.semkey, 0) + o.inc
                counts[o.semkey] = c
                o.count = c
        self.counts = counts
        return list(counts.keys())

    def emit(self, block, sems):
        per_eng = {e: [] for e in self.ENGS}
        for o in self.ops:
            per_eng[o.eng].append(o)

        def run(eng_name):
            def body(e):
                waited = {}
                for o in per_eng[eng_name]:
                    need = {}
                    for d in o.deps:
                        if not d.signalled:
                            continue
                        if d.kind == "c" and d.eng == eng_name and eng_name == "pe" and o.kind == "c":
                            continue
                        if need.get(d.semkey, 0) < d.count:
                            need[d.semkey] = d.count
                    for sk, c in need.items():
                        if waited.get(sk, 0) >= c:
                            continue
                        e.wait_ge(sems[sk], c)
                        waited[sk] = c
                    ins = o.fn(e)
                    if o.signalled:
                        ins.then_inc(sems[o.semkey], o.inc)
            return body

        block.tensor(run("pe"))
        block.scalar(run("act"))
        block.vector(run("dve"))
        block.gpsimd(run("pool"))
        block.sync(run("sp"))


class Arena:
    def __init__(self, ap, nbytes):
        self.ap, self.cap, self.top = ap, nbytes, 0

    def alloc(self, shape, dt):
        if isinstance(shape, int):
            shape = (shape,)
        n = 1
        for s in shape:
            n *= s
        nb = n * (2 if dt == BF16 else 4)
        off = self.top
        self.top = off + (nb + 63) // 64 * 64
        assert self.top <= self.cap, ("SBUF arena overflow", self.top, self.cap)
        v = self.ap[:, off // 2:(off + nb) // 2]
        if dt != BF16:
            v = v.bitcast(dt)
        if len(shape) == 2:
            v = v.rearrange("p (a b) -> p a b", a=shape[0], b=shape[1])
        elif len(shape) == 3:
            v = v.rearrange("p (a b c) -> p a b c", a=shape[0], b=shape[1], c=shape[2])
        return v


def lambda_init_fn(layer_idx):
    return 0.8 - 0.6 * math.exp(-0.3 * layer_idx)


def build_program(cfg):
    c = cfg
    S_, DM, G, H, DEPTH = c.S, c.DM, c.G, c.H, c.DEPTH
    DA, DB, DMIX, DIN = c.DA, c.DB, c.DMIX, c.DIN
    TPC, NST, NBLK, KC, HPC = c.TPC, c.NST, c.NBLK, c.KC, c.HPC
    NCA, NCB, NCM = c.NCA, c.NCB, c.NCM
    NKT = S_ // 128
    NQG = S_ // 512
    HC = HPC * 2
    OFF_U, OFF_V, OFF_GA = 0, DA, 2 * DA
    OFF_Q, OFF_K, OFF_VB, OFF_GB = 3 * DA, 3 * DA + DB, 3 * DA + 2 * DB, 3 * DA + 3 * DB

    nc = bass.Bass("TRN2", target_bir_lowering=False)

    def din(name, shape, dt=F32):
        return nc.dram_tensor(name, list(shape), dt, kind="ExternalInput").ap()

    x_in = din("x", [TPC, DM])
    pos_in = din("pos_t", [128, NBLK], I32)
    invf_in = din("invf", [1, 16])
    ngc_in = din("norm_gc", [DEPTH, 128, KC])
    lgc_in = din("ln_gc", [DEPTH, 128, NCA])
    lbc_in = din("ln_bc", [DEPTH, 128, NCA])
    sguw_in = din("sgu_w", [DEPTH, G, 128, 128])
    sgub_in = din("sgu_b", [DEPTH, 1, G * 128])
    lam_in = din("lam", [DEPTH, 1, 4 * 128])
    subg_in = din("subln_g", [DEPTH, 1, 256])
    fing_in = din("final_g", [1, DM])
    win_in = din("w_in_sh", [DEPTH, KC * 32, DIN])
    wout_in = din("w_out_sh", [DEPTH, NCM * 32, DM])
    out_d = nc.dram_tensor("out", [TPC, DM], F32, kind="ExternalOutput").ap()

    def dscr(name, shape, dt=BF16):
        return nc.dram_tensor(name, list(shape), dt).ap()

    wsh_in = [dscr("wsh_in%d" % l, [DIN // 512, 32, KC, 512]) for l in range(DEPTH)]
    wsh_out = [dscr("wsh_out%d" % l, [DM // 512, 32, NCM, 512]) for l in range(DEPTH)]
    wg_in = [dscr("wg_in%d" % l, [DIN // 512, 128, KC * 512]) for l in range(DEPTH)]
    wg_out = [dscr("wg_out%d" % l, [DM // 512, 128, NCM * 512]) for l in range(DEPTH)]
    xs = [dscr("xs%d" % l, [TPC, DM], F32) for l in range(DEPTH)]
    qT_send = dscr("qT_send", [DB, TPC])
    kT_send = dscr("kT_send", [DB, TPC])
    v_send = dscr("v_send", [H, TPC, 256])
    qT_g = dscr("qT_g", [H, RANKS, 256, TPC])
    kT_g = dscr("kT_g", [H, RANKS, 256, TPC])
    v_g = dscr("v_g", [H, RANKS * TPC, 256])
    onT_send = dscr("onT_send", [RANKS, HPC * 256, TPC])
    onT_g = dscr("onT_g", [RANKS, HPC, RANKS, 256, TPC])
    kT_my = dscr("kT_my", [HPC, RANKS, 256, TPC])
    qT_my = dscr("qT_my", [HPC, RANKS, 256, TPC])
    v_my = dscr("v_my", [HPC, RANKS * TPC, 256])
    on_my = dscr("on_my", [HPC, RANKS, 256, TPC])
    yaT_sp = dscr("yaT_sp", [DA, TPC])
    sgbT_sp = dscr("sgbT_sp", [DB, TPC])

    ARENA_BYTES = int(os.environ.get("ARENA_KB", "206")) * 1024
    ctxs = [nc.sbuf_tensor("arena", [128, ARENA_BYTES // 2], BF16)]
    ctxs += [nc.psum_tensor("pb%d" % i, [128, 512], F32) for i in range(8)]
    ents = [cx.__enter__() for cx in ctxs]
    arena = Arena(ents[0], ARENA_BYTES)
    PB = ents[1:9]
    PBb = [p[:, 0:512].bitcast(BF16) for p in PB]

    S = Sched()
    hp_cache = {}
    GROUP4 = [[0, 1, 2, 3], [4, 5, 6, 7]]
    GROUP8 = [list(range(8))]

    def collective(groups, src, dst, r, w, semkey, persist=False):
        S.op("pool", lambda e: e.collective_compute("AllGather", ALU.bypass, replica_groups=groups,
                                                    ins=[src.opt()], outs=[dst.opt()]),
             r=r, w=w, dma=semkey, inc=1, persist=persist)

    def dump(name, ap, rkeys):
        if not (isinstance(c.debug, (list, tuple)) and ("dump_" + name) in c.debug):
            return
        shp = list(ap.shape)
        n = 1
        for v_ in shp[1:]:
            n *= v_
        dd = nc.dram_tensor("dump_" + name, [shp[0], n], ap.dtype, kind="ExternalOutput").ap()
        src = ap
        if len(shp) == 3:
            dd = dd.rearrange("p (a b) -> p a b", a=shp[1])
        elif len(shp) == 4:
            dd = dd.rearrange("p (a b c) -> p a b c", a=shp[1], b=shp[2])
        S.op("sp", lambda e: e.dma_start(out=dd, in_=src), r=rkeys, w=[("dump", name)], dma="dump_" + name)

    ident = arena.alloc(128, BF16)
    ones_bf = arena.alloc(128, BF16)
    eps_t = arena.alloc(1, F32)
    cos_t = arena.alloc((NBLK, 16), F32)
    sin_t = arena.alloc((NBLK, 16), F32)
    gcol = arena.alloc(KC, F32)
    lgc = arena.alloc(NCA, F32)
    lbc = arena.alloc(NCA, F32)
    wmT = arena.alloc((G, 128), BF16)
    Cb = arena.alloc((NCA, 128), F32)
    gsub = arena.alloc(256, F32)
    neg_lam = arena.alloc(1, F32)
    PERSIST_TOP = arena.top

    def cast_gather(src3, l, sh, g, nchunk, cols, tag, order=None):
        src = src3[l]
        ntile = cols // 512
        ckeys = []
        for kc in range(nchunk):
            S.op("pool", lambda e, kc=kc: e.dma_start(
                out=sh[:, :, kc, :].rearrange("t i n -> i t n"),
                in_=src[kc * 32:(kc + 1) * 32, :].rearrange("i (t n) -> i t n", n=512)),
                 w=[(tag, "sh", kc)], dma="cast", persist=True)
            ckeys.append((tag, "sh", kc))
        order = list(order) if order is not None else list(range(ntile))
        assert sorted(order) == list(range(ntile))
        for n_, t in enumerate(order):
            collective(GROUP4, sh[t].rearrange("i k n -> i (k n)"), g[t],
                       r=(ckeys if n_ == 0 else []) + ([(tag, "thr", n_ - 8)] if n_ >= 8 else []),
                       w=[(tag, "g", t), (tag, "thr", n_)], semkey="cc_" + tag, persist=True)

    S.op("pool", lambda e: e.memset(ident, 0.0), w=["ident0"], persist=True)
    S.op("pool", lambda e: e.affine_select(out=ident, in_=ident, compare_op=ALU.not_equal, fill=1.0,
                                           base=0, pattern=[[-1, 128]], channel_multiplier=1),
         r=["ident0"], w=["ident"], persist=True)
    S.op("pool", lambda e: e.memset(ones_bf, 1.0), w=["ones"], persist=True)
    S.op("pool", lambda e: e.memset(eps_t, EPS), w=["eps"], persist=True)

    def gather_layer(l):
        first = []
        for off, wdt in ((OFF_Q, DB), (OFF_K, DB), (OFF_VB, DB), (OFF_V, DA)):
            first += list(range(off // 512, (off + wdt) // 512))
        for t_ in range(DA // 512):
            first += [OFF_U // 512 + t_, OFF_GA // 512 + t_]
        first += list(range(OFF_GB // 512, (OFF_GB + DB) // 512))
        cast_gather(win_in, l, wsh_in[l], wg_in[l], KC, DIN, "win%d" % l, order=first)
        cast_gather(wout_in, l, wsh_out[l], wg_out[l], NCM, DM, "wout%d" % l)

    gather_layer(0)

    mark = arena.top
    pos_sb = arena.alloc(NBLK, I32)
    posf = arena.alloc(NBLK, F32)
    invf = arena.alloc(16, F32)
    ang = arena.alloc((NBLK, 16), F32)
    tq = arena.alloc((NBLK, 16), F32)
    ki = arena.alloc((NBLK, 16), I32)
    kf = arena.alloc((NBLK, 16), F32)
    rr = arena.alloc((NBLK, 16), F32)
    r2 = arena.alloc((NBLK, 16), F32)
    mk = arena.alloc((NBLK, 16), F32)
    S.op("sp", lambda e: e.dma_start(out=pos_sb, in_=pos_in[:, :]), w=["pos"], dma="ldc0")
    S.op("sp", lambda e: e.dma_start(out=invf, in_=invf_in.partition_broadcast(128)), w=["invf"], dma="ldc1")
    S.op("dve", lambda e: e.tensor_copy(out=posf, in_=pos_sb), r=["pos"], w=["posf"])
    S.op("dve", lambda e: e.tensor_tensor(out=ang, in0=posf.unsqueeze(2).broadcast_to([128, NBLK, 16]),
                                          in1=invf.unsqueeze(1).broadcast_to([128, NBLK, 16]), op=ALU.mult),
         r=["posf", "invf"], w=["ang"])
    TWO_PI = float(2 * np.pi)
    C1 = 6.28125
    C2 = float(2 * np.pi - 6.28125)
    S.op("dve", lambda e: e.tensor_scalar(out=tq, in0=ang, scalar1=float(1 / (2 * np.pi)), scalar2=None, op0=ALU.mult),
         r=["ang"], w=["tq"])
    S.op("dve", lambda e: e.tensor_copy(out=ki, in_=tq), r=["tq"], w=["ki"])
    S.op("dve", lambda e: e.tensor_copy(out=kf, in_=ki), r=["ki"], w=["kf"])
    S.op("dve", lambda e: e.scalar_tensor_tensor(out=rr, in0=kf, scalar=-C1, in1=ang, op0=ALU.mult, op1=ALU.add),
         r=["kf", "ang"], w=["rr"])
    S.op("dve", lambda e: e.scalar_tensor_tensor(out=rr, in0=kf, scalar=-C2, in1=rr, op0=ALU.mult, op1=ALU.add),
         r=["kf", "rr"], w=["rr"])

    def wrap(buf, key):
        S.op("dve", lambda e: e.tensor_single_scalar(out=mk, in_=buf, scalar=float(np.pi), op=ALU.is_gt), r=[key], w=["mk"])
        S.op("dve", lambda e: e.scalar_tensor_tensor(out=buf, in0=mk, scalar=-TWO_PI, in1=buf, op0=ALU.mult, op1=ALU.add),
             r=["mk", key], w=[key])
        S.op("dve", lambda e: e.tensor_single_scalar(out=mk, in_=buf, scalar=-float(np.pi), op=ALU.is_lt), r=[key], w=["mk"])
        S.op("dve", lambda e: e.scalar_tensor_tensor(out=buf, in0=mk, scalar=TWO_PI, in1=buf, op0=ALU.mult, op1=ALU.add),
             r=["mk", key], w=[key])

    wrap(rr, "rr")
    S.op("dve", lambda e: e.tensor_scalar(out=r2, in0=rr, scalar1=float(np.pi / 2), scalar2=None, op0=ALU.add), r=["rr"], w=["r2"])
    wrap(r2, "r2")
    S.op("act", lambda e: e.activation(out=sin_t, in_=rr, func=AF.Sin), r=["rr"], w=["sin"])
    S.op("act", lambda e: e.activation(out=cos_t, in_=r2, func=AF.Sin), r=["r2"], w=["cos"])
    S.persist["sin"] = S.last_writer["sin"]
    S.persist["cos"] = S.last_writer["cos"]
    dump("ident0", ident, ["ident"])
    dump("eps0", eps_t, ["eps"])
    dump("ones0", ones_bf, ["ones"])
    dump("cos", cos_t, ["cos"])
    dump("sin", sin_t, ["sin"])
    S.fence(skip=("pool",))
    arena.top = mark

    for l in range(DEPTH if c.stop >= 1 else 0):
        x_cur = x_in if l == 0 else xs[l - 1]
        x_next = xs[l]
        S.epoch = l + 1
        lam_init = lambda_init_fn(l)

        arena.top = PERSIST_TOP
        w_sb = arena.alloc((G, 128), F32)
        wm_b = arena.alloc((G, 128), BF16)
        bs_bc = arena.alloc((G, 128), F32)
        lamv = arena.alloc((4, 128), F32)
        lprod = arena.alloc((2, 128), F32)
        lsum = arena.alloc(2, F32)
        lexp = arena.alloc(2, F32)
        S.op("sp", lambda e, l=l: e.dma_start(out=gcol, in_=ngc_in[l]), w=["gcol"], dma="ldc0")
        S.op("sp", lambda e, l=l: e.dma_start(out=lgc, in_=lgc_in[l]), w=["lgc"], dma="ldc1")
        S.op("sp", lambda e, l=l: e.dma_start(out=lbc, in_=lbc_in[l]), w=["lbc"], dma="ldc2")
        S.op("sp", lambda e, l=l: e.dma_start(out=w_sb, in_=sguw_in[l].rearrange("g i j -> i g j")), w=["w_sb"], dma="ldc3")
        S.op("sp", lambda e, l=l: e.dma_start(out=bs_bc.rearrange("p g i -> p (g i)"), in_=sgub_in[l].partition_broadcast(128)),
             w=["bs_bc"], dma="ldc4")
        S.op("sp", lambda e, l=l: e.dma_start(out=lamv.rearrange("p a b -> p (a b)"), in_=lam_in[l].partition_broadcast(128)),
             w=["lamv"], dma="ldc5")
        S.op("sp", lambda e, l=l: e.dma_start(out=gsub, in_=subg_in[l].partition_broadcast(128)), w=["gsub"], dma="ldc6")
        S.op("dve", lambda e: e.memset(w_sb[0:64, :, 64:128], 0.0), r=["w_sb"], w=["w_sb"])
        S.op("dve", lambda e: e.tensor_copy(out=wm_b, in_=w_sb), r=["w_sb"], w=["wm_b"])
        for g in range(G):
            S.op("pe", lambda e, g=g: e.transpose(PBb[6][:, g * 128:(g + 1) * 128], wm_b[:, g, :], ident),
                 r=["wm_b", "ident"], w=[("pb6", g)])
        S.op("dve", lambda e: e.tensor_copy(out=wmT.rearrange("p g i -> p (g i)"), in_=PBb[6][:, 0:G * 128]),
             r=[("pb6", g) for g in range(G)], w=["wmT"])
        if l == 0:
            dump("w_sb", w_sb, ["w_sb"])
            dump("wm_b", wm_b, ["wm_b"])
            dump("wmT2", wmT, ["wmT"])
        wmT_f = wmT.rearrange("p g i -> p (g i)")
        npiece = (G * 128 + 511) // 512
        for pc in range(npiece):
            n = min(512, G * 128 - pc * 512)
            S.op("pe", lambda e, pc=pc, n=n: e.matmul(PB[4 + pc][:, 0:n], ones_bf, wmT_f[:, pc * 512:pc * 512 + n],
                                                      start=True, stop=True),
                 r=["wmT", "ones"], w=[("rw", pc)])
        for fc in range(NCA):
            g = fc // 2
            pc, o = (g * 128) // 512, (g * 128) % 512
            S.op("dve", lambda e, fc=fc, g=g, pc=pc, o=o: e.scalar_tensor_tensor(
                out=Cb[:, fc, :], in0=PB[4 + pc][:, o:o + 128], scalar=lbc[:, fc:fc + 1], in1=bs_bc[:, g, :],
                op0=ALU.mult, op1=ALU.add), r=[("rw", pc), "lbc", "bs_bc"], w=[("Cb", fc)])
        S.op("dve", lambda e: e.tensor_tensor(out=lprod[:, 0, :], in0=lamv[:, 0, :], in1=lamv[:, 1, :], op=ALU.mult),
             r=["lamv"], w=["lp0"])
        S.op("dve", lambda e: e.tensor_tensor(out=lprod[:, 1, :], in0=lamv[:, 2, :], in1=lamv[:, 3, :], op=ALU.mult),
             r=["lamv"], w=["lp1"])
        S.op("dve", lambda e: e.reduce_sum(out=lsum, in_=lprod, axis=AX.X), r=["lp0", "lp1"], w=["lsum"])
        S.op("act", lambda e: e.activation(out=lexp, in_=lsum, func=AF.Exp), r=["lsum"], w=["lexp"])
        S.op("dve", lambda e: e.tensor_tensor(out=neg_lam, in0=lexp[:, 1:2], in1=lexp[:, 0:1], op=ALU.subtract),
             r=["lexp"], w=["neg_lam"])
        S.op("dve", lambda e, li=lam_init: e.tensor_scalar(out=neg_lam, in0=neg_lam, scalar1=-float(li), scalar2=None, op0=ALU.add),
             r=["neg_lam"], w=["neg_lam"])
        S.op("dve", lambda e, li=lam_init: e.tensor_scalar(out=gsub, in0=gsub, scalar1=float(1.0 - li), scalar2=None, op0=ALU.mult),
             r=["gsub"], w=["gsub"])
        S.fence(skip=("pool",))

        if c.stop < 2:
            continue
        arena.top = PERSIST_TOP
        wslot = [arena.alloc((KC, 512), BF16) for _ in range(2)]
        hT = arena.alloc((KC, 512), BF16)
        gv = arena.alloc((4, DA), F32)
        gv_flat = gv.rearrange("p a b -> p (a b)")
        vln = arena.alloc((4, DA), BF16)
        vln_flat = vln.rearrange("p a b -> p (a b)")
        nxl = 2 if 2 * DM <= 4 * DA else 1
        xl = [gv_flat[:, i * DM:(i + 1) * DM] for i in range(nxl)]
        nxn = 2 if 2 * DM <= 4 * DA else 1
        xn = [vln_flat[:, i * DM:(i + 1) * DM] for i in range(nxn)]

        def gv_keys(lo, hi):
            return [("gv", b) for b in range(lo // DA, (hi - 1) // DA + 1)]

        def vln_keys(lo, hi):
            return [("vln", b) for b in range(lo // DA, (hi - 1) // DA + 1)]

        ssq = arena.alloc(4, F32)
        rstd = arena.alloc(4, F32)
        stats = arena.alloc((4, 6 * ((DA + 511) // 512)), F32)
        mv = arena.alloc((4, 2), F32)
        lrs = arena.alloc(4, F32)
        qkb = [arena.alloc(512, BF16) for _ in range(2)]
        rt = [arena.alloc((4, 64), F32) for _ in range(2)]
        qst = [arena.alloc((4, 512), BF16) for _ in range(2)]
        vst = [arena.alloc(512, BF16) for _ in range(2)]
        gu = [arena.alloc(512, BF16) for _ in range(2)]
        t1 = [arena.alloc(512, F32) for _ in range(2)]
        p1 = arena.alloc((4, 512), BF16)
        sg = [arena.alloc(512, BF16) for _ in range(2)]
        yst = [arena.alloc(512, BF16) for _ in range(2)]
        sgst = [arena.alloc(512, BF16) for _ in range(2)]

        nA, nB = DA // 512, DB // 512
        tiles = []
        for t in range(nA):
            tiles.append(("va", t, OFF_V + t * 512))
        for t in range(nB):
            tiles.append(("q", t, OFF_Q + t * 512))
        for t in range(nB):
            tiles.append(("k", t, OFF_K + t * 512))
        for t in range(nB):
            tiles.append(("vb", t, OFF_VB + t * 512))
        for t in range(nA):
            tiles.append(("u", t, OFF_U + t * 512))
            tiles.append(("ga", t, OFF_GA + t * 512))
        for t in range(nB):
            tiles.append(("gb", t, OFF_GB + t * 512))
        NT = len(tiles)

        cnt = {"acc": 0, "tr": 0, "m": 0, "z": 0}

        tiles_a = [tl for tl in tiles if tl[0] in ("q", "k", "vb")]
        tiles_b = [tl for tl in tiles if tl[0] not in ("q", "k", "vb")]
        seq = [(st_, tl) for st_ in range(NST) for tl in tiles_a] + [(st_, tl) for st_ in range(NST) for tl in tiles_b]

        def wload_seq(gidx):
            _, (kind_, t_, c0_) = seq[gidx]
            sl_ = gidx % 2
            S.op("sp", lambda e, sl_=sl_, c0_=c0_: e.dma_start(out=wslot[sl_].rearrange("p k n -> p (k n)"), in_=wg_in[l][c0_ // 512]),
                 r=[("win%d" % l, "g", c0_ // 512)], w=[("w", sl_)], dma="w%d" % sl_)

        wload_seq(0)

        def run_supertile(st, sub_tiles, first_idx):
            tok0 = st * 512
            for j in range(4):
                xb = xl[j % nxl]
                xk = gv_keys((j % nxl) * DM, (j % nxl + 1) * DM)
                nb = xn[j % nxn]
                nk = vln_keys((j % nxn) * DM, (j % nxn + 1) * DM)
                r0 = tok0 + j * 128
                S.op("sp", lambda e, xb=xb, r0=r0: e.dma_start(out=xb, in_=x_cur[r0:r0 + 128, :]),
                     w=xk, dma="x%d" % (j % nxl))
                S.op("act", lambda e, xb=xb, nb=nb, j=j: e.activation(out=nb, in_=xb, func=AF.Square, accum_out=ssq[:, j:j + 1]),
                     r=xk, w=nk + [("ssq", j)])
                S.op("act", lambda e, j=j: e.activation(out=rstd[:, j:j + 1], in_=ssq[:, j:j + 1], func=AF.Sqrt,
                                                        scale=1.0 / DM, bias=eps_t), r=[("ssq", j), "eps"], w=[("rstd", j)])
                S.op("dve", lambda e, j=j: e.reciprocal(out=rstd[:, j:j + 1], in_=rstd[:, j:j + 1]),
                     r=[("rstd", j)], w=[("rstd", j)])
                S.op("dve", lambda e, xb=xb, nb=nb, j=j: e.tensor_scalar_mul(out=nb, in0=xb, scalar1=rstd[:, j:j + 1]),
                     r=xk + [("rstd", j)], w=nk)
                for k0 in range(0, KC, 8):
                    nk8 = min(8, KC - k0)
                    bank = 6 + (cnt["tr"] % 2)
                    cnt["tr"] += 1
                    for kk in range(nk8):
                        S.op("pe", lambda e, nb=nb, bank=bank, kk=kk, k0=k0: e.transpose(
                            PBb[bank][:, kk * 128:(kk + 1) * 128], nb[:, (k0 + kk) * 128:(k0 + kk + 1) * 128], ident),
                             r=nk + ["ident"], w=[("pb", bank, kk)])
                    S.op("dve", lambda e, bank=bank, k0=k0, nk8=nk8, j=j: e.tensor_tensor(
                        out=hT[:, k0:k0 + nk8, j * 128:(j + 1) * 128],
                        in0=PBb[bank][:, 0:nk8 * 128].rearrange("p (a b) -> p a b", a=nk8),
                        in1=gcol[:, k0:k0 + nk8].unsqueeze(2).broadcast_to([128, nk8, 128]), op=ALU.mult),
                         r=[("pb", bank, kk) for kk in range(nk8)] + ["gcol"], w=[("hT", j)])

            hT_all = [("hT", j) for j in range(4)]
            if st == 0 and l == 0 and first_idx == 0:
                dump("hT", hT, hT_all)
                dump("rstd", rstd, [("rstd", j) for j in range(4)])
                dump("gcol", gcol, ["gcol"])
                dump("w0", wslot[0], [("w", 0)])
                dump("wmT", wmT, ["wmT"])
                dump("ident", ident, ["ident"])
                dump("eps", eps_t, ["eps"])
                dump("ones", ones_bf, ["ones"])
                dump("ssq", ssq, [("ssq", j) for j in range(4)])
                dump("x3", xl[3 % nxl], gv_keys(0, 4 * DA))
                dump("xn3", xn[3 % nxn], vln_keys(0, 4 * DA))
                dump("Cb", Cb, [("Cb", fc) for fc in range(NCA)])
            pend = []

            def flush_pend():
                for f in pend:
                    f()
                del pend[:]

            for k_, (kind, t, c0) in enumerate(sub_tiles):
                gidx = first_idx + k_
                sl = gidx % 2
                W = wslot[sl]
                if gidx + 1 < len(seq):
                    wload_seq(gidx + 1)
                if kind in ("va", "q", "k", "vb"):
                    for j in range(4):
                        bank = cnt["acc"] % 4
                        cnt["acc"] += 1
                        for kc in range(KC):
                            S.op("pe", lambda e, bank=bank, kc=kc, j=j, W=W: e.matmul(
                                PB[bank][:, 0:512], hT[:, kc, j * 128:(j + 1) * 128], W[:, kc, :],
                                start=(kc == 0), stop=(kc == KC - 1)),
                                 r=[("hT", j), ("w", sl)], w=[("pb", bank)])
                        flush_pend()
                        if kind == "va":
                            S.op("act", lambda e, bank=bank, j=j, t=t: e.activation(
                                out=gv[:, j, t * 512:(t + 1) * 512], in_=PB[bank][:, 0:512], func=AF.Gelu),
                                 r=[("pb", bank)], w=[("gv", j)])
                        elif kind == "vb":
                            zi = cnt["z"] % 2
                            cnt["z"] += 1
                            S.op("act", lambda e, bank=bank, zi=zi: e.activation(out=vst[zi], in_=PB[bank][:, 0:512], func=AF.Copy),
                                 r=[("pb", bank)], w=[("vst", zi)])
                            r0 = tok0 + j * 128
                            S.op("sp", lambda e, zi=zi, r0=r0, t=t: e.dma_start(
                                out=v_send[2 * t:2 * t + 2, r0:r0 + 128, :].rearrange("h p e -> p h e"),
                                in_=vst[zi].rearrange("p (h e) -> p h e", h=2)),
                                 r=[("vst", zi)], w=[("v_send", st, j, t)], dma="vst%d" % zi)
                        else:
                            zi = cnt["z"] % 2
                            cnt["z"] += 1
                            z4 = PB[bank][:, 0:512].rearrange("p (a b) -> p a b", a=4)
                            o4 = qkb[zi].rearrange("p (a b) -> p a b", a=4)
                            blk = st * 4 + j
                            cb = cos_t[:, blk, :].unsqueeze(1).broadcast_to([128, 4, 16])
                            sb_ = sin_t[:, blk, :].unsqueeze(1).broadcast_to([128, 4, 16])
                            T = rt[zi]
                            kb, kr, ko = [("pb", bank)], [("rt", zi)], [("qkb", zi)]
                            S.op("dve", lambda e, z4=z4, T=T, cb=cb: e.tensor_tensor(out=T[:, :, 0:16], in0=z4[:, :, 0:16], in1=cb, op=ALU.mult), r=kb, w=kr)
                            S.op("dve", lambda e, z4=z4, T=T, sb_=sb_: e.tensor_tensor(out=T[:, :, 16:32], in0=z4[:, :, 16:32], in1=sb_, op=ALU.mult), r=kb, w=kr)
                            S.op("dve", lambda e, z4=z4, T=T, cb=cb: e.tensor_tensor(out=T[:, :, 32:48], in0=z4[:, :, 16:32], in1=cb, op=ALU.mult), r=kb, w=kr)
                            S.op("dve", lambda e, z4=z4, T=T, sb_=sb_: e.tensor_tensor(out=T[:, :, 48:64], in0=z4[:, :, 0:16], in1=sb_, op=ALU.mult), r=kb, w=kr)
                            S.op("dve", lambda e, o4=o4, T=T: e.tensor_tensor(out=o4[:, :, 0:16], in0=T[:, :, 0:16], in1=T[:, :, 16:32], op=ALU.subtract), r=kr, w=ko)
                            S.op("dve", lambda e, o4=o4, T=T: e.tensor_tensor(out=o4[:, :, 16:32], in0=T[:, :, 32:48], in1=T[:, :, 48:64], op=ALU.add), r=kr, w=ko)
                            S.op("act", lambda e, o4=o4, z4=z4: e.activation(out=o4[:, :, 32:128], in_=z4[:, :, 32:128], func=AF.Copy), r=kb, w=ko)
                            qi = (0 if kind == "q" else 1)
                            dst = qT_send if kind == "q" else kT_send

                            def do_tr(zi=zi, j=j, qi=qi, t=t, dst=dst, last=(j == 3)):
                                bank2 = 6 + (cnt["tr"] % 2)
                                cnt["tr"] += 1
                                for a in range(4):
                                    S.op("pe", lambda e, a=a, zi=zi, bank2=bank2: e.transpose(
                                        PBb[bank2][:, a * 128:(a + 1) * 128], qkb[zi][:, a * 128:(a + 1) * 128], ident),
                                         r=[("qkb", zi), "ident"], w=[("pb", bank2, a)])
                                S.op("act", lambda e, bank2=bank2, qi=qi, j=j: e.activation(
                                    out=qst[qi][:, :, j * 128:(j + 1) * 128],
                                    in_=PBb[bank2][:, 0:512].rearrange("p (a b) -> p a b", a=4), func=AF.Copy),
                                     r=[("pb", bank2, a) for a in range(4)], w=[("qst", qi, j)])
                                if last:
                                    S.op("sp", lambda e, qi=qi, t=t, dst=dst: e.dma_start(
                                        out=dst[t * 512:(t + 1) * 512, tok0:tok0 + 512].rearrange("(a d) n -> d a n", d=128),
                                        in_=qst[qi]), r=[("qst", qi, jj) for jj in range(4)],
                                         w=[("qk_send", qi, st, t)] + [("qst", qi, jj) for jj in range(4)], dma="qst%d" % qi)
                            pend.append(do_tr)
                    if kind == "va" and t == nA - 1:
                        nch = (DA + 511) // 512
                        for j in range(4):
                            for ch in range(nch):
                                n = min(512, DA - ch * 512)
                                S.op("dve", lambda e, j=j, ch=ch, n=n: e.bn_stats(out=stats[:, j, 6 * ch:6 * ch + 6],
                                                                                  in_=gv[:, j, ch * 512:ch * 512 + n]),
                                     r=[("gv", j)], w=[("stats", j, ch)])
                            S.op("dve", lambda e, j=j: e.bn_aggr(out=mv[:, j, :], in_=stats[:, j, :]),
                                 r=[("stats", j, ch) for ch in range(nch)], w=[("mv", j)])
                            S.op("act", lambda e, j=j: e.activation(out=lrs[:, j:j + 1], in_=mv[:, j, 1:2], func=AF.Sqrt, bias=eps_t),
                                 r=[("mv", j), "eps"], w=[("lrs", j)])
                            S.op("dve", lambda e, j=j: e.reciprocal(out=lrs[:, j:j + 1], in_=lrs[:, j:j + 1]), r=[("lrs", j)], w=[("lrs", j)])
                            S.op("dve", lambda e, j=j: e.tensor_scalar(out=vln[:, j, :], in0=gv[:, j, :], scalar1=mv[:, j, 0:1],
                                                                       scalar2=lrs[:, j:j + 1], op0=ALU.subtract, op1=ALU.mult),
                                 r=[("gv", j), ("mv", j), ("lrs", j)], w=[("vln", j)])
                else:
                    for fcl in range(4):
                        fc = t * 4 + fcl
                        if kind == "u":
                            g = fc // 2
                            mb = 4 + (cnt["m"] % 2)
                            cnt["m"] += 1
                            for j in range(4):
                                S.op("pe", lambda e, mb=mb, j=j, fc=fc, g=g: e.matmul(
                                    PB[mb][:, j * 128:(j + 1) * 128], vln[:, j, fc * 128:(fc + 1) * 128], wmT[:, g, :],
                                    start=True, stop=True), r=[("vln", j), "wmT"], w=[("pbm", mb, j)])
                        bank = cnt["acc"] % 4
                        cnt["acc"] += 1
                        for kc in range(KC):
                            S.op("pe", lambda e, bank=bank, kc=kc, fcl=fcl, W=W: e.matmul(
                                PB[bank][:, 0:512], W[:, kc, fcl * 128:(fcl + 1) * 128], hT[:, kc, :],
                                start=(kc == 0), stop=(kc == KC - 1)), r=hT_all + [("w", sl)], w=[("pb", bank)])
                        flush_pend()
                        zi = cnt["z"] % 2
                        cnt["z"] += 1
                        if kind == "u":
                            S.op("act", lambda e, bank=bank, zi=zi: e.activation(out=gu[zi], in_=PB[bank][:, 0:512], func=AF.Gelu),
                                 r=[("pb", bank)], w=[("gu", zi)])
                            S.op("dve", lambda e, mb=mb, zi=zi, fc=fc: e.scalar_tensor_tensor(
                                out=t1[zi].rearrange("p (a b) -> p a b", a=4),
                                in0=PB[mb][:, 0:512].rearrange("p (a b) -> p a b", a=4), scalar=lgc[:, fc:fc + 1],
                                in1=Cb[:, fc, :].unsqueeze(1).broadcast_to([128, 4, 128]), op0=ALU.mult, op1=ALU.add),
                                 r=[("pbm", mb, j) for j in range(4)] + ["lgc", ("Cb", fc)], w=[("t1", zi)])
                            S.op("dve", lambda e, zi=zi, fcl=fcl: e.tensor_tensor(out=p1[:, fcl, :], in0=t1[zi], in1=gu[zi], op=ALU.mult),
                                 r=[("t1", zi), ("gu", zi)], w=[("p1", fcl)])
                        elif kind == "ga":
                            S.op("act", lambda e, bank=bank, zi=zi: e.activation(out=sg[zi], in_=PB[bank][:, 0:512], func=AF.Silu),
                                 r=[("pb", bank)], w=[("sg", zi)])
                            S.op("dve", lambda e, zi=zi, fcl=fcl: e.tensor_tensor(out=yst[zi], in0=p1[:, fcl, :], in1=sg[zi], op=ALU.mult),
                                 r=[("p1", fcl), ("sg", zi)], w=[("yst", zi)])
                            S.op("sp", lambda e, zi=zi, fc=fc: e.dma_start(out=yaT_sp[fc * 128:(fc + 1) * 128, tok0:tok0 + 512], in_=yst[zi]),
                                 r=[("yst", zi)], w=[("yaT_sp", st, fc)], dma="yst%d" % zi)
                        else:
                            S.op("act", lambda e, bank=bank, zi=zi: e.activation(out=sgst[zi], in_=PB[bank][:, 0:512], func=AF.Silu),
                                 r=[("pb", bank)], w=[("sgst", zi)])
                            S.op("sp", lambda e, zi=zi, fc=fc: e.dma_start(out=sgbT_sp[fc * 128:(fc + 1) * 128, tok0:tok0 + 512], in_=sgst[zi]),
                                 r=[("sgst", zi)], w=[("sgbT_sp", st, fc)], dma="sgst%d" % zi)
            flush_pend()

        g0 = 0
        for st_ in range(NST):
            run_supertile(st_, tiles_a, g0)
            g0 += len(tiles_a)
        kkeys = [("qk_send", 1, st_, t_) for st_ in range(NST) for t_ in range(nB)]
        qkeys = [("qk_send", 0, st_, t_) for st_ in range(NST) for t_ in range(nB)]
        vkeys = [("v_send", st_, j_, t_) for st_ in range(NST) for j_ in range(4) for t_ in range(nB)]
        for h in range(H):
            collective(GROUP4, kT_send[h * 256:(h + 1) * 256, :], kT_g[h].rearrange("r a t -> (r a) t"), r=kkeys, w=[("kT_g", h)], semkey="cck")
        for h in range(H):
            collective(GROUP4, v_send[h], v_g[h], r=vkeys, w=[("v_g", h)], semkey="ccv")
        for h in range(H):
            collective(GROUP4, qT_send[h * 256:(h + 1) * 256, :], qT_g[h].rearrange("r a t -> (r a) t"), r=qkeys, w=[("qT_g", h)], semkey="ccq")
        for st_ in range(NST):
            run_supertile(st_, tiles_b, g0)
            g0 += len(tiles_b)
        S.fence()
        if c.stop < 3:
            continue

        if c.stop < 4:
            continue
        arena.top = PERSIST_TOP
        kT = arena.alloc((HC, S_), BF16)
        Vsb = arena.alloc((NKT, HPC, 257), BF16)
        qT = [arena.alloc((HC, 512), BF16) for _ in range(2)]
        PT = [arena.alloc(512, BF16) for _ in range(4)]
        PTd = [arena.alloc(512, BF16) for _ in range(4)]
        o1 = arena.alloc((4, 256), F32)
        oo = arena.alloc((4, 256), F32)
        junk = arena.alloc(256, BF16)
        rl = arena.alloc(8, F32)
        oss = arena.alloc(4, F32)
        ors = arena.alloc(4, F32)
        onb = arena.alloc((4, 256), BF16)
        onst = [arena.alloc((HC, 512), BF16) for _ in range(2)]

        def hp_of(e):
            if "hp" not in hp_cache:
                hp_cache["hp"] = e.partition_id() % RANKS
            return hp_cache["hp"]

        def private_copy(dst, src_g, key_src, key_dst, sem):
            nd = len(src_g.shape)
            names = " ".join("d%d" % i for i in range(nd))
            dn = " ".join("e%d" % i for i in range(len(dst.shape)))
            S.op("sp", lambda e: e.dma_start(
                out=dst.rearrange("%s -> (%s)" % (dn, dn)).rearrange("(p f) -> p f", f=4096),
                in_=src_g[bass.ds(hp_of(e), 1)].rearrange("%s -> (%s)" % (names, names)).rearrange("(p f) -> p f", f=4096)),
                 r=[key_src], w=[key_dst], dma=sem)

        private_copy(kT_my, kT_g.rearrange("(a b) r c t -> a b r c t", b=HPC), "kT_g", "kT_my", "pck")
        private_copy(v_my, v_g.rearrange("(a b) n e -> a b n e", b=HPC), "v_g", "v_my", "pcv")
        private_copy(qT_my, qT_g.rearrange("(a b) r c t -> a b r c t", b=HPC), "qT_g", "qT_my", "pcq")
        for hl in range(HPC):
            for r in range(RANKS):
                S.op("sp", lambda e, r=r, hl=hl: e.dma_start(
                    out=kT[:, hl * 2:hl * 2 + 2, r * TPC:(r + 1) * TPC],
                    in_=kT_my[hl, r].rearrange("(c d) t -> d c t", d=128)),
                     r=["kT_my"], w=[("kT", r, hl)], dma="ldk%d" % hl)
        for hl in range(HPC):
            S.op("sp", lambda e, hl=hl: e.dma_start(
                out=Vsb[:, :, hl, 0:256], in_=v_my[hl].rearrange("(kt p) e -> p kt e", p=128)),
                 r=["v_my"], w=[("V", hl)], dma="ldv%d" % hl)
        S.op("pool", lambda e: e.memset(Vsb[:, :, :, 256:257], 1.0), w=["Vone"])
        for kb in range(4):
            S.op("pool", lambda e, kb=kb: e.memset(PTd[kb], 0.0), w=[("PTd", kb)])
        if l + 1 < DEPTH:
            gather_layer(l + 1)
        scale = float(128 ** -0.5)
        ptc = 0
        sbank = 0
        deferred = []
        for Gq in range(NQG):
            rq, oq = Gq // NST, (Gq % NST) * 512
            qb_ = qT[Gq % 2]
            for hl in range(HPC):
                S.op("sp", lambda e, qb_=qb_, rq=rq, oq=oq, hl=hl: e.dma_start(
                    out=qb_[:, hl * 2:hl * 2 + 2, :],
                    in_=qT_my[hl, rq, :, oq:oq + 512].rearrange("(c d) t -> d c t", d=128)),
                     r=["qT_my"], w=[("qT", Gq % 2, hl)], dma="ldq%d_%d" % (Gq % 2, hl))
            ntile = 4 * Gq + 4
            ost = onst[Gq % 2]
            for hl in range(HPC):
                for cpt in range(2):
                    hc = hl * 2 + cpt
                    prev = None
                    for t in range(ntile + 1):
                        if t < ntile:
                            kb = t - 4 * Gq
                            c_lo = max(kb, 0) * 128
                            sb = 4 + (sbank % 2)
                            sbank += 1
                            S.op("pe", lambda e, sb=sb, hc=hc, t=t, c_lo=c_lo, qb_=qb_: e.matmul(
                                PB[sb][:, c_lo:512], kT[:, hc, t * 128:(t + 1) * 128], qb_[:, hc, c_lo:512],
                                start=True, stop=True), r=[("kT", rr_, hl) for rr_ in range(RANKS)] + [("qT", Gq % 2, hl)], w=[("pbs", sb)])
                            if kb >= 0:
                                P = PTd[kb]
                                pk = ("PTd", kb)
                                S.op("act", lambda e, P=P, sb=sb, c_lo=c_lo: e.activation(
                                    out=P[:, c_lo + 64:512], in_=PB[sb][:, c_lo + 64:512], func=AF.Exp, scale=scale),
                                     r=[("pbs", sb)], w=[pk])
                                S.op("act", lambda e, P=P, sb=sb, c_lo=c_lo: e.activation(
                                    out=P[0:64, c_lo:c_lo + 64], in_=PB[sb][0:64, c_lo:c_lo + 64], func=AF.Exp, scale=scale),
                                     r=[("pbs", sb)], w=[pk])
                            else:
                                P = PT[ptc % 4]
                                pk = ("PT", ptc % 4)
                                ptc += 1
                                S.op("act", lambda e, P=P, sb=sb: e.activation(out=P, in_=PB[sb][:, 0:512], func=AF.Exp, scale=scale),
                                     r=[("pbs", sb)], w=[pk])
                            cur = (t, max(kb, 0), P, pk)
                        else:
                            cur = None
                        if prev is not None:
                            pt_, qlo, P_, pk_ = prev
                            for qb in range(qlo, 4):
                                last_t = 4 * Gq + qb
                                S.op("pe", lambda e, qb=qb, P_=P_, pt_=pt_, hl=hl, last_t=last_t: e.matmul(
                                    PB[qb][:, 0:257], P_[:, qb * 128:(qb + 1) * 128], Vsb[:, pt_, hl, :],
                                    start=(pt_ == 0), stop=(pt_ == last_t)),
                                     r=[pk_, ("V", hl), "Vone"], w=[("pbo", qb)])
                        prev = cur
                        if t == 1 and deferred:
                            for f in deferred:
                                f()
                            del deferred[:]
                    for qb in range(4):
                        if cpt == 0:
                            S.op("dve", lambda e, qb=qb: e.reciprocal(out=rl[:, qb:qb + 1], in_=PB[qb][:, 256:257]),
                                 r=[("pbo", qb)], w=[("rl", qb)])
                            S.op("dve", lambda e, qb=qb: e.tensor_scalar_mul(out=o1[:, qb, :], in0=PB[qb][:, 0:256], scalar1=rl[:, qb:qb + 1]),
                                 r=[("pbo", qb), ("rl", qb)], w=[("o1", qb)])
                        else:
                            S.op("dve", lambda e, qb=qb: e.reciprocal(out=rl[:, 4 + qb:5 + qb], in_=PB[qb][:, 256:257]),
                                 r=[("pbo", qb)], w=[("rl2", qb)])
                            S.op("dve", lambda e, qb=qb: e.tensor_tensor(out=rl[:, 4 + qb:5 + qb], in0=rl[:, 4 + qb:5 + qb], in1=neg_lam, op=ALU.mult),
                                 r=[("rl2", qb), "neg_lam"], w=[("rl2", qb)])
                            S.op("dve", lambda e, qb=qb: e.scalar_tensor_tensor(
                                out=oo[:, qb, :], in0=PB[qb][:, 0:256], scalar=rl[:, 4 + qb:5 + qb], in1=o1[:, qb, :],
                                op0=ALU.mult, op1=ALU.add), r=[("pbo", qb), ("rl2", qb), ("o1", qb)], w=[("oo", qb)])
                            S.op("act", lambda e, qb=qb: e.activation(out=junk, in_=oo[:, qb, :], func=AF.Square, accum_out=oss[:, qb:qb + 1]),
                                 r=[("oo", qb)], w=["junk", ("oss", qb)])
                            S.op("act", lambda e, qb=qb: e.activation(out=ors[:, qb:qb + 1], in_=oss[:, qb:qb + 1], func=AF.Ln,
                                                                      scale=1.0 / 256, bias=eps_t), r=[("oss", qb), "eps"], w=[("ors", qb)])
                            S.op("act", lambda e, qb=qb: e.activation(out=ors[:, qb:qb + 1], in_=ors[:, qb:qb + 1], func=AF.Exp, scale=-0.5),
                                 r=[("ors", qb)], w=[("ors", qb)])
                            S.op("dve", lambda e, qb=qb: e.scalar_tensor_tensor(
                                out=onb[:, qb, :], in0=oo[:, qb, :], scalar=ors[:, qb:qb + 1], in1=gsub, op0=ALU.mult, op1=ALU.mult),
                                 r=[("oo", qb), ("ors", qb), "gsub"], w=[("onb", qb)])
                    if cpt == 1:
                        def make_deferred(Gq=Gq, hl=hl, ost=ost):
                            def f():
                                for qb in range(4):
                                    for half in range(2):
                                        S.op("pe", lambda e, qb=qb, half=half: e.transpose(
                                            PBb[6][:, (qb * 2 + half) * 128:(qb * 2 + half + 1) * 128],
                                            onb[:, qb, half * 128:(half + 1) * 128], ident),
                                             r=[("onb", qb), "ident"], w=[("pbt", qb, half)])
                                for half in range(2):
                                    S.op("dve", lambda e, half=half: e.tensor_copy(
                                        out=ost[:, hl * 2 + half, :].rearrange("p (a b) -> p a b", a=4),
                                        in_=PBb[6][:, 0:1024].rearrange("p (a h b) -> p a h b", a=4, h=2)[:, :, half, :]),
                                         r=[("pbt", qb, half) for qb in range(4)], w=[("onst", Gq % 2, hl, half)])
                                if hl == HPC - 1:
                                    okeys = [("onst", Gq % 2, h_, f_) for h_ in range(HPC) for f_ in range(2)]
                                    S.op("sp", lambda e: e.dma_start(
                                        out=onT_send[Gq // NST, :, (Gq % NST) * 512:(Gq % NST + 1) * 512].rearrange("(a d) n -> d a n", d=128), in_=ost),
                                         r=okeys, w=[("onT_send", Gq)] + okeys, dma="onst%d" % (Gq % 2))
                            return f
                        deferred.append(make_deferred())
        for f in deferred:
            f()
        del deferred[:]
        S.fence()
        if c.stop < 5:
            continue
        for tq_ in range(RANKS):
            for hl in range(HPC):
                collective(GROUP4, onT_send[tq_, hl * 256:(hl + 1) * 256, :], onT_g[tq_, hl].rearrange("r a t -> (r a) t"),
                           r=[], w=["onT_g"], semkey="ccon")
        S.fence()
        if c.stop < 6:
            continue

        private_copy(on_my, onT_g, "onT_g", "on_my", "pco")
        arena.top = PERSIST_TOP
        wos = [arena.alloc((NCM, 512), BF16) for _ in range(2)]
        yT = arena.alloc((NCM, 512), BF16)
        sgT = arena.alloc((NCB, 512), BF16)
        xres = [arena.alloc(512, F32) for _ in range(4)]
        xo = [arena.alloc(512, F32) for _ in range(2)]
        ND = DM // 512
        xc = 0
        acc = 0
        def wo_load(gi_):
            sl_, dg_ = gi_ % 2, gi_ % ND
            S.op("sp", lambda e, sl_=sl_, dg_=dg_: e.dma_start(out=wos[sl_].rearrange("p k n -> p (k n)"), in_=wg_out[l][dg_]),
                 r=[("wout%d" % l, "g", dg_)], w=[("wo", sl_)], dma="wo%d" % sl_)

        for st in range(NST):
            tok0 = st * 512
            S.op("sp", lambda e, tok0=tok0: e.dma_start(
                out=yT[:, 0:NCA, :], in_=yaT_sp[:, tok0:tok0 + 512].rearrange("(a d) n -> d a n", d=128)),
                 w=["yaT"], dma="ldya")
            for hl in range(HPC):
                for r in range(RANKS):
                    c0 = NCA + (r * HPC + hl) * 2
                    S.op("sp", lambda e, tok0=tok0, hl=hl, r=r, c0=c0: e.dma_start(
                        out=yT[:, c0:c0 + 2, :],
                        in_=on_my[hl, r, :, tok0:tok0 + 512].rearrange("(c d) t -> d c t", d=128)),
                         r=["on_my"], w=[("onT", c0 - NCA), ("onT", c0 - NCA + 1), ("ybT", c0 - NCA), ("ybT", c0 - NCA + 1)], dma="ldon%d" % (r * HPC + hl))
            S.op("sp", lambda e, tok0=tok0: e.dma_start(
                out=sgT, in_=sgbT_sp[:, tok0:tok0 + 512].rearrange("(a d) n -> d a n", d=128)),
                 w=["sgT"], dma="ldsg")
            for cb in range(NCB):
                eng = "dve"
                S.op(eng, lambda e, cb=cb: e.tensor_tensor(out=yT[:, NCA + cb, :], in0=yT[:, NCA + cb, :], in1=sgT[:, cb, :], op=ALU.mult),
                     r=[("onT", cb), "sgT"], w=[("ybT", cb)])
            ykeys = ["yaT"] + [("ybT", cb) for cb in range(NCB)]
            for dg in range(ND):
                gi_ = st * ND + dg
                sl = gi_ % 2
                if gi_ == 0:
                    wo_load(0)
                if gi_ + 1 < NST * ND:
                    wo_load(gi_ + 1)
                for j in range(4):
                    r0 = tok0 + j * 128
                    xi = xc % 4
                    xoi = xc % 2
                    xc += 1
                    S.op("sp", lambda e, xi=xi, r0=r0, dg=dg: e.dma_start(out=xres[xi], in_=x_cur[r0:r0 + 128, dg * 512:(dg + 1) * 512]),
                         w=[("xres", xi)], dma="xres%d" % xi)
                    bank = acc % 4
                    acc += 1
                    for kc in range(NCM):
                        S.op("pe", lambda e, bank=bank, kc=kc, j=j, sl=sl: e.matmul(
                            PB[bank][:, 0:512], yT[:, kc, j * 128:(j + 1) * 128], wos[sl][:, kc, :],
                            start=(kc == 0), stop=(kc == NCM - 1)), r=ykeys + [("wo", sl)], w=[("pb", bank)])
                    S.op("dve", lambda e, bank=bank, xi=xi, xoi=xoi: e.tensor_tensor(out=xo[xoi], in0=PB[bank][:, 0:512], in1=xres[xi], op=ALU.add),
                         r=[("pb", bank), ("xres", xi)], w=[("xo", xoi)])
                    S.op("sp", lambda e, xoi=xoi, r0=r0, dg=dg: e.dma_start(out=x_next[r0:r0 + 128, dg * 512:(dg + 1) * 512], in_=xo[xoi]),
                         r=[("xo", xoi)], w=[("x_next", st, j, dg)], dma="xo%d" % xoi)
        S.fence()

    arena.top = PERSIST_TOP
    fg = arena.alloc(DM, F32)
    NFB = 4
    xf = [arena.alloc(DM, F32) for _ in range(NFB)]
    yf = [arena.alloc(DM, F32) for _ in range(NFB)]
    fss = arena.alloc(NFB, F32)
    frs = arena.alloc(NFB, F32)
    S.op("sp", lambda e: e.dma_start(out=fg, in_=fing_in.partition_broadcast(128)), w=["fg"], dma="ldc0")
    x_last = xs[DEPTH - 1]
    for blk in range(NBLK):
        i = blk % NFB
        r0 = blk * 128
        S.op("sp", lambda e, i=i, r0=r0: e.dma_start(out=xf[i], in_=x_last[r0:r0 + 128, :]), w=[("xf", i)], dma="xf%d" % i)
        S.op("act", lambda e, i=i: e.activation(out=yf[i], in_=xf[i], func=AF.Square, accum_out=fss[:, i:i + 1]),
             r=[("xf", i)], w=[("yf", i), ("fss", i)])
        S.op("act", lambda e, i=i: e.activation(out=frs[:, i:i + 1], in_=fss[:, i:i + 1], func=AF.Sqrt, scale=1.0 / DM, bias=eps_t),
             r=[("fss", i), "eps"], w=[("frs", i)])
        S.op("dve", lambda e, i=i: e.reciprocal(out=frs[:, i:i + 1], in_=frs[:, i:i + 1]), r=[("frs", i)], w=[("frs", i)])
        S.op("dve", lambda e, i=i: e.scalar_tensor_tensor(out=yf[i], in0=xf[i], scalar=frs[:, i:i + 1], in1=fg, op0=ALU.mult, op1=ALU.mult),
             r=[("xf", i), ("frs", i), "fg"], w=[("yf", i)])
        S.op("sp", lambda e, i=i, r0=r0: e.dma_start(out=out_d[r0:r0 + 128, :], in_=yf[i]), r=[("yf", i)], w=[("out", blk)], dma="yf%d" % i)
    if c.debug:
        S.fence()
        for nm, src in (("qT", qT_send), ("kT", kT_send), ("v", v_send), ("on", onT_send), ("ya", yaT_sp),
                        ("sgb", sgbT_sp), ("x0", xs[0]), ("ong", onT_g)):
            if isinstance(c.debug, (list, tuple)) and nm not in c.debug:
                continue
            nd = len(src.shape)
            names = " ".join("d%d" % i for i in range(nd))
            tot = 1
            for v_ in src.shape:
                tot *= v_
            flat = lambda ap_, names=names: ap_.rearrange("%s -> (%s)" % (names, names)).rearrange("(p f) -> p f", p=128)
            dd = nc.dram_tensor("dbg_" + nm, [128, tot // 128], src.dtype, kind="ExternalOutput").ap()
            S.op("pool", lambda e, dd=dd, src=src, flat=flat: e.dma_start(out=dd, in_=flat(src)),
                 w=[("dbg", nm)], dma="dbg_" + nm)
    S.fence(final=True)

    keys = S.finalize()
    with nc.cleanup_on_exit():
        sems = {k: nc.alloc_semaphore(name="s%d" % i) for i, k in enumerate(keys)}
        for k in keys:
            nc.gpsimd.sem_clear(sems[k])
        nc.all_engine_barrier()
        with nc.Block() as block:
            S.emit(block, sems)
        nc.all_engine_barrier()
    for cx in reversed(ctxs):
        cx.__exit__(None, None, None)
    return nc, len(S.ops), len(keys)


def make_in_maps(cfg, x, positions, norm_g, w_in, ln_g, ln_b, sgu_w, sgu_b,
                 lam_q1, lam_k1, lam_q2, lam_k2, subln_g, w_out, final_g):
    c = cfg
    f = np.float32
    x = np.asarray(x, f)
    positions = np.asarray(positions, np.int32)
    DEPTH = c.DEPTH
    invf = (np.float32(ROPE_THETA) ** (-np.arange(0, 32, 2, dtype=np.float32) / np.float32(32))).astype(f)[None, :]
    ngc = np.ascontiguousarray(np.asarray(norm_g, f).reshape(DEPTH, c.KC, 128).transpose(0, 2, 1))
    lgc = np.ascontiguousarray(np.asarray(ln_g, f).reshape(DEPTH, c.NCA, 128).transpose(0, 2, 1))
    lbc = np.ascontiguousarray(np.asarray(ln_b, f).reshape(DEPTH, c.NCA, 128).transpose(0, 2, 1))
    sguw = np.ascontiguousarray(np.asarray(sgu_w, f))
    sgub = np.ascontiguousarray(np.asarray(sgu_b, f).reshape(DEPTH, 1, c.G * 128))
    lam = np.ascontiguousarray(np.concatenate([np.asarray(a, f) for a in (lam_q1, lam_k1, lam_q2, lam_k2)], axis=1)
                               .reshape(DEPTH, 1, 4 * 128))
    subg = np.ascontiguousarray(np.asarray(subln_g, f).reshape(DEPTH, 1, 256))
    fing = np.ascontiguousarray(np.asarray(final_g, f).reshape(1, c.DM))
    w_in = np.asarray(w_in, f)
    w_out = np.asarray(w_out, f)
    maps = []
    ri, ro = c.DM // RANKS, c.DMIX // RANKS
    for core in range(NCORES):
        b, r = core // RANKS, core % RANKS
        t0 = r * c.TPC
        maps.append({
            "x": np.ascontiguousarray(x[b, t0:t0 + c.TPC, :]),
            "pos_t": np.ascontiguousarray(positions[b, t0:t0 + c.TPC].reshape(c.NBLK, 128).T),
            "invf": invf, "norm_gc": ngc, "ln_gc": lgc, "ln_bc": lbc, "sgu_w": sguw, "sgu_b": sgub,
            "lam": lam, "subln_g": subg, "final_g": fing,
            "w_in_sh": np.ascontiguousarray(w_in.reshape(DEPTH, c.KC, RANKS, 32, c.DIN)[:, :, r].reshape(DEPTH, c.KC * 32, c.DIN)),
            "w_out_sh": np.ascontiguousarray(w_out.reshape(DEPTH, c.NCM, RANKS, 32, c.DM)[:, :, r].reshape(DEPTH, c.NCM * 32, c.DM)),
        })
    return maps


_CACHE = {}


def run(cfg, inputs):
    key = (cfg.S, cfg.DM, cfg.G, cfg.H, cfg.DEPTH, cfg.stop)
    if key not in _CACHE:
        _CACHE[key] = build_program(cfg)[0]
    nc = _CACHE[key]
    maps = make_in_maps(cfg, **inputs)
    res = run_bass_kernel_spmd(nc, maps, core_ids=list(range(NCORES)))
    out = np.empty((2, cfg.S, cfg.DM), np.float32)
    for core in range(NCORES):
        b, r = core // RANKS, core % RANKS
        out[b, r * cfg.TPC:(r + 1) * cfg.TPC, :] = res.results[core]["out"]
    if cfg.debug:
        return out, res.results
    return out


def kernel(**inputs):
    return run(Cfg(), inputs)
```

```python
import math
import os
import numpy as np
import concourse.bass as bass
import concourse.mybir as mybir
from concourse.bass_utils import run_bass_kernel_spmd

F32 = mybir.dt.float32
BF16 = mybir.dt.bfloat16
I32 = mybir.dt.int32
AF = mybir.ActivationFunctionType
ALU = mybir.AluOpType
AX = mybir.AxisListType

EPS = 1e-5
ROPE_THETA = 500000.0
NCORES = 8
RANKS = 4


class Cfg:
    def __init__(self, S=8192, DM=4096, G=8, H=8, DEPTH=2, debug=False, stop=99):
        self.S, self.DM, self.G, self.H, self.DEPTH = S, DM, G, H, DEPTH
        self.debug = debug
        self.stop = stop
        self.DA = G * 256
        self.DB = H * 256
        self.DMIX = self.DA + self.DB
        self.DIN = 3 * self.DA + 4 * self.DB
        self.TPC = S // RANKS
        self.NST = self.TPC // 512
        self.NBLK = self.TPC // 128
        self.KC = DM // 128
        self.HPC = H // RANKS
        self.NCA = self.DA // 128
        self.NCB = self.DB // 128
        self.NCM = self.DMIX // 128


import types


def _freeze(fn):
    if fn.__closure__ is None:
        return fn
    cells = []
    for cl in fn.__closure__:
        try:
            cells.append(types.CellType(cl.cell_contents))
        except ValueError:
            cells.append(cl)
    return types.FunctionType(fn.__code__, fn.__globals__, fn.__name__, fn.__defaults__, tuple(cells))


class _Op:
    __slots__ = ("eng", "fn", "deps", "kind", "semkey", "count", "signalled", "inc", "idx")

    def __init__(self, eng, fn, kind, semkey, inc):
        self.eng, self.fn, self.kind, self.semkey, self.inc = eng, fn, kind, semkey, inc
        self.deps = []
        self.count = None
        self.signalled = False


class Sched:
    ENGS = ("pe", "act", "dve", "pool", "sp")

    def __init__(self):
        self.ops = []
        self.last_writer = {}
        self.readers = {}
        self.epoch = 0
        self.last_eng_op = {}
        self.async_since_fence = []
        self.ncc = 0
        self.persist = {}
        self.persist_ops = []

    def op(self, eng, fn, r=(), w=(), dma=None, inc=None, persist=False):
        kind = "d" if dma is not None else "c"
        semkey = ("dma", dma) if dma is not None else ("eng", eng, self.epoch)
        o = _Op(eng, _freeze(fn), kind, semkey, inc if inc is not None else (16 if kind == "d" else 1))
        deps = {}
        for k in r:
            lw = self.last_writer.get(k)
            if lw is None:
                lw = self.persist.get(k)
            if lw is not None:
                deps[id(lw)] = lw
        for k in w:
            lw = self.last_writer.get(k)
            if lw is not None:
                deps[id(lw)] = lw
            for rd in self.readers.get(k, ()):
                deps[id(rd)] = rd
        o.deps = list(deps.values())
        for k in w:
            self.last_writer[k] = o
            self.readers[k] = []
        for k in r:
            self.readers.setdefault(k, []).append(o)
        self.ops.append(o)
        if persist:
            for k in w:
                self.persist[k] = o
            self.persist_ops.append(o)
        elif kind == "d":
            self.async_since_fence.append(o)
        else:
            self.last_eng_op[eng] = o
        return o

    def fence(self, final=False, skip=()):
        tails = [o for e_, o in self.last_eng_op.items() if e_ not in skip] + list(self.async_since_fence)
        if final:
            tails += self.persist_ops
        new_last = {e_: o for e_, o in self.last_eng_op.items() if e_ in skip}
        for e in self.ENGS:
            if e in skip:
                continue
            o = _Op(e, lambda eng: eng.nop(), "c", ("eng", e, self.epoch), 1)
            o.deps = [t for t in tails]
            self.ops.append(o)
            new_last[e] = o
        self.last_eng_op = new_last
        self.async_since_fence = []
        self.last_writer = {}
        self.readers = {}

    def finalize(self):
        for i, o in enumerate(self.ops):
            o.idx = i
        for o in self.ops:
            best = {}
            for d in o.deps:
                b = best.get(d.semkey)
                if b is None or d.idx > b.idx:
                    best[d.semkey] = d
            o.deps = list(best.values())
        for o in self.ops:
            for d in o.deps:
                if d.kind == "d":
                    d.signalled = True
                elif d.eng != o.eng or o.kind == "d" or d.eng != "pe":
                    d.signalled = True
        counts = {}
        for o in self.ops:
            if o.kind == "d":
                o.signalled = True
            if o.signalled:
                c = counts.get(o.semkey, 0) + o.inc
                counts[o.semkey] = c
                o.count = c
        self.counts = counts
        return list(counts.keys())

    def emit(self, block, sems):
        per_eng = {e: [] for e in self.ENGS}
        for o in self.ops:
            per_eng[o.eng].append(o)

        def run(eng_name):
            def body(e):
                waited = {}
                for o in per_eng[eng_name]:
                    need = {}
                    for d in o.deps:
                        if not d.signalled:
                            continue
                        if d.kind == "c" and d.eng == eng_name and eng_name == "pe" and o.kind == "c":
                            continue
                        if need.get(d.semkey, 0) < d.count:
                            need[d.semkey] = d.count
                    for sk, c in need.items():
                        if waited.get(sk, 0) >= c:
                            continue
                        e.wait_ge(sems[sk], c)
                        waited[sk] = c
                    ins = o.fn(e)
                    if o.signalled:
                        ins.then_inc(sems[o.semkey], o.inc)
            return body

        block.tensor(run("pe"))
        block.scalar(run("act"))
        block.vector(run("dve"))
        block.gpsimd(run("pool"))
        block.sync(run("sp"))


class Arena:
    def __init__(self, ap, nbytes):
        self.ap, self.cap, self.top = ap, nbytes, 0

    def alloc(self, shape, dt):
        if isinstance(shape, int):
            shape = (shape,)
        n = 1
        for s in shape:
            n *= s
        nb = n * (2 if dt == BF16 else 4)
        off = self.top
        self.top = off + (nb + 63) // 64 * 64
        assert self.top <= self.cap, ("SBUF arena overflow", self.top, self.cap)
        v = self.ap[:, off // 2:(off + nb) // 2]
        if dt != BF16:
            v = v.bitcast(dt)
        if len(shape) == 2:
            v = v.rearrange("p (a b) -> p a b", a=shape[0], b=shape[1])
        elif len(shape) == 3:
            v = v.rearrange("p (a b c) -> p a b c", a=shape[0], b=shape[1], c=shape[2])
        return v


def lambda_init_fn(layer_idx):
    return 0.8 - 0.6 * math.exp(-0.3 * layer_idx)


def build_program(cfg):
    c = cfg
    S_, DM, G, H, DEPTH = c.S, c.DM, c.G, c.H, c.DEPTH
    DA, DB, DMIX, DIN = c.DA, c.DB, c.DMIX, c.DIN
    TPC, NST, NBLK, KC, HPC = c.TPC, c.NST, c.NBLK, c.KC, c.HPC
    NCA, NCB, NCM = c.NCA, c.NCB, c.NCM
    NKT = S_ // 128
    NQG = S_ // 512
    HC = HPC * 2
    OFF_U, OFF_V, OFF_GA = 0, DA, 2 * DA
    OFF_Q, OFF_K, OFF_VB, OFF_GB = 3 * DA, 3 * DA + DB, 3 * DA + 2 * DB, 3 * DA + 3 * DB

    nc = bass.Bass("TRN2", target_bir_lowering=False)

    def din(name, shape, dt=F32):
        return nc.dram_tensor(name, list(shape), dt, kind="ExternalInput").ap()

    x_in = din("x", [TPC, DM])
    pos_in = din("pos_t", [128, NBLK], I32)
    invf_in = din("invf", [1, 16])
    ngc_in = din("norm_gc", [DEPTH, 128, KC])
    lgc_in = din("ln_gc", [DEPTH, 128, NCA])
    lbc_in = din("ln_bc", [DEPTH, 128, NCA])
    sguw_in = din("sgu_w", [DEPTH, G, 128, 128])
    sgub_in = din("sgu_b", [DEPTH, 1, G * 128])
    lam_in = din("lam", [DEPTH, 1, 4 * 128])
    subg_in = din("subln_g", [DEPTH, 1, 256])
    fing_in = din("final_g", [1, DM])
    win_in = din("w_in_sh", [DEPTH, KC * 32, DIN])
    wout_in = din("w_out_sh", [DEPTH, NCM * 32, DM])
    out_d = nc.dram_tensor("out", [TPC, DM], F32, kind="ExternalOutput").ap()

    def dscr(name, shape, dt=BF16):
        return nc.dram_tensor(name, list(shape), dt).ap()

    wsh_in = [dscr("wsh_in%d" % l, [DIN // 512, 32, KC, 512]) for l in range(DEPTH)]
    wsh_out = [dscr("wsh_out%d" % l, [DM // 512, 32, NCM, 512]) for l in range(DEPTH)]
    wg_in = [dscr("wg_in%d" % l, [DIN // 512, 128, KC * 512]) for l in range(DEPTH)]
    wg_out = [dscr("wg_out%d" % l, [DM // 512, 128, NCM * 512]) for l in range(DEPTH)]
    xs = [dscr("xs%d" % l, [TPC, DM], F32) for l in range(DEPTH)]
    qT_send = dscr("qT_send", [DB, TPC])
    kT_send = dscr("kT_send", [DB, TPC])
    v_send = dscr("v_send", [H, TPC, 256])
    qT_g = dscr("qT_g", [H, RANKS, 256, TPC])
    kT_g = dscr("kT_g", [H, RANKS, 256, TPC])
    v_g = dscr("v_g", [H, RANKS * TPC, 256])
    onT_send = dscr("onT_send", [RANKS, HPC * 256, TPC])
    onT_g = dscr("onT_g", [RANKS, HPC, RANKS, 256, TPC])
    kT_my = dscr("kT_my", [HPC, RANKS, 256, TPC])
    qT_my = dscr("qT_my", [HPC, RANKS, 256, TPC])
    v_my = dscr("v_my", [HPC, RANKS * TPC, 256])
    on_my = dscr("on_my", [HPC, RANKS, 256, TPC])
    yaT_sp = dscr("yaT_sp", [DA, TPC])
    sgbT_sp = dscr("sgbT_sp", [DB, TPC])

    ARENA_BYTES = int(os.environ.get("ARENA_KB", "206")) * 1024
    ctxs = [nc.sbuf_tensor("arena", [128, ARENA_BYTES // 2], BF16)]
    ctxs += [nc.psum_tensor("pb%d" % i, [128, 512], F32) for i in range(8)]
    ents = [cx.__enter__() for cx in ctxs]
    arena = Arena(ents[0], ARENA_BYTES)
    PB = ents[1:9]
    PBb = [p[:, 0:512].bitcast(BF16) for p in PB]

    S = Sched()
    hp_cache = {}
    GROUP4 = [[0, 1, 2, 3], [4, 5, 6, 7]]
    GROUP8 = [list(range(8))]

    def collective(groups, src, dst, r, w, semkey, persist=False):
        S.op("pool", lambda e: e.collective_compute("AllGather", ALU.bypass, replica_groups=groups,
                                                    ins=[src.opt()], outs=[dst.opt()]),
             r=r, w=w, dma=semkey, inc=1, persist=persist)

    def dump(name, ap, rkeys):
        if not (isinstance(c.debug, (list, tuple)) and ("dump_" + name) in c.debug):
            return
        shp = list(ap.shape)
        n = 1
        for v_ in shp[1:]:
            n *= v_
        dd = nc.dram_tensor("dump_" + name, [shp[0], n], ap.dtype, kind="ExternalOutput").ap()
        src = ap
        if len(shp) == 3:
            dd = dd.rearrange("p (a b) -> p a b", a=shp[1])
        elif len(shp) == 4:
            dd = dd.rearrange("p (a b c) -> p a b c", a=shp[1], b=shp[2])
        S.op("sp", lambda e: e.dma_start(out=dd, in_=src), r=rkeys, w=[("dump", name)], dma="dump_" + name)

    ident = arena.alloc(128, BF16)
    ones_bf = arena.alloc(128, BF16)
    eps_t = arena.alloc(1, F32)
    cos_t = arena.alloc((NBLK, 16), F32)
    sin_t = arena.alloc((NBLK, 16), F32)
    gcol = arena.alloc(KC, F32)
    lgc = arena.alloc(NCA, F32)
    lbc = arena.alloc(NCA, F32)
    wmT = arena.alloc((G, 128), BF16)
    Cb = arena.alloc((NCA, 128), F32)
    gsub = arena.alloc(256, F32)
    neg_lam = arena.alloc(1, F32)
    PERSIST_TOP = arena.top

    def cast_gather(src3, l, sh, g, nchunk, cols, tag, order=None):
        src = src3[l]
        ntile = cols // 512
        ckeys = []
        for kc in range(nchunk):
            S.op("pool", lambda e, kc=kc: e.dma_start(
                out=sh[:, :, kc, :].rearrange("t i n -> i t n"),
                in_=src[kc * 32:(kc + 1) * 32, :].rearrange("i (t n) -> i t n", n=512)),
                 w=[(tag, "sh", kc)], dma="cast", persist=True)
            ckeys.append((tag, "sh", kc))
        order = list(order) if order is not None else list(range(ntile))
        assert sorted(order) == list(range(ntile))
        for n_, t in enumerate(order):
            collective(GROUP4, sh[t].rearrange("i k n -> i (k n)"), g[t],
                       r=(ckeys if n_ == 0 else []) + ([(tag, "thr", n_ - 8)] if n_ >= 8 else []),
                       w=[(tag, "g", t), (tag, "thr", n_)], semkey="cc_" + tag, persist=True)

    S.op("pool", lambda e: e.memset(ident, 0.0), w=["ident0"], persist=True)
    S.op("pool", lambda e: e.affine_select(out=ident, in_=ident, compare_op=ALU.not_equal, fill=1.0,
                                           base=0, pattern=[[-1, 128]], channel_multiplier=1),
         r=["ident0"], w=["ident"], persist=True)
    S.op("pool", lambda e: e.memset(ones_bf, 1.0), w=["ones"], persist=True)
    S.op("pool", lambda e: e.memset(eps_t, EPS), w=["eps"], persist=True)

    def gather_layer(l):
        first = []
        for off, wdt in ((OFF_Q, DB), (OFF_K, DB), (OFF_VB, DB), (OFF_V, DA)):
            first += list(range(off // 512, (off + wdt) // 512))
        for t_ in range(DA // 512):
            first += [OFF_U // 512 + t_, OFF_GA // 512 + t_]
        first += list(range(OFF_GB // 512, (OFF_GB + DB) // 512))
        cast_gather(win_in, l, wsh_in[l], wg_in[l], KC, DIN, "win%d" % l, order=first)
        cast_gather(wout_in, l, wsh_out[l], wg_out[l], NCM, DM, "wout%d" % l)

    gather_layer(0)

    mark = arena.top
    pos_sb = arena.alloc(NBLK, I32)
    posf = arena.alloc(NBLK, F32)
    invf = arena.alloc(16, F32)
    ang = arena.alloc((NBLK, 16), F32)
    tq = arena.alloc((NBLK, 16), F32)
    ki = arena.alloc((NBLK, 16), I32)
    kf = arena.alloc((NBLK, 16), F32)
    rr = arena.alloc((NBLK, 16), F32)
    r2 = arena.alloc((NBLK, 16), F32)
    mk = arena.alloc((NBLK, 16), F32)
    S.op("sp", lambda e: e.dma_start(out=pos_sb, in_=pos_in[:, :]), w=["pos"], dma="ldc0")
    S.op("sp", lambda e: e.dma_start(out=invf, in_=invf_in.partition_broadcast(128)), w=["invf"], dma="ldc1")
    S.op("dve", lambda e: e.tensor_copy(out=posf, in_=pos_sb), r=["pos"], w=["posf"])
    S.op("dve", lambda e: e.tensor_tensor(out=ang, in0=posf.unsqueeze(2).broadcast_to([128, NBLK, 16]),
                                          in1=invf.unsqueeze(1).broadcast_to([128, NBLK, 16]), op=ALU.mult),
         r=["posf", "invf"], w=["ang"])
    TWO_PI = float(2 * np.pi)
    C1 = 6.28125
    C2 = float(2 * np.pi - 6.28125)
    S.op("dve", lambda e: e.tensor_scalar(out=tq, in0=ang, scalar1=float(1 / (2 * np.pi)), scalar2=None, op0=ALU.mult),
         r=["ang"], w=["tq"])
    S.op("dve", lambda e: e.tensor_copy(out=ki, in_=tq), r=["tq"], w=["ki"])
    S.op("dve", lambda e: e.tensor_copy(out=kf, in_=ki), r=["ki"], w=["kf"])
    S.op("dve", lambda e: e.scalar_tensor_tensor(out=rr, in0=kf, scalar=-C1, in1=ang, op0=ALU.mult, op1=ALU.add),
         r=["kf", "ang"], w=["rr"])
    S.op("dve", lambda e: e.scalar_tensor_tensor(out=rr, in0=kf, scalar=-C2, in1=rr, op0=ALU.mult, op1=ALU.add),
         r=["kf", "rr"], w=["rr"])

    def wrap(buf, key):
        S.op("dve", lambda e: e.tensor_single_scalar(out=mk, in_=buf, scalar=float(np.pi), op=ALU.is_gt), r=[key], w=["mk"])
        S.op("dve", lambda e: e.scalar_tensor_tensor(out=buf, in0=mk, scalar=-TWO_PI, in1=buf, op0=ALU.mult, op1=ALU.add),
             r=["mk", key], w=[key])
        S.op("dve", lambda e: e.tensor_single_scalar(out=mk, in_=buf, scalar=-float(np.pi), op=ALU.is_lt), r=[key], w=["mk"])
        S.op("dve", lambda e: e.scalar_tensor_tensor(out=buf, in0=mk, scalar=TWO_PI, in1=buf, op0=ALU.mult, op1=ALU.add),
             r=["mk", key], w=[key])

    wrap(rr, "rr")
    S.op("dve", lambda e: e.tensor_scalar(out=r2, in0=rr, scalar1=float(np.pi / 2), scalar2=None, op0=ALU.add), r=["rr"], w=["r2"])
    wrap(r2, "r2")
    S.op("act", lambda e: e.activation(out=sin_t, in_=rr, func=AF.Sin), r=["rr"], w=["sin"])
    S.op("act", lambda e: e.activation(out=cos_t, in_=r2, func=AF.Sin), r=["r2"], w=["cos"])
    S.persist["sin"] = S.last_writer["sin"]
    S.persist["cos"] = S.last_writer["cos"]
    dump("ident0", ident, ["ident"])
    dump("eps0", eps_t, ["eps"])
    dump("ones0", ones_bf, ["ones"])
    dump("cos", cos_t, ["cos"])
    dump("sin", sin_t, ["sin"])
    S.fence(skip=("pool",))
    arena.top = mark

    for l in range(DEPTH if c.stop >= 1 else 0):
        x_cur = x_in if l == 0 else xs[l - 1]
        x_next = xs[l]
        S.epoch = l + 1
        lam_init = lambda_init_fn(l)

        arena.top = PERSIST_TOP
        w_sb = arena.alloc((G, 128), F32)
        wm_b = arena.alloc((G, 128), BF16)
        bs_bc = arena.alloc((G, 128), F32)
        lamv = arena.alloc((4, 128), F32)
        lprod = arena.alloc((2, 128), F32)
        lsum = arena.alloc(2, F32)
        lexp = arena.alloc(2, F32)
        S.op("sp", lambda e, l=l: e.dma_start(out=gcol, in_=ngc_in[l]), w=["gcol"], dma="ldc0")
        S.op("sp", lambda e, l=l: e.dma_start(out=lgc, in_=lgc_in[l]), w=["lgc"], dma="ldc1")
        S.op("sp", lambda e, l=l: e.dma_start(out=lbc, in_=lbc_in[l]), w=["lbc"], dma="ldc2")
        S.op("sp", lambda e, l=l: e.dma_start(out=w_sb, in_=sguw_in[l].rearrange("g i j -> i g j")), w=["w_sb"], dma="ldc3")
        S.op("sp", lambda e, l=l: e.dma_start(out=bs_bc.rearrange("p g i -> p (g i)"), in_=sgub_in[l].partition_broadcast(128)),
             w=["bs_bc"], dma="ldc4")
        S.op("sp", lambda e, l=l: e.dma_start(out=lamv.rearrange("p a b -> p (a b)"), in_=lam_in[l].partition_broadcast(128)),
             w=["lamv"], dma="ldc5")
        S.op("sp", lambda e, l=l: e.dma_start(out=gsub, in_=subg_in[l].partition_broadcast(128)), w=["gsub"], dma="ldc6")
        S.op("dve", lambda e: e.memset(w_sb[0:64, :, 64:128], 0.0), r=["w_sb"], w=["w_sb"])
        S.op("dve", lambda e: e.tensor_copy(out=wm_b, in_=w_sb), r=["w_sb"], w=["wm_b"])
        for g in range(G):
            S.op("pe", lambda e, g=g: e.transpose(PBb[6][:, g * 128:(g + 1) * 128], wm_b[:, g, :], ident),
                 r=["wm_b", "ident"], w=[("pb6", g)])
        S.op("dve", lambda e: e.tensor_copy(out=wmT.rearrange("p g i -> p (g i)"), in_=PBb[6][:, 0:G * 128]),
             r=[("pb6", g) for g in range(G)], w=["wmT"])
        if l == 0:
            dump("w_sb", w_sb, ["w_sb"])
            dump("wm_b", wm_b, ["wm_b"])
            dump("wmT2", wmT, ["wmT"])
        wmT_f = wmT.rearrange("p g i -> p (g i)")
        npiece = (G * 128 + 511) // 512
        for pc in range(npiece):
            n = min(512, G * 128 - pc * 512)
            S.op("pe", lambda e, pc=pc, n=n: e.matmul(PB[4 + pc][:, 0:n], ones_bf, wmT_f[:, pc * 512:pc * 512 + n],
                                                      start=True, stop=True),
                 r=["wmT", "ones"], w=[("rw", pc)])
        for fc in range(NCA):
            g = fc // 2
            pc, o = (g * 128) // 512, (g * 128) % 512
            S.op("dve", lambda e, fc=fc, g=g, pc=pc, o=o: e.scalar_tensor_tensor(
                out=Cb[:, fc, :], in0=PB[4 + pc][:, o:o + 128], scalar=lbc[:, fc:fc + 1], in1=bs_bc[:, g, :],
                op0=ALU.mult, op1=ALU.add), r=[("rw", pc), "lbc", "bs_bc"], w=[("Cb", fc)])
        S.op("dve", lambda e: e.tensor_tensor(out=lprod[:, 0, :], in0=lamv[:, 0, :], in1=lamv[:, 1, :], op=ALU.mult),
             r=["lamv"], w=["lp0"])
        S.op("dve", lambda e: e.tensor_tensor(out=lprod[:, 1, :], in0=lamv[:, 2, :], in1=lamv[:, 3, :], op=ALU.mult),
             r=["lamv"], w=["lp1"])
        S.op("dve", lambda e: e.reduce_sum(out=lsum, in_=lprod, axis=AX.X), r=["lp0", "lp1"], w=["lsum"])
        S.op("act", lambda e: e.activation(out=lexp, in_=lsum, func=AF.Exp), r=["lsum"], w=["lexp"])
        S.op("dve", lambda e: e.tensor_tensor(out=neg_lam, in0=lexp[:, 1:2], in1=lexp[:, 0:1], op=ALU.subtract),
             r=["lexp"], w=["neg_lam"])
        S.op("dve", lambda e, li=lam_init: e.tensor_scalar(out=neg_lam, in0=neg_lam, scalar1=-float(li), scalar2=None, op0=ALU.add),
             r=["neg_lam"], w=["neg_lam"])
        S.op("dve", lambda e, li=lam_init: e.tensor_scalar(out=gsub, in0=gsub, scalar1=float(1.0 - li), scalar2=None, op0=ALU.mult),
             r=["gsub"], w=["gsub"])
        S.fence(skip=("pool",))

        if c.stop < 2:
            continue
        arena.top = PERSIST_TOP
        wslot = [arena.alloc((KC, 512), BF16) for _ in range(2)]
        hT = arena.alloc((KC, 512), BF16)
        gv = arena.alloc((4, DA), F32)
        gv_flat = gv.rearrange("p a b -> p (a b)")
        vln = arena.alloc((4, DA), BF16)
        vln_flat = vln.rearrange("p a b -> p (a b)")
        nxl = 2 if 2 * DM <= 4 * DA else 1
        xl = [gv_flat[:, i * DM:(i + 1) * DM] for i in range(nxl)]
        nxn = 2 if 2 * DM <= 4 * DA else 1
        xn = [vln_flat[:, i * DM:(i + 1) * DM] for i in range(nxn)]

        def gv_keys(lo, hi):
            return [("gv", b) for b in range(lo // DA, (hi - 1) // DA + 1)]

        def vln_keys(lo, hi):
            return [("vln", b) for b in range(lo // DA, (hi - 1) // DA + 1)]

        ssq = arena.alloc(4, F32)
        rstd = arena.alloc(4, F32)
        stats = arena.alloc((4, 6 * ((DA + 511) // 512)), F32)
        mv = arena.alloc((4, 2), F32)
        lrs = arena.alloc(4, F32)
        qkb = [arena.alloc(512, BF16) for _ in range(2)]
        rt = [arena.alloc((4, 64), F32) for _ in range(2)]
        qst = [arena.alloc((4, 512), BF16) for _ in range(2)]
        vst = [arena.alloc(512, BF16) for _ in range(2)]
        gu = [arena.alloc(512, BF16) for _ in range(2)]
        t1 = [arena.alloc(512, F32) for _ in range(2)]
        p1 = arena.alloc((4, 512), BF16)
        sg = [arena.alloc(512, BF16) for _ in range(2)]
        yst = [arena.alloc(512, BF16) for _ in range(2)]
        sgst = [arena.alloc(512, BF16) for _ in range(2)]

        nA, nB = DA // 512, DB // 512
        tiles = []
        for t in range(nA):
            tiles.append(("va", t, OFF_V + t * 512))
        for t in range(nB):
            tiles.append(("q", t, OFF_Q + t * 512))
        for t in range(nB):
            tiles.append(("k", t, OFF_K + t * 512))
        for t in range(nB):
            tiles.append(("vb", t, OFF_VB + t * 512))
        for t in range(nA):
            tiles.append(("u", t, OFF_U + t * 512))
            tiles.append(("ga", t, OFF_GA + t * 512))
        for t in range(nB):
            tiles.append(("gb", t, OFF_GB + t * 512))
        NT = len(tiles)

        cnt = {"acc": 0, "tr": 0, "m": 0, "z": 0}

        def hp_of(e):
            if "hp" not in hp_cache:
                hp_cache["hp"] = e.partition_id() % RANKS
            return hp_cache["hp"]

        def private_copy(dst, src_g, key_src, key_dst, sem):
            nd = len(src_g.shape)
            names = " ".join("d%d" % i for i in range(nd))
            dn = " ".join("e%d" % i for i in range(len(dst.shape)))
            S.op("sp", lambda e: e.dma_start(
                out=dst.rearrange("%s -> (%s)" % (dn, dn)).rearrange("(p f) -> p f", f=4096),
                in_=src_g[bass.ds(hp_of(e), 1)].rearrange("%s -> (%s)" % (names, names)).rearrange("(p f) -> p f", f=4096)),
                 r=(list(key_src) if isinstance(key_src, (list, tuple)) and key_src and isinstance(key_src[0], tuple) else [key_src]), w=[key_dst], dma=sem)

        tiles_a = [tl for tl in tiles if tl[0] in ("q", "k", "vb")]
        tiles_b = [tl for tl in tiles if tl[0] not in ("q", "k", "vb")]
        seq = [(st_, tl) for st_ in range(NST) for tl in tiles_a] + [(st_, tl) for st_ in range(NST) for tl in tiles_b]

        def wload_seq(gidx):
            _, (kind_, t_, c0_) = seq[gidx]
            sl_ = gidx % 2
            S.op("sp", lambda e, sl_=sl_, c0_=c0_: e.dma_start(out=wslot[sl_].rearrange("p k n -> p (k n)"), in_=wg_in[l][c0_ // 512]),
                 r=[("win%d" % l, "g", c0_ // 512)], w=[("w", sl_)], dma="w%d" % sl_)

        wload_seq(0)

        def run_supertile(st, sub_tiles, first_idx):
            tok0 = st * 512
            for j in range(4):
                xb = xl[j % nxl]
                xk = gv_keys((j % nxl) * DM, (j % nxl + 1) * DM)
                nb = xn[j % nxn]
                nk = vln_keys((j % nxn) * DM, (j % nxn + 1) * DM)
                r0 = tok0 + j * 128
                S.op("sp", lambda e, xb=xb, r0=r0: e.dma_start(out=xb, in_=x_cur[r0:r0 + 128, :]),
                     w=xk, dma="x%d" % (j % nxl))
                S.op("act", lambda e, xb=xb, nb=nb, j=j: e.activation(out=nb, in_=xb, func=AF.Square, accum_out=ssq[:, j:j + 1]),
                     r=xk, w=nk + [("ssq", j)])
                S.op("act", lambda e, j=j: e.activation(out=rstd[:, j:j + 1], in_=ssq[:, j:j + 1], func=AF.Sqrt,
                                                        scale=1.0 / DM, bias=eps_t), r=[("ssq", j), "eps"], w=[("rstd", j)])
                S.op("dve", lambda e, j=j: e.reciprocal(out=rstd[:, j:j + 1], in_=rstd[:, j:j + 1]),
                     r=[("rstd", j)], w=[("rstd", j)])
                S.op("dve", lambda e, xb=xb, nb=nb, j=j: e.tensor_scalar_mul(out=nb, in0=xb, scalar1=rstd[:, j:j + 1]),
                     r=xk + [("rstd", j)], w=nk)
                for k0 in range(0, KC, 8):
                    nk8 = min(8, KC - k0)
                    bank = 6 + (cnt["tr"] % 2)
                    cnt["tr"] += 1
                    for kk in range(nk8):
                        S.op("pe", lambda e, nb=nb, bank=bank, kk=kk, k0=k0: e.transpose(
                            PBb[bank][:, kk * 128:(kk + 1) * 128], nb[:, (k0 + kk) * 128:(k0 + kk + 1) * 128], ident),
                             r=nk + ["ident"], w=[("pb", bank, kk)])
                    S.op("dve", lambda e, bank=bank, k0=k0, nk8=nk8, j=j: e.tensor_tensor(
                        out=hT[:, k0:k0 + nk8, j * 128:(j + 1) * 128],
                        in0=PBb[bank][:, 0:nk8 * 128].rearrange("p (a b) -> p a b", a=nk8),
                        in1=gcol[:, k0:k0 + nk8].unsqueeze(2).broadcast_to([128, nk8, 128]), op=ALU.mult),
                         r=[("pb", bank, kk) for kk in range(nk8)] + ["gcol"], w=[("hT", j)])

            hT_all = [("hT", j) for j in range(4)]
            if st == 0 and l == 0 and first_idx == 0:
                dump("hT", hT, hT_all)
                dump("rstd", rstd, [("rstd", j) for j in range(4)])
                dump("gcol", gcol, ["gcol"])
                dump("w0", wslot[0], [("w", 0)])
                dump("wmT", wmT, ["wmT"])
                dump("ident", ident, ["ident"])
                dump("eps", eps_t, ["eps"])
                dump("ones", ones_bf, ["ones"])
                dump("ssq", ssq, [("ssq", j) for j in range(4)])
                dump("x3", xl[3 % nxl], gv_keys(0, 4 * DA))
                dump("xn3", xn[3 % nxn], vln_keys(0, 4 * DA))
                dump("Cb", Cb, [("Cb", fc) for fc in range(NCA)])
            pend = []

            def flush_pend():
                for f in pend:
                    f()
                del pend[:]

            for k_, (kind, t, c0) in enumerate(sub_tiles):
                gidx = first_idx + k_
                sl = gidx % 2
                W = wslot[sl]
                if gidx + 1 < len(seq):
                    wload_seq(gidx + 1)
                if kind in ("va", "q", "k", "vb"):
                    for j in range(4):
                        bank = cnt["acc"] % 4
                        cnt["acc"] += 1
                        for kc in range(KC):
                            S.op("pe", lambda e, bank=bank, kc=kc, j=j, W=W: e.matmul(
                                PB[bank][:, 0:512], hT[:, kc, j * 128:(j + 1) * 128], W[:, kc, :],
                                start=(kc == 0), stop=(kc == KC - 1)),
                                 r=[("hT", j), ("w", sl)], w=[("pb", bank)])
                        flush_pend()
                        if kind == "va":
                            S.op("act", lambda e, bank=bank, j=j, t=t: e.activation(
                                out=gv[:, j, t * 512:(t + 1) * 512], in_=PB[bank][:, 0:512], func=AF.Gelu),
                                 r=[("pb", bank)], w=[("gv", j)])
                        elif kind == "vb":
                            zi = cnt["z"] % 2
                            cnt["z"] += 1
                            S.op("act", lambda e, bank=bank, zi=zi: e.activation(out=vst[zi], in_=PB[bank][:, 0:512], func=AF.Copy),
                                 r=[("pb", bank)], w=[("vst", zi)])
                            r0 = tok0 + j * 128
                            S.op("sp", lambda e, zi=zi, r0=r0, t=t: e.dma_start(
                                out=v_send[2 * t:2 * t + 2, r0:r0 + 128, :].rearrange("h p e -> p h e"),
                                in_=vst[zi].rearrange("p (h e) -> p h e", h=2)),
                                 r=[("vst", zi)], w=[("v_send", st, j, t)], dma="vst%d" % zi)
                        else:
                            zi = cnt["z"] % 2
                            cnt["z"] += 1
                            z4 = PB[bank][:, 0:512].rearrange("p (a b) -> p a b", a=4)
                            o4 = qkb[zi].rearrange("p (a b) -> p a b", a=4)
                            blk = st * 4 + j
                            cb = cos_t[:, blk, :].unsqueeze(1).broadcast_to([128, 4, 16])
                            sb_ = sin_t[:, blk, :].unsqueeze(1).broadcast_to([128, 4, 16])
                            T = rt[zi]
                            kb, kr, ko = [("pb", bank)], [("rt", zi)], [("qkb", zi)]
                            S.op("dve", lambda e, z4=z4, T=T, cb=cb: e.tensor_tensor(out=T[:, :, 0:16], in0=z4[:, :, 0:16], in1=cb, op=ALU.mult), r=kb, w=kr)
                            S.op("dve", lambda e, z4=z4, T=T, sb_=sb_: e.tensor_tensor(out=T[:, :, 16:32], in0=z4[:, :, 16:32], in1=sb_, op=ALU.mult), r=kb, w=kr)
                            S.op("dve", lambda e, z4=z4, T=T, cb=cb: e.tensor_tensor(out=T[:, :, 32:48], in0=z4[:, :, 16:32], in1=cb, op=ALU.mult), r=kb, w=kr)
                            S.op("dve", lambda e, z4=z4, T=T, sb_=sb_: e.tensor_tensor(out=T[:, :, 48:64], in0=z4[:, :, 0:16], in1=sb_, op=ALU.mult), r=kb, w=kr)
                            S.op("dve", lambda e, o4=o4, T=T: e.tensor_tensor(out=o4[:, :, 0:16], in0=T[:, :, 0:16], in1=T[:, :, 16:32], op=ALU.subtract), r=kr, w=ko)
                            S.op("dve", lambda e, o4=o4, T=T: e.tensor_tensor(out=o4[:, :, 16:32], in0=T[:, :, 32:48], in1=T[:, :, 48:64], op=ALU.add), r=kr, w=ko)
                            S.op("act", lambda e, o4=o4, z4=z4: e.activation(out=o4[:, :, 32:128], in_=z4[:, :, 32:128], func=AF.Copy), r=kb, w=ko)
                            qi = (0 if kind == "q" else 1)
                            dst = qT_send if kind == "q" else kT_send

                            def do_tr(zi=zi, j=j, qi=qi, t=t, dst=dst, last=(j == 3)):
                                bank2 = 6 + (cnt["tr"] % 2)
                                cnt["tr"] += 1
                                for a in range(4):
                                    S.op("pe", lambda e, a=a, zi=zi, bank2=bank2: e.transpose(
                                        PBb[bank2][:, a * 128:(a + 1) * 128], qkb[zi][:, a * 128:(a + 1) * 128], ident),
                                         r=[("qkb", zi), "ident"], w=[("pb", bank2, a)])
                                S.op("act", lambda e, bank2=bank2, qi=qi, j=j: e.activation(
                                    out=qst[qi][:, :, j * 128:(j + 1) * 128],
                                    in_=PBb[bank2][:, 0:512].rearrange("p (a b) -> p a b", a=4), func=AF.Copy),
                                     r=[("pb", bank2, a) for a in range(4)], w=[("qst", qi, j)])
                                if last:
                                    S.op("sp", lambda e, qi=qi, t=t, dst=dst: e.dma_start(
                                        out=dst[t * 512:(t + 1) * 512, tok0:tok0 + 512].rearrange("(a d) n -> d a n", d=128),
                                        in_=qst[qi]), r=[("qst", qi, jj) for jj in range(4)],
                                         w=[("qk_send", qi, st, t)] + [("qst", qi, jj) for jj in range(4)], dma="qst%d" % qi)
                            pend.append(do_tr)
                    if kind == "va" and t == nA - 1:
                        nch = (DA + 511) // 512
                        for j in range(4):
                            for ch in range(nch):
                                n = min(512, DA - ch * 512)
                                S.op("dve", lambda e, j=j, ch=ch, n=n: e.bn_stats(out=stats[:, j, 6 * ch:6 * ch + 6],
                                                                                  in_=gv[:, j, ch * 512:ch * 512 + n]),
                                     r=[("gv", j)], w=[("stats", j, ch)])
                            S.op("dve", lambda e, j=j: e.bn_aggr(out=mv[:, j, :], in_=stats[:, j, :]),
                                 r=[("stats", j, ch) for ch in range(nch)], w=[("mv", j)])
                            S.op("act", lambda e, j=j: e.activation(out=lrs[:, j:j + 1], in_=mv[:, j, 1:2], func=AF.Sqrt, bias=eps_t),
                                 r=[("mv", j), "eps"], w=[("lrs", j)])
                            S.op("dve", lambda e, j=j: e.reciprocal(out=lrs[:, j:j + 1], in_=lrs[:, j:j + 1]), r=[("lrs", j)], w=[("lrs", j)])
                            S.op("dve", lambda e, j=j: e.tensor_scalar(out=vln[:, j, :], in0=gv[:, j, :], scalar1=mv[:, j, 0:1],
                                                                       scalar2=lrs[:, j:j + 1], op0=ALU.subtract, op1=ALU.mult),
                                 r=[("gv", j), ("mv", j), ("lrs", j)], w=[("vln", j)])
                else:
                    for fcl in range(4):
                        fc = t * 4 + fcl
                        if kind == "u":
                            g = fc // 2
                            mb = 4 + (cnt["m"] % 2)
                            cnt["m"] += 1
                            for j in range(4):
                                S.op("pe", lambda e, mb=mb, j=j, fc=fc, g=g: e.matmul(
                                    PB[mb][:, j * 128:(j + 1) * 128], vln[:, j, fc * 128:(fc + 1) * 128], wmT[:, g, :],
                                    start=True, stop=True), r=[("vln", j), "wmT"], w=[("pbm", mb, j)])
                        bank = cnt["acc"] % 4
                        cnt["acc"] += 1
                        for kc in range(KC):
                            S.op("pe", lambda e, bank=bank, kc=kc, fcl=fcl, W=W: e.matmul(
                                PB[bank][:, 0:512], W[:, kc, fcl * 128:(fcl + 1) * 128], hT[:, kc, :],
                                start=(kc == 0), stop=(kc == KC - 1)), r=hT_all + [("w", sl)], w=[("pb", bank)])
                        flush_pend()
                        zi = cnt["z"] % 2
                        cnt["z"] += 1
                        if kind == "u":
                            S.op("act", lambda e, bank=bank, zi=zi: e.activation(out=gu[zi], in_=PB[bank][:, 0:512], func=AF.Gelu),
                                 r=[("pb", bank)], w=[("gu", zi)])
                            S.op("dve", lambda e, mb=mb, zi=zi, fc=fc: e.scalar_tensor_tensor(
                                out=t1[zi].rearrange("p (a b) -> p a b", a=4),
                                in0=PB[mb][:, 0:512].rearrange("p (a b) -> p a b", a=4), scalar=lgc[:, fc:fc + 1],
                                in1=Cb[:, fc, :].unsqueeze(1).broadcast_to([128, 4, 128]), op0=ALU.mult, op1=ALU.add),
                                 r=[("pbm", mb, j) for j in range(4)] + ["lgc", ("Cb", fc)], w=[("t1", zi)])
                            S.op("dve", lambda e, zi=zi, fcl=fcl: e.tensor_tensor(out=p1[:, fcl, :], in0=t1[zi], in1=gu[zi], op=ALU.mult),
                                 r=[("t1", zi), ("gu", zi)], w=[("p1", fcl)])
                        elif kind == "ga":
                            S.op("act", lambda e, bank=bank, zi=zi: e.activation(out=sg[zi], in_=PB[bank][:, 0:512], func=AF.Silu),
                                 r=[("pb", bank)], w=[("sg", zi)])
                            S.op("dve", lambda e, zi=zi, fcl=fcl: e.tensor_tensor(out=yst[zi], in0=p1[:, fcl, :], in1=sg[zi], op=ALU.mult),
                                 r=[("p1", fcl), ("sg", zi)], w=[("yst", zi)])
                            S.op("sp", lambda e, zi=zi, fc=fc: e.dma_start(out=yaT_sp[fc * 128:(fc + 1) * 128, tok0:tok0 + 512], in_=yst[zi]),
                                 r=[("yst", zi)], w=[("yaT_sp", st, fc)], dma="yst%d" % zi)
                        else:
                            S.op("act", lambda e, bank=bank, zi=zi: e.activation(out=sgst[zi], in_=PB[bank][:, 0:512], func=AF.Silu),
                                 r=[("pb", bank)], w=[("sgst", zi)])
                            S.op("sp", lambda e, zi=zi, fc=fc: e.dma_start(out=sgbT_sp[fc * 128:(fc + 1) * 128, tok0:tok0 + 512], in_=sgst[zi]),
                                 r=[("sgst", zi)], w=[("sgbT_sp", st, fc)], dma="sgst%d" % zi)
            flush_pend()

        g0 = 0
        for st_ in range(NST):
            run_supertile(st_, tiles_a, g0)
            g0 += len(tiles_a)
        kkeys = [("qk_send", 1, st_, t_) for st_ in range(NST) for t_ in range(nB)]
        qkeys = [("qk_send", 0, st_, t_) for st_ in range(NST) for t_ in range(nB)]
        vkeys = [("v_send", st_, j_, t_) for st_ in range(NST) for j_ in range(4) for t_ in range(nB)]
        for h in range(H):
            collective(GROUP4, kT_send[h * 256:(h + 1) * 256, :], kT_g[h].rearrange("r a t -> (r a) t"), r=kkeys, w=[("kT_g", h)], semkey="cck")
        for h in range(H):
            collective(GROUP4, v_send[h], v_g[h], r=vkeys, w=[("v_g", h)], semkey="ccv")
        for h in range(H):
            collective(GROUP4, qT_send[h * 256:(h + 1) * 256, :], qT_g[h].rearrange("r a t -> (r a) t"), r=qkeys, w=[("qT_g", h)], semkey="ccq")
        for st_ in range(NST):
            if st_ == NST - 1:
                private_copy(kT_my, kT_g.rearrange("(a b) r c t -> a b r c t", b=HPC), [("kT_g", h_) for h_ in range(H)], "kT_my", "pck")
                private_copy(v_my, v_g.rearrange("(a b) n e -> a b n e", b=HPC), [("v_g", h_) for h_ in range(H)], "v_my", "pcv")
                private_copy(qT_my, qT_g.rearrange("(a b) r c t -> a b r c t", b=HPC), [("qT_g", h_) for h_ in range(H)], "qT_my", "pcq")
            run_supertile(st_, tiles_b, g0)
            g0 += len(tiles_b)
        S.fence()
        if c.stop < 3:
            continue

        if c.stop < 4:
            continue
        arena.top = PERSIST_TOP
        kT = arena.alloc((HC, S_), BF16)
        Vsb = arena.alloc((NKT, HPC, 257), BF16)
        qT = [arena.alloc((HC, 512), BF16) for _ in range(2)]
        PT = [arena.alloc(512, BF16) for _ in range(4)]
        PTd = [arena.alloc(512, BF16) for _ in range(4)]
        o1 = arena.alloc((4, 256), F32)
        oo = arena.alloc((4, 256), F32)
        junk = arena.alloc(256, BF16)
        rl = arena.alloc(8, F32)
        oss = arena.alloc(4, F32)
        ors = arena.alloc(4, F32)
        onb = arena.alloc((4, 256), BF16)
        onst = [arena.alloc((HC, 512), BF16) for _ in range(2)]

        for hl in range(HPC):
            for r in range(RANKS):
                S.op("sp", lambda e, r=r, hl=hl: e.dma_start(
                    out=kT[:, hl * 2:hl * 2 + 2, r * TPC:(r + 1) * TPC],
                    in_=kT_my[hl, r].rearrange("(c d) t -> d c t", d=128)),
                     r=["kT_my"], w=[("kT", r, hl)], dma="ldk%d" % hl)
        for hl in range(HPC):
            S.op("sp", lambda e, hl=hl: e.dma_start(
                out=Vsb[:, :, hl, 0:256], in_=v_my[hl].rearrange("(kt p) e -> p kt e", p=128)),
                 r=["v_my"], w=[("V", hl)], dma="ldv%d" % hl)
        S.op("pool", lambda e: e.memset(Vsb[:, :, :, 256:257], 1.0), w=["Vone"])
        for kb in range(4):
            S.op("pool", lambda e, kb=kb: e.memset(PTd[kb], 0.0), w=[("PTd", kb)])
        if l + 1 < DEPTH:
            gather_layer(l + 1)
        scale = float(128 ** -0.5)
        ptc = 0
        sbank = 0
        deferred = []
        for Gq in range(NQG):
            rq, oq = Gq // NST, (Gq % NST) * 512
            qb_ = qT[Gq % 2]
            for hl in range(HPC):
                S.op("sp", lambda e, qb_=qb_, rq=rq, oq=oq, hl=hl: e.dma_start(
                    out=qb_[:, hl * 2:hl * 2 + 2, :],
                    in_=qT_my[hl, rq, :, oq:oq + 512].rearrange("(c d) t -> d c t", d=128)),
                     r=["qT_my"], w=[("qT", Gq % 2, hl)], dma="ldq%d_%d" % (Gq % 2, hl))
            ntile = 4 * Gq + 4
            ost = onst[Gq % 2]
            for hl in range(HPC):
                for cpt in range(2):
                    hc = hl * 2 + cpt
                    prev = None
                    for t in range(ntile + 1):
                        if t < ntile:
                            kb = t - 4 * Gq
                            c_lo = max(kb, 0) * 128
                            sb = 4 + (sbank % 2)
                            sbank += 1
                            S.op("pe", lambda e, sb=sb, hc=hc, t=t, c_lo=c_lo, qb_=qb_: e.matmul(
                                PB[sb][:, c_lo:512], kT[:, hc, t * 128:(t + 1) * 128], qb_[:, hc, c_lo:512],
                                start=True, stop=True), r=[("kT", rr_, hl) for rr_ in range(RANKS)] + [("qT", Gq % 2, hl)], w=[("pbs", sb)])
                            if kb >= 0:
                                P = PTd[kb]
                                pk = ("PTd", kb)
                                S.op("act", lambda e, P=P, sb=sb, c_lo=c_lo: e.activation(
                                    out=P[:, c_lo + 64:512], in_=PB[sb][:, c_lo + 64:512], func=AF.Exp, scale=scale),
                                     r=[("pbs", sb)], w=[pk])
                                S.op("act", lambda e, P=P, sb=sb, c_lo=c_lo: e.activation(
                                    out=P[0:64, c_lo:c_lo + 64], in_=PB[sb][0:64, c_lo:c_lo + 64], func=AF.Exp, scale=scale),
                                     r=[("pbs", sb)], w=[pk])
                            else:
                                P = PT[ptc % 4]
                                pk = ("PT", ptc % 4)
                                ptc += 1
                                S.op("act", lambda e, P=P, sb=sb: e.activation(out=P, in_=PB[sb][:, 0:512], func=AF.Exp, scale=scale),
                                     r=[("pbs", sb)], w=[pk])
                            cur = (t, max(kb, 0), P, pk)
                        else:
                            cur = None
                        if prev is not None:
                            pt_, qlo, P_, pk_ = prev
                            for qb in range(qlo, 4):
                                last_t = 4 * Gq + qb
                                S.op("pe", lambda e, qb=qb, P_=P_, pt_=pt_, hl=hl, last_t=last_t: e.matmul(
                                    PB[qb][:, 0:257], P_[:, qb * 128:(qb + 1) * 128], Vsb[:, pt_, hl, :],
                                    start=(pt_ == 0), stop=(pt_ == last_t)),
                                     r=[pk_, ("V", hl), "Vone"], w=[("pbo", qb)])
                        prev = cur
                        if t == 1 and deferred:
                            for f in deferred:
                                f()
                            del deferred[:]
                    for qb in range(4):
                        if cpt == 0:
                            S.op("dve", lambda e, qb=qb: e.reciprocal(out=rl[:, qb:qb + 1], in_=PB[qb][:, 256:257]),
                                 r=[("pbo", qb)], w=[("rl", qb)])
                            S.op("dve", lambda e, qb=qb: e.tensor_scalar_mul(out=o1[:, qb, :], in0=PB[qb][:, 0:256], scalar1=rl[:, qb:qb + 1]),
                                 r=[("pbo", qb), ("rl", qb)], w=[("o1", qb)])
                        else:
                            S.op("dve", lambda e, qb=qb: e.reciprocal(out=rl[:, 4 + qb:5 + qb], in_=PB[qb][:, 256:257]),
                                 r=[("pbo", qb)], w=[("rl2", qb)])
                            S.op("dve", lambda e, qb=qb: e.tensor_tensor(out=rl[:, 4 + qb:5 + qb], in0=rl[:, 4 + qb:5 + qb], in1=neg_lam, op=ALU.mult),
                                 r=[("rl2", qb), "neg_lam"], w=[("rl2", qb)])
                            S.op("dve", lambda e, qb=qb: e.scalar_tensor_tensor(
                                out=oo[:, qb, :], in0=PB[qb][:, 0:256], scalar=rl[:, 4 + qb:5 + qb], in1=o1[:, qb, :],
                                op0=ALU.mult, op1=ALU.add), r=[("pbo", qb), ("rl2", qb), ("o1", qb)], w=[("oo", qb)])
                            S.op("act", lambda e, qb=qb: e.activation(out=junk, in_=oo[:, qb, :], func=AF.Square, accum_out=oss[:, qb:qb + 1]),
                                 r=[("oo", qb)], w=["junk", ("oss", qb)])
                            S.op("act", lambda e, qb=qb: e.activation(out=ors[:, qb:qb + 1], in_=oss[:, qb:qb + 1], func=AF.Ln,
                                                                      scale=1.0 / 256, bias=eps_t), r=[("oss", qb), "eps"], w=[("ors", qb)])
                            S.op("act", lambda e, qb=qb: e.activation(out=ors[:, qb:qb + 1], in_=ors[:, qb:qb + 1], func=AF.Exp, scale=-0.5),
                                 r=[("ors", qb)], w=[("ors", qb)])
                            S.op("dve", lambda e, qb=qb: e.scalar_tensor_tensor(
                                out=onb[:, qb, :], in0=oo[:, qb, :], scalar=ors[:, qb:qb + 1], in1=gsub, op0=ALU.mult, op1=ALU.mult),
                                 r=[("oo", qb), ("ors", qb), "gsub"], w=[("onb", qb)])
                    if cpt == 1:
                        def make_deferred(Gq=Gq, hl=hl, ost=ost):
                            def f():
                                for qb in range(4):
                                    for half in range(2):
                                        S.op("pe", lambda e, qb=qb, half=half: e.transpose(
                                            PBb[6][:, (qb * 2 + half) * 128:(qb * 2 + half + 1) * 128],
                                            onb[:, qb, half * 128:(half + 1) * 128], ident),
                                             r=[("onb", qb), "ident"], w=[("pbt", qb, half)])
                                for half in range(2):
                                    S.op("dve", lambda e, half=half: e.tensor_copy(
                                        out=ost[:, hl * 2 + half, :].rearrange("p (a b) -> p a b", a=4),
                                        in_=PBb[6][:, 0:1024].rearrange("p (a h b) -> p a h b", a=4, h=2)[:, :, half, :]),
                                         r=[("pbt", qb, half) for qb in range(4)], w=[("onst", Gq % 2, hl, half)])
                                if hl == HPC - 1:
                                    okeys = [("onst", Gq % 2, h_, f_) for h_ in range(HPC) for f_ in range(2)]
                                    S.op("sp", lambda e: e.dma_start(
                                        out=onT_send[Gq // NST, :, (Gq % NST) * 512:(Gq % NST + 1) * 512].rearrange("(a d) n -> d a n", d=128), in_=ost),
                                         r=okeys, w=[("onT_send", Gq)] + okeys, dma="onst%d" % (Gq % 2))
                            return f
                        deferred.append(make_deferred())
        for f in deferred:
            f()
        del deferred[:]
        S.fence()
        if c.stop < 5:
            continue
        for tq_ in range(RANKS):
            for hl in range(HPC):
                collective(GROUP4, onT_send[tq_, hl * 256:(hl + 1) * 256, :], onT_g[tq_, hl].rearrange("r a t -> (r a) t"),
                           r=[], w=["onT_g"], semkey="ccon")
        S.fence()
        if c.stop < 6:
            continue

        private_copy(on_my, onT_g, "onT_g", "on_my", "pco")
        arena.top = PERSIST_TOP
        wos = [arena.alloc((NCM, 512), BF16) for _ in range(2)]
        yT = arena.alloc((NCM, 512), BF16)
        sgT = arena.alloc((NCB, 512), BF16)
        xres = [arena.alloc(512, F32) for _ in range(4)]
        xo = [arena.alloc(512, F32) for _ in range(2)]
        ND = DM // 512
        xc = 0
        acc = 0
        def wo_load(gi_):
            sl_, dg_ = gi_ % 2, gi_ % ND
            S.op("sp", lambda e, sl_=sl_, dg_=dg_: e.dma_start(out=wos[sl_].rearrange("p k n -> p (k n)"), in_=wg_out[l][dg_]),
                 r=[("wout%d" % l, "g", dg_)], w=[("wo", sl_)], dma="wo%d" % sl_)

        for st in range(NST):
            tok0 = st * 512
            S.op("sp", lambda e, tok0=tok0: e.dma_start(
                out=yT[:, 0:NCA, :], in_=yaT_sp[:, tok0:tok0 + 512].rearrange("(a d) n -> d a n", d=128)),
                 w=["yaT"], dma="ldya")
            for hl in range(HPC):
                for r in range(RANKS):
                    c0 = NCA + (r * HPC + hl) * 2
                    S.op("sp", lambda e, tok0=tok0, hl=hl, r=r, c0=c0: e.dma_start(
                        out=yT[:, c0:c0 + 2, :],
                        in_=on_my[hl, r, :, tok0:tok0 + 512].rearrange("(c d) t -> d c t", d=128)),
                         r=["on_my"], w=[("onT", c0 - NCA), ("onT", c0 - NCA + 1), ("ybT", c0 - NCA), ("ybT", c0 - NCA + 1)], dma="ldon%d" % (r * HPC + hl))
            S.op("sp", lambda e, tok0=tok0: e.dma_start(
                out=sgT, in_=sgbT_sp[:, tok0:tok0 + 512].rearrange("(a d) n -> d a n", d=128)),
                 w=["sgT"], dma="ldsg")
            for cb in range(NCB):
                eng = "dve"
                S.op(eng, lambda e, cb=cb: e.tensor_tensor(out=yT[:, NCA + cb, :], in0=yT[:, NCA + cb, :], in1=sgT[:, cb, :], op=ALU.mult),
                     r=[("onT", cb), "sgT"], w=[("ybT", cb)])
            ykeys = ["yaT"] + [("ybT", cb) for cb in range(NCB)]
            for dg in range(ND):
                gi_ = st * ND + dg
                sl = gi_ % 2
                if gi_ == 0:
                    wo_load(0)
                if gi_ + 1 < NST * ND:
                    wo_load(gi_ + 1)
                for j in range(4):
                    r0 = tok0 + j * 128
                    xi = xc % 4
                    xoi = xc % 2
                    xc += 1
                    S.op("sp", lambda e, xi=xi, r0=r0, dg=dg: e.dma_start(out=xres[xi], in_=x_cur[r0:r0 + 128, dg * 512:(dg + 1) * 512]),
                         w=[("xres", xi)], dma="xres%d" % xi)
                    bank = acc % 4
                    acc += 1
                    for kc in range(NCM):
                        S.op("pe", lambda e, bank=bank, kc=kc, j=j, sl=sl: e.matmul(
                            PB[bank][:, 0:512], yT[:, kc, j * 128:(j + 1) * 128], wos[sl][:, kc, :],
                            start=(kc == 0), stop=(kc == NCM - 1)), r=ykeys + [("wo", sl)], w=[("pb", bank)])
                    S.op("dve", lambda e, bank=bank, xi=xi, xoi=xoi: e.tensor_tensor(out=xo[xoi], in0=PB[bank][:, 0:512], in1=xres[xi], op=ALU.add),
                         r=[("pb", bank), ("xres", xi)], w=[("xo", xoi)])
                    S.op("sp", lambda e, xoi=xoi, r0=r0, dg=dg: e.dma_start(out=x_next[r0:r0 + 128, dg * 512:(dg + 1) * 512], in_=xo[xoi]),
                         r=[("xo", xoi)], w=[("x_next", st, j, dg)], dma="xo%d" % xoi)
        S.fence()

    arena.top = PERSIST_TOP
    fg = arena.alloc(DM, F32)
    NFB = 4
    xf = [arena.alloc(DM, F32) for _ in range(NFB)]
    yf = [arena.alloc(DM, F32) for _ in range(NFB)]
    fss = arena.alloc(NFB, F32)
    frs = arena.alloc(NFB, F32)
    S.op("sp", lambda e: e.dma_start(out=fg, in_=fing_in.partition_broadcast(128)), w=["fg"], dma="ldc0")
    x_last = xs[DEPTH - 1]
    for blk in range(NBLK):
        i = blk % NFB
        r0 = blk * 128
        S.op("sp", lambda e, i=i, r0=r0: e.dma_start(out=xf[i], in_=x_last[r0:r0 + 128, :]), w=[("xf", i)], dma="xf%d" % i)
        S.op("act", lambda e, i=i: e.activation(out=yf[i], in_=xf[i], func=AF.Square, accum_out=fss[:, i:i + 1]),
             r=[("xf", i)], w=[("yf", i), ("fss", i)])
        S.op("act", lambda e, i=i: e.activation(out=frs[:, i:i + 1], in_=fss[:, i:i + 1], func=AF.Sqrt, scale=1.0 / DM, bias=eps_t),
             r=[("fss", i), "eps"], w=[("frs", i)])
        S.op("dve", lambda e, i=i: e.reciprocal(out=frs[:, i:i + 1], in_=frs[:, i:i + 1]), r=[("frs", i)], w=[("frs", i)])
        S.op("dve", lambda e, i=i: e.scalar_tensor_tensor(out=yf[i], in0=xf[i], scalar=frs[:, i:i + 1], in1=fg, op0=ALU.mult, op1=ALU.mult),
             r=[("xf", i), ("frs", i), "fg"], w=[("yf", i)])
        S.op("sp", lambda e, i=i, r0=r0: e.dma_start(out=out_d[r0:r0 + 128, :], in_=yf[i]), r=[("yf", i)], w=[("out", blk)], dma="yf%d" % i)
    if c.debug:
        S.fence()
        for nm, src in (("qT", qT_send), ("kT", kT_send), ("v", v_send), ("on", onT_send), ("ya", yaT_sp),
                        ("sgb", sgbT_sp), ("x0", xs[0]), ("ong", onT_g)):
            if isinstance(c.debug, (list, tuple)) and nm not in c.debug:
                continue
            nd = len(src.shape)
            names = " ".join("d%d" % i for i in range(nd))
            tot = 1
            for v_ in src.shape:
                tot *= v_
            flat = lambda ap_, names=names: ap_.rearrange("%s -> (%s)" % (names, names)).rearrange("(p f) -> p f", p=128)
            dd = nc.dram_tensor("dbg_" + nm, [128, tot // 128], src.dtype, kind="ExternalOutput").ap()
            S.op("pool", lambda e, dd=dd, src=src, flat=flat: e.dma_start(out=dd, in_=flat(src)),
                 w=[("dbg", nm)], dma="dbg_" + nm)
    S.fence(final=True)

    keys = S.finalize()
    with nc.cleanup_on_exit():
        sems = {k: nc.alloc_semaphore(name="s%d" % i) for i, k in enumerate(keys)}
        for k in keys:
            nc.gpsimd.sem_clear(sems[k])
        nc.all_engine_barrier()
        with nc.Block() as block:
            S.emit(block, sems)
        nc.all_engine_barrier()
    for cx in reversed(ctxs):
        cx.__exit__(None, None, None)
    return nc, len(S.ops), len(keys)


def make_in_maps(cfg, x, positions, norm_g, w_in, ln_g, ln_b, sgu_w, sgu_b,
                 lam_q1, lam_k1, lam_q2, lam_k2, subln_g, w_out, final_g):
    c = cfg
    f = np.float32
    x = np.asarray(x, f)
    positions = np.asarray(positions, np.int32)
    DEPTH = c.DEPTH
    invf = (np.float32(ROPE_THETA) ** (-np.arange(0, 32, 2, dtype=np.float32) / np.float32(32))).astype(f)[None, :]
    ngc = np.ascontiguousarray(np.asarray(norm_g, f).reshape(DEPTH, c.KC, 128).transpose(0, 2, 1))
    lgc = np.ascontiguousarray(np.asarray(ln_g, f).reshape(DEPTH, c.NCA, 128).transpose(0, 2, 1))
    lbc = np.ascontiguousarray(np.asarray(ln_b, f).reshape(DEPTH, c.NCA, 128).transpose(0, 2, 1))
    sguw = np.ascontiguousarray(np.asarray(sgu_w, f))
    sgub = np.ascontiguousarray(np.asarray(sgu_b, f).reshape(DEPTH, 1, c.G * 128))
    lam = np.ascontiguousarray(np.concatenate([np.asarray(a, f) for a in (lam_q1, lam_k1, lam_q2, lam_k2)], axis=1)
                               .reshape(DEPTH, 1, 4 * 128))
    subg = np.ascontiguousarray(np.asarray(subln_g, f).reshape(DEPTH, 1, 256))
    fing = np.ascontiguousarray(np.asarray(final_g, f).reshape(1, c.DM))
    w_in = np.asarray(w_in, f)
    w_out = np.asarray(w_out, f)
    maps = []
    ri, ro = c.DM // RANKS, c.DMIX // RANKS
    for core in range(NCORES):
        b, r = core // RANKS, core % RANKS
        t0 = r * c.TPC
        maps.append({
            "x": np.ascontiguousarray(x[b, t0:t0 + c.TPC, :]),
            "pos_t": np.ascontiguousarray(positions[b, t0:t0 + c.TPC].reshape(c.NBLK, 128).T),
            "invf": invf, "norm_gc": ngc, "ln_gc": lgc, "ln_bc": lbc, "sgu_w": sguw, "sgu_b": sgub,
            "lam": lam, "subln_g": subg, "final_g": fing,
            "w_in_sh": np.ascontiguousarray(w_in.reshape(DEPTH, c.KC, RANKS, 32, c.DIN)[:, :, r].reshape(DEPTH, c.KC * 32, c.DIN)),
            "w_out_sh": np.ascontiguousarray(w_out.reshape(DEPTH, c.NCM, RANKS, 32, c.DM)[:, :, r].reshape(DEPTH, c.NCM * 32, c.DM)),
        })
    return maps


_CACHE = {}


def run(cfg, inputs):
    key = (cfg.S, cfg.DM, cfg.G, cfg.H, cfg.DEPTH, cfg.stop)
    if key not in _CACHE:
        _CACHE[key] = build_program(cfg)[0]
    nc = _CACHE[key]
    maps = make_in_maps(cfg, **inputs)
    res = run_bass_kernel_spmd(nc, maps, core_ids=list(range(NCORES)))
    out = np.empty((2, cfg.S, cfg.DM), np.float32)
    for core in range(NCORES):
        b, r = core // RANKS, core % RANKS
        out[b, r * cfg.TPC:(r + 1) * cfg.TPC, :] = res.results[core]["out"]
    if cfg.debug:
        return out, res.results
    return out


def kernel(**inputs):
    return run(Cfg(), inputs)
```

```python
import math
import os
import numpy as np
import concourse.bass as bass
import concourse.mybir as mybir
from concourse.bass_utils import run_bass_kernel_spmd

F32 = mybir.dt.float32
BF16 = mybir.dt.bfloat16
I32 = mybir.dt.int32
AF = mybir.ActivationFunctionType
ALU = mybir.AluOpType
AX = mybir.AxisListType

EPS = 1e-5
ROPE_THETA = 500000.0
NCORES = 8
RANKS = 4


class Cfg:
    def __init__(self, S=8192, DM=4096, G=8, H=8, DEPTH=2, debug=False, stop=99):
        self.S, self.DM, self.G, self.H, self.DEPTH = S, DM, G, H, DEPTH
        self.debug = debug
        self.stop = stop
        self.DA = G * 256
        self.DB = H * 256
        self.DMIX = self.DA + self.DB
        self.DIN = 3 * self.DA + 4 * self.DB
        self.TPC = S // RANKS
        self.NST = self.TPC // 512
        self.NBLK = self.TPC // 128
        self.KC = DM // 128
        self.HPC = H // RANKS
        self.NCA = self.DA // 128
        self.NCB = self.DB // 128
        self.NCM = self.DMIX // 128


import types


def _freeze(fn):
    if fn.__closure__ is None:
        return fn
    cells = []
    for cl in fn.__closure__:
        try:
            cells.append(types.CellType(cl.cell_contents))
        except ValueError:
            cells.append(cl)
    return types.FunctionType(fn.__code__, fn.__globals__, fn.__name__, fn.__defaults__, tuple(cells))


class _Op:
    __slots__ = ("eng", "fn", "deps", "kind", "semkey", "count", "signalled", "inc", "idx")

    def __init__(self, eng, fn, kind, semkey, inc):
        self.eng, self.fn, self.kind, self.semkey, self.inc = eng, fn, kind, semkey, inc
        self.deps = []
        self.count = None
        self.signalled = False


class Sched:
    ENGS = ("pe", "act", "dve", "pool", "sp")

    def __init__(self):
        self.ops = []
        self.last_writer = {}
        self.readers = {}
        self.epoch = 0
        self.last_eng_op = {}
        self.async_since_fence = []
        self.ncc = 0
        self.persist = {}
        self.persist_ops = []

    def op(self, eng, fn, r=(), w=(), dma=None, inc=None, persist=False):
        kind = "d" if dma is not None else "c"
        semkey = ("dma", dma) if dma is not None else ("eng", eng, self.epoch)
        o = _Op(eng, _freeze(fn), kind, semkey, inc if inc is not None else (16 if kind == "d" else 1))
        deps = {}
        for k in r:
            lw = self.last_writer.get(k)
            if lw is None:
                lw = self.persist.get(k)
            if lw is not None:
                deps[id(lw)] = lw
        for k in w:
            lw = self.last_writer.get(k)
            if lw is not None:
                deps[id(lw)] = lw
            for rd in self.readers.get(k, ()):
                deps[id(rd)] = rd
        o.deps = list(deps.values())
        for k in w:
            self.last_writer[k] = o
            self.readers[k] = []
        for k in r:
            self.readers.setdefault(k, []).append(o)
        self.ops.append(o)
        if persist:
            for k in w:
                self.persist[k] = o
            self.persist_ops.append(o)
        elif kind == "d":
            self.async_since_fence.append(o)
        else:
            self.last_eng_op[eng] = o
        return o

    def fence(self, final=False, skip=()):
        tails = [o for e_, o in self.last_eng_op.items() if e_ not in skip] + list(self.async_since_fence)
        if final:
            tails += self.persist_ops
        new_last = {e_: o for e_, o in self.last_eng_op.items() if e_ in skip}
        for e in self.ENGS:
            if e in skip:
                continue
            o = _Op(e, lambda eng: eng.nop(), "c", ("eng", e, self.epoch), 1)
            o.deps = [t for t in tails]
            self.ops.append(o)
            new_last[e] = o
        self.last_eng_op = new_last
        self.async_since_fence = []
        self.last_writer = {}
        self.readers = {}

    def finalize(self):
        for i, o in enumerate(self.ops):
            o.idx = i
        for o in self.ops:
            best = {}
            for d in o.deps:
                b = best.get(d.semkey)
                if b is None or d.idx > b.idx:
                    best[d.semkey] = d
            o.deps = list(best.values())
        for o in self.ops:
            for d in o.deps:
                if d.kind == "d":
                    d.signalled = True
                elif d.eng != o.eng or o.kind == "d" or d.eng != "pe":
                    d.signalled = True
        counts = {}
        for o in self.ops:
            if o.kind == "d":
                o.signalled = True
            if o.signalled:
                c = counts.get(o.semkey, 0) + o.inc
                counts[o.semkey] = c
                o.count = c
        self.counts = counts
        return list(counts.keys())

    def emit(self, block, sems):
        per_eng = {e: [] for e in self.ENGS}
        for o in self.ops:
            per_eng[o.eng].append(o)

        def run(eng_name):
            def body(e):
                waited = {}
                for o in per_eng[eng_name]:
                    need = {}
                    for d in o.deps:
                        if not d.signalled:
                            continue
                        if d.kind == "c" and d.eng == eng_name and eng_name == "pe" and o.kind == "c":
                            continue
                        if need.get(d.semkey, 0) < d.count:
                            need[d.semkey] = d.count
                    for sk, c in need.items():
                        if waited.get(sk, 0) >= c:
                            continue
                        e.wait_ge(sems[sk], c)
                        waited[sk] = c
                    ins = o.fn(e)
                    if o.signalled:
                        ins.then_inc(sems[o.semkey], o.inc)
            return body

        block.tensor(run("pe"))
        block.scalar(run("act"))
        block.vector(run("dve"))
        block.gpsimd(run("pool"))
        block.sync(run("sp"))


class Arena:
    def __init__(self, ap, nbytes):
        self.ap, self.cap, self.top = ap, nbytes, 0

    def alloc(self, shape, dt):
        if isinstance(shape, int):
            shape = (shape,)
        n = 1
        for s in shape:
            n *= s
        nb = n * (2 if dt == BF16 else 4)
        off = self.top
        self.top = off + (nb + 63) // 64 * 64
        assert self.top <= self.cap, ("SBUF arena overflow", self.top, self.cap)
        v = self.ap[:, off // 2:(off + nb) // 2]
        if dt != BF16:
            v = v.bitcast(dt)
        if len(shape) == 2:
            v = v.rearrange("p (a b) -> p a b", a=shape[0], b=shape[1])
        elif len(shape) == 3:
            v = v.rearrange("p (a b c) -> p a b c", a=shape[0], b=shape[1], c=shape[2])
        return v


def lambda_init_fn(layer_idx):
    return 0.8 - 0.6 * math.exp(-0.3 * layer_idx)


def build_program(cfg):
    c = cfg
    S_, DM, G, H, DEPTH = c.S, c.DM, c.G, c.H, c.DEPTH
    DA, DB, DMIX, DIN = c.DA, c.DB, c.DMIX, c.DIN
    TPC, NST, NBLK, KC, HPC = c.TPC, c.NST, c.NBLK, c.KC, c.HPC
    NCA, NCB, NCM = c.NCA, c.NCB, c.NCM
    NKT = S_ // 128
    NQG = S_ // 512
    HC = HPC * 2
    OFF_U, OFF_V, OFF_GA = 0, DA, 2 * DA
    OFF_Q, OFF_K, OFF_VB, OFF_GB = 3 * DA, 3 * DA + DB, 3 * DA + 2 * DB, 3 * DA + 3 * DB

    nc = bass.Bass("TRN2", target_bir_lowering=False)

    def din(name, shape, dt=F32):
        return nc.dram_tensor(name, list(shape), dt, kind="ExternalInput").ap()

    x_in = din("x", [TPC, DM])
    pos_in = din("pos_t", [128, NBLK], I32)
    invf_in = din("invf", [1, 16])
    ngc_in = din("norm_gc", [DEPTH, 128, KC])
    lgc_in = din("ln_gc", [DEPTH, 128, NCA])
    lbc_in = din("ln_bc", [DEPTH, 128, NCA])
    sguw_in = din("sgu_w", [DEPTH, G, 128, 128])
    sgub_in = din("sgu_b", [DEPTH, 1, G * 128])
    lam_in = din("lam", [DEPTH, 1, 4 * 128])
    subg_in = din("subln_g", [DEPTH, 1, 256])
    fing_in = din("final_g", [1, DM])
    win_in = din("w_in_sh", [DEPTH, KC * 32, DIN])
    wout_in = din("w_out_sh", [DEPTH, NCM * 32, DM])
    out_d = nc.dram_tensor("out", [TPC, DM], F32, kind="ExternalOutput").ap()

    def dscr(name, shape, dt=BF16):
        return nc.dram_tensor(name, list(shape), dt).ap()

    wsh_in = [dscr("wsh_in%d" % l, [DIN // 512, 32, KC, 512]) for l in range(DEPTH)]
    wsh_out = [dscr("wsh_out%d" % l, [DM // 512, 32, NCM, 512]) for l in range(DEPTH)]
    wg_in = [dscr("wg_in%d" % l, [DIN // 512, 128, KC * 512]) for l in range(DEPTH)]
    wg_out = [dscr("wg_out%d" % l, [DM // 512, 128, NCM * 512]) for l in range(DEPTH)]
    xs = [dscr("xs%d" % l, [TPC, DM], F32) for l in range(DEPTH)]
    qT_send = dscr("qT_send", [DB, TPC])
    kT_send = dscr("kT_send", [DB, TPC])
    v_send = dscr("v_send", [H, TPC, 256])
    qT_g = dscr("qT_g", [H, RANKS, 256, TPC])
    kT_g = dscr("kT_g", [H, RANKS, 256, TPC])
    v_g = dscr("v_g", [H, RANKS * TPC, 256])
    onT_send = dscr("onT_send", [RANKS, HPC * 256, TPC])
    onT_g = dscr("onT_g", [RANKS, HPC, RANKS, 256, TPC])
    kT_my = dscr("kT_my", [HPC, RANKS, 256, TPC])
    qT_my = dscr("qT_my", [HPC, RANKS, 256, TPC])
    v_my = dscr("v_my", [HPC, RANKS * TPC, 256])
    on_my = dscr("on_my", [HPC, RANKS, 256, TPC])
    yaT_sp = dscr("yaT_sp", [DA, TPC])
    sgbT_sp = dscr("sgbT_sp", [DB, TPC])

    ARENA_BYTES = int(os.environ.get("ARENA_KB", "206")) * 1024
    ctxs = [nc.sbuf_tensor("arena", [128, ARENA_BYTES // 2], BF16)]
    ctxs += [nc.psum_tensor("pb%d" % i, [128, 512], F32) for i in range(8)]
    ents = [cx.__enter__() for cx in ctxs]
    arena = Arena(ents[0], ARENA_BYTES)
    PB = ents[1:9]
    PBb = [p[:, 0:512].bitcast(BF16) for p in PB]

    S = Sched()
    hp_cache = {}
    GROUP4 = [[0, 1, 2, 3], [4, 5, 6, 7]]
    GROUP8 = [list(range(8))]

    def collective(groups, src, dst, r, w, semkey, persist=False):
        S.op("pool", lambda e: e.collective_compute("AllGather", ALU.bypass, replica_groups=groups,
                                                    ins=[src.opt()], outs=[dst.opt()]),
             r=r, w=w, dma=semkey, inc=1, persist=persist)

    def dump(name, ap, rkeys):
        if not (isinstance(c.debug, (list, tuple)) and ("dump_" + name) in c.debug):
            return
        shp = list(ap.shape)
        n = 1
        for v_ in shp[1:]:
            n *= v_
        dd = nc.dram_tensor("dump_" + name, [shp[0], n], ap.dtype, kind="ExternalOutput").ap()
        src = ap
        if len(shp) == 3:
            dd = dd.rearrange("p (a b) -> p a b", a=shp[1])
        elif len(shp) == 4:
            dd = dd.rearrange("p (a b c) -> p a b c", a=shp[1], b=shp[2])
        S.op("sp", lambda e: e.dma_start(out=dd, in_=src), r=rkeys, w=[("dump", name)], dma="dump_" + name)

    ident = arena.alloc(128, BF16)
    ones_bf = arena.alloc(128, BF16)
    eps_t = arena.alloc(1, F32)
    cos_t = arena.alloc((NBLK, 16), F32)
    sin_t = arena.alloc((NBLK, 16), F32)
    gcol = arena.alloc(KC, F32)
    lgc = arena.alloc(NCA, F32)
    lbc = arena.alloc(NCA, F32)
    wmT = arena.alloc((G, 128), BF16)
    Cb = arena.alloc((NCA, 128), F32)
    gsub = arena.alloc(256, F32)
    neg_lam = arena.alloc(1, F32)
    PERSIST_TOP = arena.top

    def cast_gather(src3, l, sh, g, nchunk, cols, tag, order=None):
        src = src3[l]
        ntile = cols // 512
        ckeys = []
        for kc in range(nchunk):
            S.op("pool", lambda e, kc=kc: e.dma_start(
                out=sh[:, :, kc, :].rearrange("t i n -> i t n"),
                in_=src[kc * 32:(kc + 1) * 32, :].rearrange("i (t n) -> i t n", n=512)),
                 w=[(tag, "sh", kc)], dma="cast", persist=True)
            ckeys.append((tag, "sh", kc))
        order = list(order) if order is not None else list(range(ntile))
        assert sorted(order) == list(range(ntile))
        for n_, t in enumerate(order):
            collective(GROUP4, sh[t].rearrange("i k n -> i (k n)"), g[t],
                       r=(ckeys if n_ == 0 else []) + ([(tag, "thr", n_ - 8)] if n_ >= 8 else []),
                       w=[(tag, "g", t), (tag, "thr", n_)], semkey="cc_" + tag, persist=True)

    S.op("pool", lambda e: e.memset(ident, 0.0), w=["ident0"], persist=True)
    S.op("pool", lambda e: e.affine_select(out=ident, in_=ident, compare_op=ALU.not_equal, fill=1.0,
                                           base=0, pattern=[[-1, 128]], channel_multiplier=1),
         r=["ident0"], w=["ident"], persist=True)
    S.op("pool", lambda e: e.memset(ones_bf, 1.0), w=["ones"], persist=True)
    S.op("pool", lambda e: e.memset(eps_t, EPS), w=["eps"], persist=True)

    def gather_layer(l):
        first = []
        for off, wdt in ((OFF_Q, DB), (OFF_K, DB), (OFF_VB, DB), (OFF_V, DA)):
            first += list(range(off // 512, (off + wdt) // 512))
        for t_ in range(DA // 512):
            first += [OFF_U // 512 + t_, OFF_GA // 512 + t_]
        first += list(range(OFF_GB // 512, (OFF_GB + DB) // 512))
        cast_gather(win_in, l, wsh_in[l], wg_in[l], KC, DIN, "win%d" % l, order=first)
        cast_gather(wout_in, l, wsh_out[l], wg_out[l], NCM, DM, "wout%d" % l)

    gather_layer(0)

    mark = arena.top
    pos_sb = arena.alloc(NBLK, I32)
    posf = arena.alloc(NBLK, F32)
    invf = arena.alloc(16, F32)
    ang = arena.alloc((NBLK, 16), F32)
    tq = arena.alloc((NBLK, 16), F32)
    ki = arena.alloc((NBLK, 16), I32)
    kf = arena.alloc((NBLK, 16), F32)
    rr = arena.alloc((NBLK, 16), F32)
    r2 = arena.alloc((NBLK, 16), F32)
    mk = arena.alloc((NBLK, 16), F32)
    S.op("sp", lambda e: e.dma_start(out=pos_sb, in_=pos_in[:, :]), w=["pos"], dma="ldc0")
    S.op("sp", lambda e: e.dma_start(out=invf, in_=invf_in.partition_broadcast(128)), w=["invf"], dma="ldc1")
    S.op("dve", lambda e: e.tensor_copy(out=posf, in_=pos_sb), r=["pos"], w=["posf"])
    S.op("dve", lambda e: e.tensor_tensor(out=ang, in0=posf.unsqueeze(2).broadcast_to([128, NBLK, 16]),
                                          in1=invf.unsqueeze(1).broadcast_to([128, NBLK, 16]), op=ALU.mult),
         r=["posf", "invf"], w=["ang"])
    TWO_PI = float(2 * np.pi)
    C1 = 6.28125
    C2 = float(2 * np.pi - 6.28125)
    S.op("dve", lambda e: e.tensor_scalar(out=tq, in0=ang, scalar1=float(1 / (2 * np.pi)), scalar2=None, op0=ALU.mult),
         r=["ang"], w=["tq"])
    S.op("dve", lambda e: e.tensor_copy(out=ki, in_=tq), r=["tq"], w=["ki"])
    S.op("dve", lambda e: e.tensor_copy(out=kf, in_=ki), r=["ki"], w=["kf"])
    S.op("dve", lambda e: e.scalar_tensor_tensor(out=rr, in0=kf, scalar=-C1, in1=ang, op0=ALU.mult, op1=ALU.add),
         r=["kf", "ang"], w=["rr"])
    S.op("dve", lambda e: e.scalar_tensor_tensor(out=rr, in0=kf, scalar=-C2, in1=rr, op0=ALU.mult, op1=ALU.add),
         r=["kf", "rr"], w=["rr"])

    def wrap(buf, key):
        S.op("dve", lambda e: e.tensor_single_scalar(out=mk, in_=buf, scalar=float(np.pi), op=ALU.is_gt), r=[key], w=["mk"])
        S.op("dve", lambda e: e.scalar_tensor_tensor(out=buf, in0=mk, scalar=-TWO_PI, in1=buf, op0=ALU.mult, op1=ALU.add),
             r=["mk", key], w=[key])
        S.op("dve", lambda e: e.tensor_single_scalar(out=mk, in_=buf, scalar=-float(np.pi), op=ALU.is_lt), r=[key], w=["mk"])
        S.op("dve", lambda e: e.scalar_tensor_tensor(out=buf, in0=mk, scalar=TWO_PI, in1=buf, op0=ALU.mult, op1=ALU.add),
             r=["mk", key], w=[key])

    wrap(rr, "rr")
    S.op("dve", lambda e: e.tensor_scalar(out=r2, in0=rr, scalar1=float(np.pi / 2), scalar2=None, op0=ALU.add), r=["rr"], w=["r2"])
    wrap(r2, "r2")
    S.op("act", lambda e: e.activation(out=sin_t, in_=rr, func=AF.Sin), r=["rr"], w=["sin"])
    S.op("act", lambda e: e.activation(out=cos_t, in_=r2, func=AF.Sin), r=["r2"], w=["cos"])
    S.persist["sin"] = S.last_writer["sin"]
    S.persist["cos"] = S.last_writer["cos"]
    dump("ident0", ident, ["ident"])
    dump("eps0", eps_t, ["eps"])
    dump("ones0", ones_bf, ["ones"])
    dump("cos", cos_t, ["cos"])
    dump("sin", sin_t, ["sin"])
    S.fence(skip=("pool",))
    arena.top = mark

    for l in range(DEPTH if c.stop >= 1 else 0):
        x_cur = x_in if l == 0 else xs[l - 1]
        x_next = xs[l]
        S.epoch = l + 1
        lam_init = lambda_init_fn(l)

        arena.top = PERSIST_TOP
        w_sb = arena.alloc((G, 128), F32)
        wm_b = arena.alloc((G, 128), BF16)
        bs_bc = arena.alloc((G, 128), F32)
        lamv = arena.alloc((4, 128), F32)
        lprod = arena.alloc((2, 128), F32)
        lsum = arena.alloc(2, F32)
        lexp = arena.alloc(2, F32)
        S.op("sp", lambda e, l=l: e.dma_start(out=gcol, in_=ngc_in[l]), w=["gcol"], dma="ldc0")
        S.op("sp", lambda e, l=l: e.dma_start(out=lgc, in_=lgc_in[l]), w=["lgc"], dma="ldc1")
        S.op("sp", lambda e, l=l: e.dma_start(out=lbc, in_=lbc_in[l]), w=["lbc"], dma="ldc2")
        S.op("sp", lambda e, l=l: e.dma_start(out=w_sb, in_=sguw_in[l].rearrange("g i j -> i g j")), w=["w_sb"], dma="ldc3")
        S.op("sp", lambda e, l=l: e.dma_start(out=bs_bc.rearrange("p g i -> p (g i)"), in_=sgub_in[l].partition_broadcast(128)),
             w=["bs_bc"], dma="ldc4")
        S.op("sp", lambda e, l=l: e.dma_start(out=lamv.rearrange("p a b -> p (a b)"), in_=lam_in[l].partition_broadcast(128)),
             w=["lamv"], dma="ldc5")
        S.op("sp", lambda e, l=l: e.dma_start(out=gsub, in_=subg_in[l].partition_broadcast(128)), w=["gsub"], dma="ldc6")
        S.op("dve", lambda e: e.memset(w_sb[0:64, :, 64:128], 0.0), r=["w_sb"], w=["w_sb"])
        S.op("dve", lambda e: e.tensor_copy(out=wm_b, in_=w_sb), r=["w_sb"], w=["wm_b"])
        for g in range(G):
            S.op("pe", lambda e, g=g: e.transpose(PBb[6][:, g * 128:(g + 1) * 128], wm_b[:, g, :], ident),
                 r=["wm_b", "ident"], w=[("pb6", g)])
        S.op("dve", lambda e: e.tensor_copy(out=wmT.rearrange("p g i -> p (g i)"), in_=PBb[6][:, 0:G * 128]),
             r=[("pb6", g) for g in range(G)], w=["wmT"])
        if l == 0:
            dump("w_sb", w_sb, ["w_sb"])
            dump("wm_b", wm_b, ["wm_b"])
            dump("wmT2", wmT, ["wmT"])
        wmT_f = wmT.rearrange("p g i -> p (g i)")
        npiece = (G * 128 + 511) // 512
        for pc in range(npiece):
            n = min(512, G * 128 - pc * 512)
            S.op("pe", lambda e, pc=pc, n=n: e.matmul(PB[4 + pc][:, 0:n], ones_bf, wmT_f[:, pc * 512:pc * 512 + n],
                                                      start=True, stop=True),
                 r=["wmT", "ones"], w=[("rw", pc)])
        for fc in range(NCA):
            g = fc // 2
            pc, o = (g * 128) // 512, (g * 128) % 512
            S.op("dve", lambda e, fc=fc, g=g, pc=pc, o=o: e.scalar_tensor_tensor(
                out=Cb[:, fc, :], in0=PB[4 + pc][:, o:o + 128], scalar=lbc[:, fc:fc + 1], in1=bs_bc[:, g, :],
                op0=ALU.mult, op1=ALU.add), r=[("rw", pc), "lbc", "bs_bc"], w=[("Cb", fc)])
        S.op("dve", lambda e: e.tensor_tensor(out=lprod[:, 0, :], in0=lamv[:, 0, :], in1=lamv[:, 1, :], op=ALU.mult),
             r=["lamv"], w=["lp0"])
        S.op("dve", lambda e: e.tensor_tensor(out=lprod[:, 1, :], in0=lamv[:, 2, :], in1=lamv[:, 3, :], op=ALU.mult),
             r=["lamv"], w=["lp1"])
        S.op("dve", lambda e: e.reduce_sum(out=lsum, in_=lprod, axis=AX.X), r=["lp0", "lp1"], w=["lsum"])
        S.op("act", lambda e: e.activation(out=lexp, in_=lsum, func=AF.Exp), r=["lsum"], w=["lexp"])
        S.op("dve", lambda e: e.tensor_tensor(out=neg_lam, in0=lexp[:, 1:2], in1=lexp[:, 0:1], op=ALU.subtract),
             r=["lexp"], w=["neg_lam"])
        S.op("dve", lambda e, li=lam_init: e.tensor_scalar(out=neg_lam, in0=neg_lam, scalar1=-float(li), scalar2=None, op0=ALU.add),
             r=["neg_lam"], w=["neg_lam"])
        S.op("dve", lambda e, li=lam_init: e.tensor_scalar(out=gsub, in0=gsub, scalar1=float(1.0 - li), scalar2=None, op0=ALU.mult),
             r=["gsub"], w=["gsub"])
        S.fence(skip=("pool",))

        if c.stop < 2:
            continue
        arena.top = PERSIST_TOP
        wslot = [arena.alloc((KC, 512), BF16) for _ in range(2)]
        hT = arena.alloc((KC, 512), BF16)
        gv = arena.alloc((4, DA), F32)
        gv_flat = gv.rearrange("p a b -> p (a b)")
        vln = arena.alloc((4, DA), BF16)
        vln_flat = vln.rearrange("p a b -> p (a b)")
        nxl = 2 if 2 * DM <= 4 * DA else 1
        xl = [gv_flat[:, i * DM:(i + 1) * DM] for i in range(nxl)]
        nxn = 2 if 2 * DM <= 4 * DA else 1
        xn = [vln_flat[:, i * DM:(i + 1) * DM] for i in range(nxn)]

        def gv_keys(lo, hi):
            return [("gv", b) for b in range(lo // DA, (hi - 1) // DA + 1)]

        def vln_keys(lo, hi):
            return [("vln", b) for b in range(lo // DA, (hi - 1) // DA + 1)]

        ssq = arena.alloc(4, F32)
        rstd = arena.alloc(4, F32)
        stats = arena.alloc((4, 6 * ((DA + 511) // 512)), F32)
        mv = arena.alloc((4, 2), F32)
        lrs = arena.alloc(4, F32)
        qkb = [arena.alloc(512, BF16) for _ in range(2)]
        rt = [arena.alloc((4, 64), F32) for _ in range(2)]
        qst = [arena.alloc((4, 512), BF16) for _ in range(2)]
        vst = [arena.alloc(512, BF16) for _ in range(2)]
        gu = [arena.alloc(512, BF16) for _ in range(2)]
        t1 = [arena.alloc(512, F32) for _ in range(2)]
        p1 = arena.alloc((4, 512), BF16)
        sg = [arena.alloc(512, BF16) for _ in range(2)]
        yst = [arena.alloc(512, BF16) for _ in range(2)]
        sgst = [arena.alloc(512, BF16) for _ in range(2)]

        nA, nB = DA // 512, DB // 512
        tiles = []
        for t in range(nA):
            tiles.append(("va", t, OFF_V + t * 512))
        for t in range(nB):
            tiles.append(("q", t, OFF_Q + t * 512))
        for t in range(nB):
            tiles.append(("k", t, OFF_K + t * 512))
        for t in range(nB):
            tiles.append(("vb", t, OFF_VB + t * 512))
        for t in range(nA):
            tiles.append(("u", t, OFF_U + t * 512))
            tiles.append(("ga", t, OFF_GA + t * 512))
        for t in range(nB):
            tiles.append(("gb", t, OFF_GB + t * 512))
        NT = len(tiles)

        cnt = {"acc": 0, "tr": 0, "m": 0, "z": 0}

        tiles_a = [tl for tl in tiles if tl[0] in ("q", "k", "vb")]
        tiles_b = [tl for tl in tiles if tl[0] not in ("q", "k", "vb")]
        seq = [(st_, tl) for st_ in range(NST) for tl in tiles_a] + [(st_, tl) for st_ in range(NST) for tl in tiles_b]

        def wload_seq(gidx):
            _, (kind_, t_, c0_) = seq[gidx]
            sl_ = gidx % 2
            S.op("sp", lambda e, sl_=sl_, c0_=c0_: e.dma_start(out=wslot[sl_].rearrange("p k n -> p (k n)"), in_=wg_in[l][c0_ // 512]),
                 r=[("win%d" % l, "g", c0_ // 512)], w=[("w", sl_)], dma="w%d" % sl_)

        wload_seq(0)

        def run_supertile(st, sub_tiles, first_idx):
            tok0 = st * 512
            for j in range(4):
                xb = xl[j % nxl]
                xk = gv_keys((j % nxl) * DM, (j % nxl + 1) * DM)
                nb = xn[j % nxn]
                nk = vln_keys((j % nxn) * DM, (j % nxn + 1) * DM)
                r0 = tok0 + j * 128
                S.op("sp", lambda e, xb=xb, r0=r0: e.dma_start(out=xb, in_=x_cur[r0:r0 + 128, :]),
                     w=xk, dma="x%d" % (j % nxl))
                S.op("act", lambda e, xb=xb, nb=nb, j=j: e.activation(out=nb, in_=xb, func=AF.Square, accum_out=ssq[:, j:j + 1]),
                     r=xk, w=nk + [("ssq", j)])
                S.op("act", lambda e, j=j: e.activation(out=rstd[:, j:j + 1], in_=ssq[:, j:j + 1], func=AF.Sqrt,
                                                        scale=1.0 / DM, bias=eps_t), r=[("ssq", j), "eps"], w=[("rstd", j)])
                S.op("dve", lambda e, j=j: e.reciprocal(out=rstd[:, j:j + 1], in_=rstd[:, j:j + 1]),
                     r=[("rstd", j)], w=[("rstd", j)])
                S.op("dve", lambda e, xb=xb, nb=nb, j=j: e.tensor_scalar_mul(out=nb, in0=xb, scalar1=rstd[:, j:j + 1]),
                     r=xk + [("rstd", j)], w=nk)
                for k0 in range(0, KC, 8):
                    nk8 = min(8, KC - k0)
                    bank = 6 + (cnt["tr"] % 2)
                    cnt["tr"] += 1
                    for kk in range(nk8):
                        S.op("pe", lambda e, nb=nb, bank=bank, kk=kk, k0=k0: e.transpose(
                            PBb[bank][:, kk * 128:(kk + 1) * 128], nb[:, (k0 + kk) * 128:(k0 + kk + 1) * 128], ident),
                             r=nk + ["ident"], w=[("pb", bank, kk)])
                    S.op("dve", lambda e, bank=bank, k0=k0, nk8=nk8, j=j: e.tensor_tensor(
                        out=hT[:, k0:k0 + nk8, j * 128:(j + 1) * 128],
                        in0=PBb[bank][:, 0:nk8 * 128].rearrange("p (a b) -> p a b", a=nk8),
                        in1=gcol[:, k0:k0 + nk8].unsqueeze(2).broadcast_to([128, nk8, 128]), op=ALU.mult),
                         r=[("pb", bank, kk) for kk in range(nk8)] + ["gcol"], w=[("hT", j)])

            hT_all = [("hT", j) for j in range(4)]
            if st == 0 and l == 0 and first_idx == 0:
                dump("hT", hT, hT_all)
                dump("rstd", rstd, [("rstd", j) for j in range(4)])
                dump("gcol", gcol, ["gcol"])
                dump("w0", wslot[0], [("w", 0)])
                dump("wmT", wmT, ["wmT"])
                dump("ident", ident, ["ident"])
                dump("eps", eps_t, ["eps"])
                dump("ones", ones_bf, ["ones"])
                dump("ssq", ssq, [("ssq", j) for j in range(4)])
                dump("x3", xl[3 % nxl], gv_keys(0, 4 * DA))
                dump("xn3", xn[3 % nxn], vln_keys(0, 4 * DA))
                dump("Cb", Cb, [("Cb", fc) for fc in range(NCA)])
            pend = []

            def flush_pend():
                for f in pend:
                    f()
                del pend[:]

            for k_, (kind, t, c0) in enumerate(sub_tiles):
                gidx = first_idx + k_
                sl = gidx % 2
                W = wslot[sl]
                if gidx + 1 < len(seq):
                    wload_seq(gidx + 1)
                if kind in ("va", "q", "k", "vb"):
                    for j in range(4):
                        bank = cnt["acc"] % 4
                        cnt["acc"] += 1
                        for kc in range(KC):
                            S.op("pe", lambda e, bank=bank, kc=kc, j=j, W=W: e.matmul(
                                PB[bank][:, 0:512], hT[:, kc, j * 128:(j + 1) * 128], W[:, kc, :],
                                start=(kc == 0), stop=(kc == KC - 1)),
                                 r=[("hT", j), ("w", sl)], w=[("pb", bank)])
                        flush_pend()
                        if kind == "va":
                            S.op("act", lambda e, bank=bank, j=j, t=t: e.activation(
                                out=gv[:, j, t * 512:(t + 1) * 512], in_=PB[bank][:, 0:512], func=AF.Gelu),
                                 r=[("pb", bank)], w=[("gv", j)])
                        elif kind == "vb":
                            zi = cnt["z"] % 2
                            cnt["z"] += 1
                            S.op("act", lambda e, bank=bank, zi=zi: e.activation(out=vst[zi], in_=PB[bank][:, 0:512], func=AF.Copy),
                                 r=[("pb", bank)], w=[("vst", zi)])
                            r0 = tok0 + j * 128
                            S.op("sp", lambda e, zi=zi, r0=r0, t=t: e.dma_start(
                                out=v_send[2 * t:2 * t + 2, r0:r0 + 128, :].rearrange("h p e -> p h e"),
                                in_=vst[zi].rearrange("p (h e) -> p h e", h=2)),
                                 r=[("vst", zi)], w=[("v_send", st, j, t)], dma="vst%d" % zi)
                        else:
                            zi = cnt["z"] % 2
                            cnt["z"] += 1
                            z4 = PB[bank][:, 0:512].rearrange("p (a b) -> p a b", a=4)
                            o4 = qkb[zi].rearrange("p (a b) -> p a b", a=4)
                            blk = st * 4 + j
                            cb = cos_t[:, blk, :].unsqueeze(1).broadcast_to([128, 4, 16])
                            sb_ = sin_t[:, blk, :].unsqueeze(1).broadcast_to([128, 4, 16])
                            T = rt[zi]
                            kb, kr, ko = [("pb", bank)], [("rt", zi)], [("qkb", zi)]
                            S.op("dve", lambda e, z4=z4, T=T, cb=cb: e.tensor_tensor(out=T[:, :, 0:16], in0=z4[:, :, 0:16], in1=cb, op=ALU.mult), r=kb, w=kr)
                            S.op("dve", lambda e, z4=z4, T=T, sb_=sb_: e.tensor_tensor(out=T[:, :, 16:32], in0=z4[:, :, 16:32], in1=sb_, op=ALU.mult), r=kb, w=kr)
                            S.op("dve", lambda e, z4=z4, T=T, cb=cb: e.tensor_tensor(out=T[:, :, 32:48], in0=z4[:, :, 16:32], in1=cb, op=ALU.mult), r=kb, w=kr)
                            S.op("dve", lambda e, z4=z4, T=T, sb_=sb_: e.tensor_tensor(out=T[:, :, 48:64], in0=z4[:, :, 0:16], in1=sb_, op=ALU.mult), r=kb, w=kr)
                            S.op("dve", lambda e, o4=o4, T=T: e.tensor_tensor(out=o4[:, :, 0:16], in0=T[:, :, 0:16], in1=T[:, :, 16:32], op=ALU.subtract), r=kr, w=ko)
                            S.op("dve", lambda e, o4=o4, T=T: e.tensor_tensor(out=o4[:, :, 16:32], in0=T[:, :, 32:48], in1=T[:, :, 48:64], op=ALU.add), r=kr, w=ko)
                            S.op("act", lambda e, o4=o4, z4=z4: e.activation(out=o4[:, :, 32:128], in_=z4[:, :, 32:128], func=AF.Copy), r=kb, w=ko)
                            qi = (0 if kind == "q" else 1)
                            dst = qT_send if kind == "q" else kT_send

                            def do_tr(zi=zi, j=j, qi=qi, t=t, dst=dst, last=(j == 3)):
                                bank2 = 6 + (cnt["tr"] % 2)
                                cnt["tr"] += 1
                                for a in range(4):
                                    S.op("pe", lambda e, a=a, zi=zi, bank2=bank2: e.transpose(
                                        PBb[bank2][:, a * 128:(a + 1) * 128], qkb[zi][:, a * 128:(a + 1) * 128], ident),
                                         r=[("qkb", zi), "ident"], w=[("pb", bank2, a)])
                                S.op("act", lambda e, bank2=bank2, qi=qi, j=j: e.activation(
                                    out=qst[qi][:, :, j * 128:(j + 1) * 128],
                                    in_=PBb[bank2][:, 0:512].rearrange("p (a b) -> p a b", a=4), func=AF.Copy),
                                     r=[("pb", bank2, a) for a in range(4)], w=[("qst", qi, j)])
                                if last:
                                    S.op("sp", lambda e, qi=qi, t=t, dst=dst: e.dma_start(
                                        out=dst[t * 512:(t + 1) * 512, tok0:tok0 + 512].rearrange("(a d) n -> d a n", d=128),
                                        in_=qst[qi]), r=[("qst", qi, jj) for jj in range(4)],
                                         w=[("qk_send", qi, st, t)] + [("qst", qi, jj) for jj in range(4)], dma="qst%d" % qi)
                            pend.append(do_tr)
                    if kind == "va" and t == nA - 1:
                        nch = (DA + 511) // 512
                        for j in range(4):
                            for ch in range(nch):
                                n = min(512, DA - ch * 512)
                                S.op("dve", lambda e, j=j, ch=ch, n=n: e.bn_stats(out=stats[:, j, 6 * ch:6 * ch + 6],
                                                                                  in_=gv[:, j, ch * 512:ch * 512 + n]),
                                     r=[("gv", j)], w=[("stats", j, ch)])
                            S.op("dve", lambda e, j=j: e.bn_aggr(out=mv[:, j, :], in_=stats[:, j, :]),
                                 r=[("stats", j, ch) for ch in range(nch)], w=[("mv", j)])
                            S.op("act", lambda e, j=j: e.activation(out=lrs[:, j:j + 1], in_=mv[:, j, 1:2], func=AF.Sqrt, bias=eps_t),
                                 r=[("mv", j), "eps"], w=[("lrs", j)])
                            S.op("dve", lambda e, j=j: e.reciprocal(out=lrs[:, j:j + 1], in_=lrs[:, j:j + 1]), r=[("lrs", j)], w=[("lrs", j)])
                            S.op("dve", lambda e, j=j: e.tensor_scalar(out=vln[:, j, :], in0=gv[:, j, :], scalar1=mv[:, j, 0:1],
                                                                       scalar2=lrs[:, j:j + 1], op0=ALU.subtract, op1=ALU.mult),
                                 r=[("gv", j), ("mv", j), ("lrs", j)], w=[("vln", j)])
                else:
                    for fcl in range(4):
                        fc = t * 4 + fcl
                        if kind == "u":
                            g = fc // 2
                            mb = 4 + (cnt["m"] % 2)
                            cnt["m"] += 1
                            for j in range(4):
                                S.op("pe", lambda e, mb=mb, j=j, fc=fc, g=g: e.matmul(
                                    PB[mb][:, j * 128:(j + 1) * 128], vln[:, j, fc * 128:(fc + 1) * 128], wmT[:, g, :],
                                    start=True, stop=True), r=[("vln", j), "wmT"], w=[("pbm", mb, j)])
                        bank = cnt["acc"] % 4
                        cnt["acc"] += 1
                        for kc in range(KC):
                            S.op("pe", lambda e, bank=bank, kc=kc, fcl=fcl, W=W: e.matmul(
                                PB[bank][:, 0:512], W[:, kc, fcl * 128:(fcl + 1) * 128], hT[:, kc, :],
                                start=(kc == 0), stop=(kc == KC - 1)), r=hT_all + [("w", sl)], w=[("pb", bank)])
                        flush_pend()
                        zi = cnt["z"] % 2
                        cnt["z"] += 1
                        if kind == "u":
                            S.op("act", lambda e, bank=bank, zi=zi: e.activation(out=gu[zi], in_=PB[bank][:, 0:512], func=AF.Gelu),
                                 r=[("pb", bank)], w=[("gu", zi)])
                            S.op("dve", lambda e, mb=mb, zi=zi, fc=fc: e.scalar_tensor_tensor(
                                out=t1[zi].rearrange("p (a b) -> p a b", a=4),
                                in0=PB[mb][:, 0:512].rearrange("p (a b) -> p a b", a=4), scalar=lgc[:, fc:fc + 1],
                                in1=Cb[:, fc, :].unsqueeze(1).broadcast_to([128, 4, 128]), op0=ALU.mult, op1=ALU.add),
                                 r=[("pbm", mb, j) for j in range(4)] + ["lgc", ("Cb", fc)], w=[("t1", zi)])
                            S.op("dve", lambda e, zi=zi, fcl=fcl: e.tensor_tensor(out=p1[:, fcl, :], in0=t1[zi], in1=gu[zi], op=ALU.mult),
                                 r=[("t1", zi), ("gu", zi)], w=[("p1", fcl)])
                        elif kind == "ga":
                            S.op("act", lambda e, bank=bank, zi=zi: e.activation(out=sg[zi], in_=PB[bank][:, 0:512], func=AF.Silu),
                                 r=[("pb", bank)], w=[("sg", zi)])
                            S.op("dve", lambda e, zi=zi, fcl=fcl: e.tensor_tensor(out=yst[zi], in0=p1[:, fcl, :], in1=sg[zi], op=ALU.mult),
                                 r=[("p1", fcl), ("sg", zi)], w=[("yst", zi)])
                            S.op("sp", lambda e, zi=zi, fc=fc: e.dma_start(out=yaT_sp[fc * 128:(fc + 1) * 128, tok0:tok0 + 512], in_=yst[zi]),
                                 r=[("yst", zi)], w=[("yaT_sp", st, fc)], dma="yst%d" % zi)
                        else:
                            S.op("act", lambda e, bank=bank, zi=zi: e.activation(out=sgst[zi], in_=PB[bank][:, 0:512], func=AF.Silu),
                                 r=[("pb", bank)], w=[("sgst", zi)])
                            S.op("sp", lambda e, zi=zi, fc=fc: e.dma_start(out=sgbT_sp[fc * 128:(fc + 1) * 128, tok0:tok0 + 512], in_=sgst[zi]),
                                 r=[("sgst", zi)], w=[("sgbT_sp", st, fc)], dma="sgst%d" % zi)
            flush_pend()

        g0 = 0
        for st_ in range(NST):
            run_supertile(st_, tiles_a, g0)
            g0 += len(tiles_a)
        kkeys = [("qk_send", 1, st_, t_) for st_ in range(NST) for t_ in range(nB)]
        qkeys = [("qk_send", 0, st_, t_) for st_ in range(NST) for t_ in range(nB)]
        vkeys = [("v_send", st_, j_, t_) for st_ in range(NST) for j_ in range(4) for t_ in range(nB)]
        for h in range(H):
            collective(GROUP4, kT_send[h * 256:(h + 1) * 256, :], kT_g[h].rearrange("r a t -> (r a) t"), r=kkeys, w=[("kT_g", h)], semkey="cck")
        for h in range(H):
            collective(GROUP4, v_send[h], v_g[h], r=vkeys, w=[("v_g", h)], semkey="ccv")
        for h in range(H):
            collective(GROUP4, qT_send[h * 256:(h + 1) * 256, :], qT_g[h].rearrange("r a t -> (r a) t"), r=qkeys, w=[("qT_g", h)], semkey="ccq")
        for st_ in range(NST):
            run_supertile(st_, tiles_b, g0)
            g0 += len(tiles_b)
        S.fence()
        if c.stop < 3:
            continue

        if c.stop < 4:
            continue
        arena.top = PERSIST_TOP
        kT = arena.alloc((HC, S_), BF16)
        Vsb = arena.alloc((NKT, HPC, 257), BF16)
        qT = [arena.alloc((HC, 512), BF16) for _ in range(2)]
        PT = [arena.alloc(512, BF16) for _ in range(6)]
        PTd = [arena.alloc(512, BF16) for _ in range(4)]
        o1 = arena.alloc((4, 256), F32)
        oo = arena.alloc((4, 256), F32)
        junk = arena.alloc(256, BF16)
        rl = arena.alloc(8, F32)
        oss = arena.alloc(4, F32)
        ors = arena.alloc(4, F32)
        onb = arena.alloc((4, 256), BF16)
        onst = [arena.alloc((HC, 512), BF16) for _ in range(2)]

        def hp_of(e):
            if "hp" not in hp_cache:
                hp_cache["hp"] = e.partition_id() % RANKS
            return hp_cache["hp"]

        def private_copy(dst, src_g, key_src, key_dst, sem):
            nd = len(src_g.shape)
            names = " ".join("d%d" % i for i in range(nd))
            dn = " ".join("e%d" % i for i in range(len(dst.shape)))
            S.op("sp", lambda e: e.dma_start(
                out=dst.rearrange("%s -> (%s)" % (dn, dn)).rearrange("(p f) -> p f", f=4096),
                in_=src_g[bass.ds(hp_of(e), 1)].rearrange("%s -> (%s)" % (names, names)).rearrange("(p f) -> p f", f=4096)),
                 r=[key_src], w=[key_dst], dma=sem)

        private_copy(kT_my, kT_g.rearrange("(a b) r c t -> a b r c t", b=HPC), "kT_g", "kT_my", "pck")
        private_copy(v_my, v_g.rearrange("(a b) n e -> a b n e", b=HPC), "v_g", "v_my", "pcv")
        private_copy(qT_my, qT_g.rearrange("(a b) r c t -> a b r c t", b=HPC), "qT_g", "qT_my", "pcq")
        for hl in range(HPC):
            for r in range(RANKS):
                S.op("sp", lambda e, r=r, hl=hl: e.dma_start(
                    out=kT[:, hl * 2:hl * 2 + 2, r * TPC:(r + 1) * TPC],
                    in_=kT_my[hl, r].rearrange("(c d) t -> d c t", d=128)),
                     r=["kT_my"], w=[("kT", r, hl)], dma="ldk%d" % hl)
        for hl in range(HPC):
            S.op("sp", lambda e, hl=hl: e.dma_start(
                out=Vsb[:, :, hl, 0:256], in_=v_my[hl].rearrange("(kt p) e -> p kt e", p=128)),
                 r=["v_my"], w=[("V", hl)], dma="ldv%d" % hl)
        S.op("pool", lambda e: e.memset(Vsb[:, :, :, 256:257], 1.0), w=["Vone"])
        for kb in range(4):
            S.op("pool", lambda e, kb=kb: e.memset(PTd[kb], 0.0), w=[("PTd", kb)])
        if l + 1 < DEPTH:
            gather_layer(l + 1)
        scale = float(128 ** -0.5)
        ptc = 0
        sbank = 0
        deferred = []
        for Gq in range(NQG):
            rq, oq = Gq // NST, (Gq % NST) * 512
            qb_ = qT[Gq % 2]
            for hl in range(HPC):
                S.op("sp", lambda e, qb_=qb_, rq=rq, oq=oq, hl=hl: e.dma_start(
                    out=qb_[:, hl * 2:hl * 2 + 2, :],
                    in_=qT_my[hl, rq, :, oq:oq + 512].rearrange("(c d) t -> d c t", d=128)),
                     r=["qT_my"], w=[("qT", Gq % 2, hl)], dma="ldq%d_%d" % (Gq % 2, hl))
            ntile = 4 * Gq + 4
            ost = onst[Gq % 2]
            for hl in range(HPC):
                for cpt in range(2):
                    hc = hl * 2 + cpt
                    prev = None
                    for t in range(ntile + 1):
                        if t < ntile:
                            kb = t - 4 * Gq
                            c_lo = max(kb, 0) * 128
                            sb = (4, 5, 7)[sbank % 3]
                            sbank += 1
                            S.op("pe", lambda e, sb=sb, hc=hc, t=t, c_lo=c_lo, qb_=qb_: e.matmul(
                                PB[sb][:, c_lo:512], kT[:, hc, t * 128:(t + 1) * 128], qb_[:, hc, c_lo:512],
                                start=True, stop=True), r=[("kT", rr_, hl) for rr_ in range(RANKS)] + [("qT", Gq % 2, hl)], w=[("pbs", sb)])
                            if kb >= 0:
                                P = PTd[kb]
                                pk = ("PTd", kb)
                                S.op("act", lambda e, P=P, sb=sb, c_lo=c_lo: e.activation(
                                    out=P[:, c_lo + 64:512], in_=PB[sb][:, c_lo + 64:512], func=AF.Exp, scale=scale),
                                     r=[("pbs", sb)], w=[pk])
                                S.op("act", lambda e, P=P, sb=sb, c_lo=c_lo: e.activation(
                                    out=P[0:64, c_lo:c_lo + 64], in_=PB[sb][0:64, c_lo:c_lo + 64], func=AF.Exp, scale=scale),
                                     r=[("pbs", sb)], w=[pk])
                            else:
                                P = PT[ptc % 6]
                                pk = ("PT", ptc % 6)
                                ptc += 1
                                S.op("act", lambda e, P=P, sb=sb: e.activation(out=P, in_=PB[sb][:, 0:512], func=AF.Exp, scale=scale),
                                     r=[("pbs", sb)], w=[pk])
                            cur = (t, max(kb, 0), P, pk)
                        else:
                            cur = None
                        if prev is not None:
                            pt_, qlo, P_, pk_ = prev
                            for qb in range(qlo, 4):
                                last_t = 4 * Gq + qb
                                S.op("pe", lambda e, qb=qb, P_=P_, pt_=pt_, hl=hl, last_t=last_t: e.matmul(
                                    PB[qb][:, 0:257], P_[:, qb * 128:(qb + 1) * 128], Vsb[:, pt_, hl, :],
                                    start=(pt_ == 0), stop=(pt_ == last_t)),
                                     r=[pk_, ("V", hl), "Vone"], w=[("pbo", qb)])
                        prev = cur
                        if t == 1 and deferred:
                            for f in deferred:
                                f()
                            del deferred[:]
                    for qb in range(4):
                        if cpt == 0:
                            S.op("dve", lambda e, qb=qb: e.reciprocal(out=rl[:, qb:qb + 1], in_=PB[qb][:, 256:257]),
                                 r=[("pbo", qb)], w=[("rl", qb)])
                            S.op("dve", lambda e, qb=qb: e.tensor_scalar_mul(out=o1[:, qb, :], in0=PB[qb][:, 0:256], scalar1=rl[:, qb:qb + 1]),
                                 r=[("pbo", qb), ("rl", qb)], w=[("o1", qb)])
                        else:
                            S.op("dve", lambda e, qb=qb: e.reciprocal(out=rl[:, 4 + qb:5 + qb], in_=PB[qb][:, 256:257]),
                                 r=[("pbo", qb)], w=[("rl2", qb)])
                            S.op("dve", lambda e, qb=qb: e.tensor_tensor(out=rl[:, 4 + qb:5 + qb], in0=rl[:, 4 + qb:5 + qb], in1=neg_lam, op=ALU.mult),
                                 r=[("rl2", qb), "neg_lam"], w=[("rl2", qb)])
                            S.op("dve", lambda e, qb=qb: e.scalar_tensor_tensor(
                                out=oo[:, qb, :], in0=PB[qb][:, 0:256], scalar=rl[:, 4 + qb:5 + qb], in1=o1[:, qb, :],
                                op0=ALU.mult, op1=ALU.add), r=[("pbo", qb), ("rl2", qb), ("o1", qb)], w=[("oo", qb)])
                            S.op("act", lambda e, qb=qb: e.activation(out=junk, in_=oo[:, qb, :], func=AF.Square, accum_out=oss[:, qb:qb + 1]),
                                 r=[("oo", qb)], w=["junk", ("oss", qb)])
                            S.op("act", lambda e, qb=qb: e.activation(out=ors[:, qb:qb + 1], in_=oss[:, qb:qb + 1], func=AF.Ln,
                                                                      scale=1.0 / 256, bias=eps_t), r=[("oss", qb), "eps"], w=[("ors", qb)])
                            S.op("act", lambda e, qb=qb: e.activation(out=ors[:, qb:qb + 1], in_=ors[:, qb:qb + 1], func=AF.Exp, scale=-0.5),
                                 r=[("ors", qb)], w=[("ors", qb)])
                            S.op("dve", lambda e, qb=qb: e.scalar_tensor_tensor(
                                out=onb[:, qb, :], in0=oo[:, qb, :], scalar=ors[:, qb:qb + 1], in1=gsub, op0=ALU.mult, op1=ALU.mult),
                                 r=[("oo", qb), ("ors", qb), "gsub"], w=[("onb", qb)])
                    if cpt == 1:
                        def make_deferred(Gq=Gq, hl=hl, ost=ost):
                            def f():
                                for qb in range(4):
                                    for half in range(2):
                                        S.op("pe", lambda e, qb=qb, half=half: e.transpose(
                                            PBb[6][:, (qb * 2 + half) * 128:(qb * 2 + half + 1) * 128],
                                            onb[:, qb, half * 128:(half + 1) * 128], ident),
                                             r=[("onb", qb), "ident"], w=[("pbt", qb, half)])
                                for half in range(2):
                                    S.op("dve", lambda e, half=half: e.tensor_copy(
                                        out=ost[:, hl * 2 + half, :].rearrange("p (a b) -> p a b", a=4),
                                        in_=PBb[6][:, 0:1024].rearrange("p (a h b) -> p a h b", a=4, h=2)[:, :, half, :]),
                                         r=[("pbt", qb, half) for qb in range(4)], w=[("onst", Gq % 2, hl, half)])
                                if hl == HPC - 1:
                                    okeys = [("onst", Gq % 2, h_, f_) for h_ in range(HPC) for f_ in range(2)]
                                    S.op("sp", lambda e: e.dma_start(
                                        out=onT_send[Gq // NST, :, (Gq % NST) * 512:(Gq % NST + 1) * 512].rearrange("(a d) n -> d a n", d=128), in_=ost),
                                         r=okeys, w=[("onT_send", Gq)] + okeys, dma="onst%d" % (Gq % 2))
                            return f
                        deferred.append(make_deferred())
        for f in deferred:
            f()
        del deferred[:]
        S.fence()
        if c.stop < 5:
            continue
        for tq_ in range(RANKS):
            for hl in range(HPC):
                collective(GROUP4, onT_send[tq_, hl * 256:(hl + 1) * 256, :], onT_g[tq_, hl].rearrange("r a t -> (r a) t"),
                           r=[], w=["onT_g"], semkey="ccon")
        S.fence()
        if c.stop < 6:
            continue

        private_copy(on_my, onT_g, "onT_g", "on_my", "pco")
        arena.top = PERSIST_TOP
        wos = [arena.alloc((NCM, 512), BF16) for _ in range(2)]
        yT = arena.alloc((NCM, 512), BF16)
        sgT = arena.alloc((NCB, 512), BF16)
        xres = [arena.alloc(512, F32) for _ in range(4)]
        xo = [arena.alloc(512, F32) for _ in range(2)]
        ND = DM // 512
        xc = 0
        acc = 0
        def wo_load(gi_):
            sl_, dg_ = gi_ % 2, gi_ % ND
            S.op("sp", lambda e, sl_=sl_, dg_=dg_: e.dma_start(out=wos[sl_].rearrange("p k n -> p (k n)"), in_=wg_out[l][dg_]),
                 r=[("wout%d" % l, "g", dg_)], w=[("wo", sl_)], dma="wo%d" % sl_)

        for st in range(NST):
            tok0 = st * 512
            S.op("sp", lambda e, tok0=tok0: e.dma_start(
                out=yT[:, 0:NCA, :], in_=yaT_sp[:, tok0:tok0 + 512].rearrange("(a d) n -> d a n", d=128)),
                 w=["yaT"], dma="ldya")
            for hl in range(HPC):
                for r in range(RANKS):
                    c0 = NCA + (r * HPC + hl) * 2
                    S.op("sp", lambda e, tok0=tok0, hl=hl, r=r, c0=c0: e.dma_start(
                        out=yT[:, c0:c0 + 2, :],
                        in_=on_my[hl, r, :, tok0:tok0 + 512].rearrange("(c d) t -> d c t", d=128)),
                         r=["on_my"], w=[("onT", c0 - NCA), ("onT", c0 - NCA + 1), ("ybT", c0 - NCA), ("ybT", c0 - NCA + 1)], dma="ldon%d" % (r * HPC + hl))
            S.op("sp", lambda e, tok0=tok0: e.dma_start(
                out=sgT, in_=sgbT_sp[:, tok0:tok0 + 512].rearrange("(a d) n -> d a n", d=128)),
                 w=["sgT"], dma="ldsg")
            for cb in range(NCB):
                eng = "dve"
                S.op(eng, lambda e, cb=cb: e.tensor_tensor(out=yT[:, NCA + cb, :], in0=yT[:, NCA + cb, :], in1=sgT[:, cb, :], op=ALU.mult),
                     r=[("onT", cb), "sgT"], w=[("ybT", cb)])
            ykeys = ["yaT"] + [("ybT", cb) for cb in range(NCB)]
            for dg in range(ND):
                gi_ = st * ND + dg
                sl = gi_ % 2
                if gi_ == 0:
                    wo_load(0)
                if gi_ + 1 < NST * ND:
                    wo_load(gi_ + 1)
                for j in range(4):
                    r0 = tok0 + j * 128
                    xi = xc % 4
                    xoi = xc % 2
                    xc += 1
                    S.op("sp", lambda e, xi=xi, r0=r0, dg=dg: e.dma_start(out=xres[xi], in_=x_cur[r0:r0 + 128, dg * 512:(dg + 1) * 512]),
                         w=[("xres", xi)], dma="xres%d" % xi)
                    bank = acc % 4
                    acc += 1
                    for kc in range(NCM):
                        S.op("pe", lambda e, bank=bank, kc=kc, j=j, sl=sl: e.matmul(
                            PB[bank][:, 0:512], yT[:, kc, j * 128:(j + 1) * 128], wos[sl][:, kc, :],
                            start=(kc == 0), stop=(kc == NCM - 1)), r=ykeys + [("wo", sl)], w=[("pb", bank)])
                    S.op("dve", lambda e, bank=bank, xi=xi, xoi=xoi: e.tensor_tensor(out=xo[xoi], in0=PB[bank][:, 0:512], in1=xres[xi], op=ALU.add),
                         r=[("pb", bank), ("xres", xi)], w=[("xo", xoi)])
                    S.op("sp", lambda e, xoi=xoi, r0=r0, dg=dg: e.dma_start(out=x_next[r0:r0 + 128, dg * 512:(dg + 1) * 512], in_=xo[xoi]),
                         r=[("xo", xoi)], w=[("x_next", st, j, dg)], dma="xo%d" % xoi)
        S.fence()

    arena.top = PERSIST_TOP
    fg = arena.alloc(DM, F32)
    NFB = 4
    xf = [arena.alloc(DM, F32) for _ in range(NFB)]
    yf = [arena.alloc(DM, F32) for _ in range(NFB)]
    fss = arena.alloc(NFB, F32)
    frs = arena.alloc(NFB, F32)
    S.op("sp", lambda e: e.dma_start(out=fg, in_=fing_in.partition_broadcast(128)), w=["fg"], dma="ldc0")
    x_last = xs[DEPTH - 1]
    for blk in range(NBLK):
        i = blk % NFB
        r0 = blk * 128
        S.op("sp", lambda e, i=i, r0=r0: e.dma_start(out=xf[i], in_=x_last[r0:r0 + 128, :]), w=[("xf", i)], dma="xf%d" % i)
        S.op("act", lambda e, i=i: e.activation(out=yf[i], in_=xf[i], func=AF.Square, accum_out=fss[:, i:i + 1]),
             r=[("xf", i)], w=[("yf", i), ("fss", i)])
        S.op("act", lambda e, i=i: e.activation(out=frs[:, i:i + 1], in_=fss[:, i:i + 1], func=AF.Sqrt, scale=1.0 / DM, bias=eps_t),
             r=[("fss", i), "eps"], w=[("frs", i)])
        S.op("dve", lambda e, i=i: e.reciprocal(out=frs[:, i:i + 1], in_=frs[:, i:i + 1]), r=[("frs", i)], w=[("frs", i)])
        S.op("dve", lambda e, i=i: e.scalar_tensor_tensor(out=yf[i], in0=xf[i], scalar=frs[:, i:i + 1], in1=fg, op0=ALU.mult, op1=ALU.mult),
             r=[("xf", i), ("frs", i), "fg"], w=[("yf", i)])
        S.op("sp", lambda e, i=i, r0=r0: e.dma_start(out=out_d[r0:r0 + 128, :], in_=yf[i]), r=[("yf", i)], w=[("out", blk)], dma="yf%d" % i)
    if c.debug:
        S.fence()
        for nm, src in (("qT", qT_send), ("kT", kT_send), ("v", v_send), ("on", onT_send), ("ya", yaT_sp),
                        ("sgb", sgbT_sp), ("x0", xs[0]), ("ong", onT_g)):
            if isinstance(c.debug, (list, tuple)) and nm not in c.debug:
                continue
            nd = len(src.shape)
            names = " ".join("d%d" % i for i in range(nd))
            tot = 1
            for v_ in src.shape:
                tot *= v_
            flat = lambda ap_, names=names: ap_.rearrange("%s -> (%s)" % (names, names)).rearrange("(p f) -> p f", p=128)
            dd = nc.dram_tensor("dbg_" + nm, [128, tot // 128], src.dtype, kind="ExternalOutput").ap()
            S.op("pool", lambda e, dd=dd, src=src, flat=flat: e.dma_start(out=dd, in_=flat(src)),
                 w=[("dbg", nm)], dma="dbg_" + nm)
    S.fence(final=True)

    keys = S.finalize()
    with nc.cleanup_on_exit():
        sems = {k: nc.alloc_semaphore(name="s%d" % i) for i, k in enumerate(keys)}
        for k in keys:
            nc.gpsimd.sem_clear(sems[k])
        nc.all_engine_barrier()
        with nc.Block() as block:
            S.emit(block, sems)
        nc.all_engine_barrier()
    for cx in reversed(ctxs):
        cx.__exit__(None, None, None)
    return nc, len(S.ops), len(keys)


def make_in_maps(cfg, x, positions, norm_g, w_in, ln_g, ln_b, sgu_w, sgu_b,
                 lam_q1, lam_k1, lam_q2, lam_k2, subln_g, w_out, final_g):
    c = cfg
    f = np.float32
    x = np.asarray(x, f)
    positions = np.asarray(positions, np.int32)
    DEPTH = c.DEPTH
    invf = (np.float32(ROPE_THETA) ** (-np.arange(0, 32, 2, dtype=np.float32) / np.float32(32))).astype(f)[None, :]
    ngc = np.ascontiguousarray(np.asarray(norm_g, f).reshape(DEPTH, c.KC, 128).transpose(0, 2, 1))
    lgc = np.ascontiguousarray(np.asarray(ln_g, f).reshape(DEPTH, c.NCA, 128).transpose(0, 2, 1))
    lbc = np.ascontiguousarray(np.asarray(ln_b, f).reshape(DEPTH, c.NCA, 128).transpose(0, 2, 1))
    sguw = np.ascontiguousarray(np.asarray(sgu_w, f))
    sgub = np.ascontiguousarray(np.asarray(sgu_b, f).reshape(DEPTH, 1, c.G * 128))
    lam = np.ascontiguousarray(np.concatenate([np.asarray(a, f) for a in (lam_q1, lam_k1, lam_q2, lam_k2)], axis=1)
                               .reshape(DEPTH, 1, 4 * 128))
    subg = np.ascontiguousarray(np.asarray(subln_g, f).reshape(DEPTH, 1, 256))
    fing = np.ascontiguousarray(np.asarray(final_g, f).reshape(1, c.DM))
    w_in = np.asarray(w_in, f)
    w_out = np.asarray(w_out, f)
    maps = []
    ri, ro = c.DM // RANKS, c.DMIX // RANKS
    for core in range(NCORES):
        b, r = core // RANKS, core % RANKS
        t0 = r * c.TPC
        maps.append({
            "x": np.ascontiguousarray(x[b, t0:t0 + c.TPC, :]),
            "pos_t": np.ascontiguousarray(positions[b, t0:t0 + c.TPC].reshape(c.NBLK, 128).T),
            "invf": invf, "norm_gc": ngc, "ln_gc": lgc, "ln_bc": lbc, "sgu_w": sguw, "sgu_b": sgub,
            "lam": lam, "subln_g": subg, "final_g": fing,
            "w_in_sh": np.ascontiguousarray(w_in.reshape(DEPTH, c.KC, RANKS, 32, c.DIN)[:, :, r].reshape(DEPTH, c.KC * 32, c.DIN)),
            "w_out_sh": np.ascontiguousarray(w_out.reshape(DEPTH, c.NCM, RANKS, 32, c.DM)[:, :, r].reshape(DEPTH, c.NCM * 32, c.DM)),
        })
    return maps


_CACHE = {}


def run(cfg, inputs):
    key = (cfg.S, cfg.DM, cfg.G, cfg.H, cfg.DEPTH, cfg.stop)
    if key not in _CACHE:
        _CACHE[key] = build_program(cfg)[0]
    nc = _CACHE[key]
    maps = make_in_maps(cfg, **inputs)
    res = run_bass_kernel_spmd(nc, maps, core_ids=list(range(NCORES)))
    out = np.empty((2, cfg.S, cfg.DM), np.float32)
    for core in range(NCORES):
        b, r = core // RANKS, core % RANKS
        out[b, r * cfg.TPC:(r + 1) * cfg.TPC, :] = res.results[core]["out"]
    if cfg.debug:
        return out, res.results
    return out


def kernel(**inputs):
    return run(Cfg(), inputs)
```
